# Optimizing a Trainium2 kernel written in Bass

```python
import math
import jax
import jax.numpy as jnp
from jax import lax
import numpy as np

D_MODEL = 1024
BATCH = 4
SEQ = 8192
DEPTH = 1

CHUNK = 64
Q_BLOCK = 128
ATTN_WIDTH = 1024
SSM_WIDTH = 1024
MIX_WIDTH = ATTN_WIDTH + SSM_WIDTH
HEAD_DIM = 64
V_HEAD_DIM = 2 * HEAD_DIM
ATTN_HEADS = ATTN_WIDTH // V_HEAD_DIM
SSM_GROUP = 16
SSM_GROUPS = SSM_WIDTH // SSM_GROUP
SSM_STATE = 64
ROPE_THETA = 10000.0
NORM_EPS = 1e-6
MASK_VALUE = -1e30
IN_SPLITS = (ATTN_WIDTH, 2 * ATTN_WIDTH, 3 * ATTN_WIDTH, 4 * ATTN_WIDTH, 4 * ATTN_WIDTH + SSM_WIDTH)
IN_COLS = 4 * ATTN_WIDTH + 2 * SSM_WIDTH

kernel_name = "hybrid_diffattn_s5_adaln_layer"


def rms_norm(x, g):
    xf = x.astype(jnp.float32)
    y = xf * lax.rsqrt(jnp.mean(xf * xf, axis=-1, keepdims=True) + NORM_EPS)
    return (y * g.astype(jnp.float32)).astype(x.dtype)


def rope(x, cos, sin):
    half = x.shape[-1] // 2
    x1, x2 = x[..., :half], x[..., half:]
    return jnp.concatenate([x1 * cos - x2 * sin, x2 * cos + x1 * sin], axis=-1)


def diff_attention(q, k, v, lam):
    b_, h_, _, s_, d = q.shape
    nblk = s_ // Q_BLOCK
    scale = d ** -0.5
    qb = q.reshape(b_, h_, 2, nblk, Q_BLOCK, d).transpose(3, 0, 1, 2, 4, 5)
    k_chunk = jnp.arange(s_) // CHUNK

    def one_block(args):
        i, qi = args
        s = jnp.einsum('bhmqd,bhmkd->bhmqk', qi, k).astype(jnp.float32) * scale
        q_chunk = (i * Q_BLOCK + jnp.arange(Q_BLOCK)) // CHUNK
        allowed = k_chunk[None, :] <= q_chunk[:, None]
        s = jnp.where(allowed, s, MASK_VALUE)
        p = jax.nn.softmax(s, axis=-1)
        w = p[:, :, 0] - lam * p[:, :, 1]
        return jnp.einsum('bhqk,bhkv->bhqv', w.astype(v.dtype), v)

    out = lax.map(one_block, (jnp.arange(nblk), qb))
    return out.transpose(1, 0, 3, 2, 4).reshape(b_, s_, h_, v.shape[-1])


def _complex_affine_combine(e1, e2):
    a1r, a1i, b1r, b1i = e1
    a2r, a2i, b2r, b2i = e2
    ar = a2r * a1r - a2i * a1i
    ai = a2r * a1i + a2i * a1r
    br = a2r * b1r - a2i * b1i + b2r
    bi = a2r * b1i + a2i * b1r + b2i
    return (ar, ai, br, bi)


def s5_ssm(u, a_re, a_im, log_dt, b_re, b_im, c_re, c_im, d_skip):
    b_, s_, w_ = u.shape
    n_chunks = s_ // CHUNK
    f32 = jnp.float32
    uf = u.astype(f32)
    uc = uf.reshape(b_, n_chunks, CHUNK, SSM_GROUPS, SSM_GROUP).transpose(1, 0, 2, 3, 4)
    dt = jnp.exp(log_dt.astype(f32))[:, None]
    ar, ai = a_re.astype(f32), a_im.astype(f32)
    mag = jnp.exp(ar * dt)
    abar_re, abar_im = mag * jnp.cos(ai * dt), mag * jnp.sin(ai * dt)
    nr, ni = abar_re - 1.0, abar_im
    den = ar * ar + ai * ai
    f_re = (nr * ar + ni * ai) / den
    f_im = (ni * ar - nr * ai) / den
    br_, bi_ = b_re.astype(f32), b_im.astype(f32)
    bb_re = f_re[..., None] * br_ - f_im[..., None] * bi_
    bb_im = f_re[..., None] * bi_ + f_im[..., None] * br_
    cr, ci = c_re.astype(f32), c_im.astype(f32)
    a_seq_re = jnp.broadcast_to(abar_re[None, None], (1, CHUNK, SSM_GROUPS, SSM_STATE))
    a_seq_im = jnp.broadcast_to(abar_im[None, None], (1, CHUNK, SSM_GROUPS, SSM_STATE))

    def chunk_step(carry, u_c):
        prev_re, prev_im = carry
        bu_re = jnp.einsum('bcgh,gph->bcgp', u_c, bb_re)
        bu_im = jnp.einsum('bcgh,gph->bcgp', u_c, bb_im)
        acum_re, acum_im, h_re, h_im = lax.associative_scan(
            _complex_affine_combine, (a_seq_re, a_seq_im, bu_re, bu_im), axis=1)
        h_re = h_re + acum_re * prev_re[:, None] - acum_im * prev_im[:, None]
        h_im = h_im + acum_re * prev_im[:, None] + acum_im * prev_re[:, None]
        y = jnp.einsum('bcgp,ghp->bcgh', h_re, cr) - jnp.einsum('bcgp,ghp->bcgh', h_im, ci)
        return (h_re[:, -1], h_im[:, -1]), y

    init = (jnp.zeros((b_, SSM_GROUPS, SSM_STATE), f32), jnp.zeros((b_, SSM_GROUPS, SSM_STATE), f32))
    _, ys = lax.scan(chunk_step, init, uc)
    y = ys.transpose(1, 0, 2, 3, 4).reshape(b_, s_, w_)
    y = y + d_skip.astype(f32) * uf
    return y.astype(u.dtype)


def setup_inputs(seed: int = 0) -> dict:
    key = jax.random.key(seed)
    ks = jax.random.split(key, 24)
    f32 = jnp.float32

    def nrm(k, shape, scale):
        return jax.random.normal(k, shape, f32) * scale

    n_idx = jnp.arange(SSM_STATE, dtype=f32)
    gps = (DEPTH, SSM_GROUPS, SSM_STATE)
    return {
        "x": nrm(ks[0], (BATCH, SEQ, D_MODEL), 1.0),
        "c": nrm(ks[1], (BATCH, D_MODEL), 1.0),
        "w_ada": nrm(ks[2], (DEPTH, D_MODEL, 3 * D_MODEL), 0.5 * D_MODEL ** -0.5),
        "b_ada": nrm(ks[3], (DEPTH, 3 * D_MODEL), 0.02),
        "norm_g": 1.0 + nrm(ks[4], (DEPTH, D_MODEL), 0.02),
        "w_in": nrm(ks[5], (DEPTH, D_MODEL, IN_COLS), D_MODEL ** -0.5),
        "q_norm_g": 1.0 + nrm(ks[6], (DEPTH, HEAD_DIM), 0.02),
        "k_norm_g": 1.0 + nrm(ks[7], (DEPTH, HEAD_DIM), 0.02),
        "lam_q1": nrm(ks[8], (DEPTH, HEAD_DIM), 0.1),
        "lam_k1": nrm(ks[9], (DEPTH, HEAD_DIM), 0.1),
        "lam_q2": nrm(ks[10], (DEPTH, HEAD_DIM), 0.1),
        "lam_k2": nrm(ks[11], (DEPTH, HEAD_DIM), 0.1),
        "head_norm_g": 1.0 + nrm(ks[12], (DEPTH, V_HEAD_DIM), 0.02),
        "ssm_a_re": -0.5 + nrm(ks[13], gps, 0.01),
        "ssm_a_im": math.pi * n_idx + nrm(ks[14], gps, 0.01),
        "ssm_log_dt": jax.random.uniform(ks[15], (DEPTH, SSM_GROUPS), f32, math.log(1e-3), math.log(1e-1)),
        "ssm_b_re": nrm(ks[16], (DEPTH, SSM_GROUPS, SSM_STATE, SSM_GROUP), (2 * SSM_GROUP) ** -0.5),
        "ssm_b_im": nrm(ks[17], (DEPTH, SSM_GROUPS, SSM_STATE, SSM_GROUP), (2 * SSM_GROUP) ** -0.5),
        "ssm_c_re": nrm(ks[18], (DEPTH, SSM_GROUPS, SSM_GROUP, SSM_STATE), SSM_STATE ** -0.5),
        "ssm_c_im": nrm(ks[19], (DEPTH, SSM_GROUPS, SSM_GROUP, SSM_STATE), SSM_STATE ** -0.5),
        "ssm_d": nrm(ks[20], (DEPTH, SSM_WIDTH), 1.0),
        "w_glu": nrm(ks[21], (DEPTH, SSM_WIDTH, SSM_WIDTH), SSM_WIDTH ** -0.5),
        "b_glu": nrm(ks[22], (DEPTH, SSM_WIDTH), 0.02),
        "w_out": nrm(ks[23], (DEPTH, MIX_WIDTH, D_MODEL), MIX_WIDTH ** -0.5),
    }


def reference(x, c, w_ada, b_ada, norm_g, w_in, q_norm_g, k_norm_g, lam_q1, lam_k1, lam_q2, lam_k2,
              head_norm_g, ssm_a_re, ssm_a_im, ssm_log_dt, ssm_b_re, ssm_b_im, ssm_c_re, ssm_c_im,
              ssm_d, w_glu, b_glu, w_out):
    b_, s_, _ = x.shape
    f32 = jnp.float32
    pos = jnp.arange(s_, dtype=f32)
    inv_freq = 1.0 / (ROPE_THETA ** (jnp.arange(0, HEAD_DIM, 2, dtype=f32) / HEAD_DIM))
    ang = pos[:, None] * inv_freq[None, :]
    cos, sin = jnp.cos(ang).astype(x.dtype), jnp.sin(ang).astype(x.dtype)

    for l in range(DEPTH):
        lam_init = 0.8 - 0.6 * math.exp(-0.3 * l)
        mod = jax.nn.silu(c) @ w_ada[l] + b_ada[l]
        shift, scale, gate = jnp.split(mod, 3, axis=-1)
        h = rms_norm(x, norm_g[l]) * (1.0 + scale[:, None, :]) + shift[:, None, :]
        proj = h @ w_in[l]
        q, k, v, g_attn, u, g_ssm = jnp.split(proj, IN_SPLITS, axis=-1)

        q = q.reshape(b_, s_, ATTN_HEADS, 2, HEAD_DIM).transpose(0, 2, 3, 1, 4)
        k = k.reshape(b_, s_, ATTN_HEADS, 2, HEAD_DIM).transpose(0, 2, 3, 1, 4)
        v = v.reshape(b_, s_, ATTN_HEADS, V_HEAD_DIM).transpose(0, 2, 1, 3)
        q = rope(rms_norm(q, q_norm_g[l]), cos, sin)
        k = rope(rms_norm(k, k_norm_g[l]), cos, sin)
        lam = (jnp.exp(jnp.sum(lam_q1[l].astype(f32) * lam_k1[l].astype(f32)))
               - jnp.exp(jnp.sum(lam_q2[l].astype(f32) * lam_k2[l].astype(f32))) + lam_init)
        o = diff_attention(q, k, v, lam)
        o = rms_norm(o, head_norm_g[l]) * (1.0 - lam_init)
        o = o.reshape(b_, s_, ATTN_WIDTH) * jax.nn.silu(g_attn)

        y = s5_ssm(u, ssm_a_re[l], ssm_a_im[l], ssm_log_dt[l], ssm_b_re[l], ssm_b_im[l],
                   ssm_c_re[l], ssm_c_im[l], ssm_d[l])
        z = jax.nn.gelu(y)
        z = z * jax.nn.sigmoid(z @ w_glu[l] + b_glu[l])
        z = z * jax.nn.silu(g_ssm)

        mixed = jnp.concatenate([o, z], axis=-1) @ w_out[l]
        x = x + gate[:, None, :] * mixed
    return x
```

```python
import numpy as np
import concourse.bass as bass
import concourse.mybir as mybir
from concourse.bass_utils import run_bass_kernel_spmd

F32 = mybir.dt.float32
BF16 = mybir.dt.bfloat16
I32 = mybir.dt.int32
AF = mybir.ActivationFunctionType
ALU = mybir.AluOpType
AX = mybir.AxisListType
PI = float(np.pi)
TWO_PI = float(2 * np.pi)
EPS = 1e-6
NEG = -30000.0
LAM_INIT = 0.2


class Buf:
    __slots__ = ("name", "sem", "dcount", "last_w", "readers")

    def __init__(self, name, sem=None):
        self.name = name
        self.sem = sem
        self.dcount = 0
        self.last_w = None
        self.readers = []


class Op:
    __slots__ = ("eng", "fn", "deps", "is_dma", "sem", "val", "need_inc", "idx")


class Sched:
    ENG = ("pe", "act", "dve", "pool", "sp")

    def __init__(self, nc):
        self.nc = nc
        self.ops = {e: [] for e in self.ENG}
        self.esem = {e: nc.alloc_semaphore("es_" + e) for e in self.ENG}
        self.nbuf = 0
        self.pending_dma = []

    def buf(self, name, dma=False):
        self.nbuf += 1
        return Buf(name, self.nc.alloc_semaphore("ds%d" % self.nbuf) if dma else None)

    def _mk(self, eng, fn, reads, writes, is_dma=False):
        op = Op()
        op.eng = eng
        op.fn = fn
        op.is_dma = is_dma
        op.sem = None
        op.val = None
        op.need_inc = False
        cand = []
        for b in reads:
            if b.last_w is not None:
                cand.append(b.last_w)
        for b in writes:
            if b.last_w is not None:
                cand.append(b.last_w)
            cand.extend(b.readers)
        best = {}
        for d in cand:
            k = ("d", id(d.sem)) if d.is_dma else ("e", d.eng)
            cur = best.get(k)
            if cur is None or (d.val > cur.val if d.is_dma else d.idx > cur.idx):
                best[k] = d
        op.deps = list(best.values())
        for b in reads:
            if not is_dma:
                b.readers = [r for r in b.readers if (r.is_dma or r.eng != eng)]
            b.readers.append(op)
        for b in writes:
            b.last_w = op
            b.readers = []
        op.idx = len(self.ops[eng])
        self.ops[eng].append(op)
        return op

    def op(self, eng, fn, reads=(), writes=()):
        return self._mk(eng, fn, reads, writes)

    def dma(self, eng, out, in_, reads=(), writes=(), sem=None):
        assert sem is not None and sem.sem is not None
        sem.dcount += 1
        op = self._mk(eng, lambda e: e.dma_start(out=out, in_=in_), reads, writes, is_dma=True)
        op.sem = sem.sem
        op.val = 16 * sem.dcount
        self.pending_dma.append(op)
        return op

    def barrier(self, dummies):
        arr = []
        for e in ("act", "dve", "pool"):
            b = Buf("bar_" + e)
            op = self._mk(e, dummies[e], [], [b])
            op.deps = list(op.deps) + list(self.pending_dma)
            arr.append(b)
        self.pending_dma = []
        for e in self.ENG:
            self._mk(e, None, arr, [])

    def emit(self, block, final_ops=()):
        for e in self.ENG:
            for op in self.ops[e]:
                for d in op.deps:
                    if not d.is_dma:
                        if d.eng == "pe" and op.eng == "pe" and not op.is_dma:
                            continue
                        d.need_inc = True
        for op in final_ops:
            if not op.is_dma:
                op.need_inc = True
        self.stats = {}
        for e in self.ENG:
            c = 0
            for op in self.ops[e]:
                if op.is_dma:
                    continue
                if op.need_inc:
                    assert op.fn is not None
                    c += 1
                    op.sem = self.esem[e]
                    op.val = c
            self.stats[e] = (len(self.ops[e]), c)

        def run(e, eng):
            waited = {}
            for op in self.ops[e]:
                need = {}
                for d in op.deps:
                    if (not d.is_dma) and d.eng == "pe" and e == "pe" and not op.is_dma:
                        continue
                    k = id(d.sem)
                    if k not in need or need[k][1] < d.val:
                        need[k] = (d.sem, d.val)
                for k, (s, v) in need.items():
                    if waited.get(k, 0) < v:
                        eng.wait_ge(s, v)
                        waited[k] = v
                if op.fn is None:
                    continue
                ins = op.fn(eng)
                if op.is_dma:
                    ins.then_inc(op.sem, 16)
                elif op.need_inc:
                    ins.then_inc(op.sem, 1)
            if e == "sp":
                for op in final_ops:
                    eng.wait_ge(op.sem, op.val)

        block.tensor(lambda eng: run("pe", eng))
        block.scalar(lambda eng: run("act", eng))
        block.vector(lambda eng: run("dve", eng))
        block.gpsimd(lambda eng: run("pool", eng))
        block.sync(lambda eng: run("sp", eng))


def build_nc(NSTEP, debug=False):
    NT = 2 * NSTEP
    SLEN = NT * 512
    nc = bass.Bass("TRN2", target_bir_lowering=False)
    S = Sched(nc)
    dbg_kind = "ExternalOutput" if debug else "Internal"

    def din(name, shape, dt=F32):
        return nc.dram_tensor(name, list(shape), dt, kind="ExternalInput").ap()

    xc = din("xc", [SLEN, 1024])
    xo = din("xo", [NSTEP * 512, 1024])
    out = nc.dram_tensor("out", [NSTEP * 512, 1024], F32, kind="ExternalOutput").ap()
    w_ada = din("w_ada", [1024, 3072])
    w_in = din("w_in", [1024, 6144])
    w_glu = din("w_glu", [1024, 1024])
    w_out = din("w_out", [2048, 1024])
    smalls = din("smalls", [128, 64])
    lamv = din("lamv", [128, 256])
    b_gate_row = din("b_gate_row", [128, 1024])
    base_own = din("base_own", [128, 16])
    ident_in = din("ident", [128, 128])
    perm_in = din("perm", [128, 128])
    a64_in = din("a64", [128, 512])
    bm_in = din("bm", [128, 1024])
    pos0_in = din("pos0", [128, 512])
    sel_in = din("sel", [128, 256])
    ssmS = din("ssmS", [128, 5, 1024])
    ssmK = din("ssmK", [128, 4, 1024])
    ssmT = din("ssmT", [128, 4, 1024])
    ddiag = din("ddiag", [128, 1024])
    w_in_b = nc.dram_tensor("w_in_b", [48, 128, 1024], BF16).ap()
    w_glu_b = nc.dram_tensor("w_glu_b", [8, 128, 1024], BF16).ap()
    w_out_b = nc.dram_tensor("w_out_b", [8, 128, 2048], BF16).ap()
    K_scr = nc.dram_tensor("K_scr", [NT, 8, 128, 512], BF16, kind=dbg_kind).ap()
    V_scr = nc.dram_tensor("V_scr", [NT, 8, 128, 512], BF16, kind=dbg_kind).ap()
    WS_scr = nc.dram_tensor("WS_scr", [8, 128, 8, 2, 128], BF16, kind=dbg_kind).ap()
    WC_scr = nc.dram_tensor("WC_scr", [8, 128, 4, 8, 2, 32], BF16, kind=dbg_kind).ap()
    WK_scr = nc.dram_tensor("WK_scr", [8, 128, 8, 128], BF16, kind=dbg_kind).ap()
    dbg_outs = {}

    def dbg_out(name, shape, dt=F32):
        dbg_outs[name] = nc.dram_tensor(name, list(shape), dt, kind="ExternalOutput").ap()
        return dbg_outs[name]

    def sb(name, shape, dt=F32):
        return nc.alloc_sbuf_tensor("sb_" + name, list(shape), dt)

    PS = nc.alloc_psum_tensor("ps", [128, 4096], F32)
    PB = [S.buf("pb%d" % i) for i in range(8)]

    def bank(i, lo=0, hi=512):
        return PS.ap()[:, 512 * i + lo:512 * i + hi]

    gbc = [0]

    def gb():
        gbc[0] = (gbc[0] + 1) % 8
        return gbc[0]

    ident = sb("ident", [128, 128]); B_ident = S.buf("ident", True)
    perm_f = sb("perm_f", [128, 128]); B_permf = S.buf("permf", True)
    perm_b = sb("perm_b", [128, 128], BF16); B_perm = S.buf("perm")
    ones_b = sb("ones_b", [128, 128], BF16); B_ones = S.buf("ones")
    ones_f = sb("ones_f", [128, 128]); B_onesf = S.buf("onesf")
    sacc = [[sb("sacc%d%d" % (a_, m_), [128, 512]) for m_ in range(2)] for a_ in range(2)]
    B_sacc = [[S.buf("sacc%d%d" % (a_, m_)) for m_ in range(2)] for a_ in range(2)]
    saccp = [sb("saccp%d" % m_, [128, 512]) for m_ in range(2)]
    B_saccp = [S.buf("saccp%d" % m_) for m_ in range(2)]
    blk1_b = sb("blk1_b", [128, 128], BF16); B_blk1 = S.buf("blk1")
    B_a64f = S.buf("a64f", True)
    a64_b = sb("a64_b", [128, 512], BF16); B_a64 = S.buf("a64")
    B_bmf = S.buf("bmf", True)
    bm_b = sb("bm_b", [128, 1024], BF16); B_bm = S.buf("bm")
    pos0 = sb("pos0", [128, 512]); B_pos0 = S.buf("pos0", True)
    sel_f = sb("sel_f", [128, 256]); B_sel = S.buf("sel", True)
    sm = sb("smalls", [128, 64]); B_sm = S.buf("sm", True)
    bown = sb("bown", [128, 16]); B_bown = S.buf("bown", True)
    gate_row = sb("gate_row", [128, 1024]); B_grow = S.buf("grow", True)
    cols = sb("cols", [128, 64]); B_cols = S.buf("cols")
    dmy_a = sb("dmy_a", [128, 1]); dmy_d = sb("dmy_d", [128, 1]); dmy_p = sb("dmy_p", [128, 1])
    C_CT = 0
    C_BADA = 8
    C_NG = 32
    C_GQ = 40; C_GK = 41; C_GH = 42
    C_BGLU = 43
    C_INVF = 51; C_R = 52; C_OMR = 53
    D_A1 = 0
    D_SHIFT = 8
    D_GQ8 = 16; D_NEGLAM = 17; D_GH8 = 18
    D_SC = 20
    D_MOD = 28

    def smc(i, n=1):
        return sm.ap()[:, i:i + n]

    def dc(i, n=1):
        return cols.ap()[:, i:i + n]

    ARENA = sb("arena", [128, 32768], BF16)
    B_arena_init = S.buf("arena_init")
    arena_f = ARENA.ap().bitcast(F32)

    def scr(i, n=512):
        return arena_f[:, 512 * i:512 * i + n]

    def aview(off, n):
        return ARENA.ap()[:, off:off + n]

    HTC = aview(0, 4096).rearrange("p (c t) -> p c t", c=8); B_htc = S.buf("htc")
    HTO = aview(4096, 4096).rearrange("p (c t) -> p c t", c=8); B_hto = S.buf("hto")
    UTC = aview(8192, 8192).rearrange("p (c t) -> p c t", c=8); B_utcl = S.buf("utcl"); B_utch = S.buf("utch")
    B_utc = [B_utcl, B_utch]
    QT = aview(16384, 4096).rearrange("p (c t) -> p c t", c=8); B_qt = [S.buf("qt%d" % h) for h in range(8)]
    GAT = aview(20480, 4096).rearrange("p (c t) -> p c t", c=8); B_gat = [S.buf("gat%d" % h) for h in range(8)]
    UTO = aview(24576, 4096).rearrange("p (c t) -> p c t", c=8); B_uto = S.buf("uto")
    GST = aview(28672, 4096).rearrange("p (c t) -> p c t", c=8); B_gst = [S.buf("gst%d" % h) for h in range(8)]
    OT = HTC
    B_ot = B_htc
    ZG = UTC[:, :, 0:512]
    ZT = UTC[:, :, 512:1024]
    B_zg = B_utcl
    B_zt = B_utch

    R16 = sb("r16", [128, 4096]); B_R = [S.buf("r16_%d" % i, True) for i in range(8)]
    xt_v = R16.ap().rearrange("p (s c) -> p s c", s=4)
    gin_v = R16.ap().rearrange("p (r g c) -> p r g c", r=2, g=4)
    B_gin = [[B_R[4 * ri + sg] for sg in range(4)] for ri in range(2)]
    xr_v = R16.ap()[:, 0:2048].rearrange("p (s c) -> p s c", s=4)
    otile_v = [R16.ap()[:, 2048 + 512 * k:2048 + 512 * (k + 1)] for k in range(2)]
    TMP = sb("tmp", [128, 8, 512]); B_T8 = [S.buf("tmp%d" % i) for i in range(8)]

    def T(i):
        return TMP.ap()[:, i, :]

    sqjunk = sb("sqjunk", [128, 1024], BF16); B_junk = S.buf("junk")
    ss = sb("ss", [128, 8]); B_ss = S.buf("ss")
    ropeC = sb("ropeC", [128, 512]); ropeS = sb("ropeS", [128, 512]); B_rope = S.buf("rope")
    ropeCo = sb("ropeCo", [128, 512]); ropeSo = sb("ropeSo", [128, 512]); B_ropeo = S.buf("ropeo")
    NWB = 4
    wbuf = [sb("wbuf%d" % i, [128, 8, 128], BF16) for i in range(NWB)]
    B_wbuf = [S.buf("wbuf%d" % i, True) for i in range(NWB)]
    wbi = [0]
    wobuf = [sb("wobuf%d" % i, [128, 16, 128], BF16) for i in range(2)]
    B_wobuf = [S.buf("wobuf%d" % i, True) for i in range(2)]
    qsq2 = [sb("qsq%d" % i, [128, 512], BF16) for i in range(3)]; qknb2 = [sb("qknb%d" % i, [128, 512], BF16) for i in range(3)]
    B_qsq2 = [S.buf("qsq%d" % i) for i in range(3)]; B_qknb2 = [S.buf("qknb%d" % i) for i in range(3)]
    qx = sb("qx", [128, 512]); B_qx = S.buf("qx")
    qsq = qsq2[0]; B_qsq = B_qsq2[0]
    NKS = 4
    kst = [sb("kst%d" % i, [128, 512], BF16) for i in range(NKS)]
    B_kst = [S.buf("kst%d" % i, True) for i in range(NKS)]
    vst = [sb("vst%d" % i, [128, 512], BF16) for i in range(NKS)]
    B_vst = [S.buf("vst%d" % i, True) for i in range(NKS)]
    ksi = [0]; vsi = [0]
    NKV = 4
    kbuf_t = sb("kbuf", [128, NKV, 512], BF16); vbuf_t = sb("vbuf", [128, NKV, 512], BF16)
    B_kbuf = [S.buf("kbuf%d" % i, True) for i in range(NKV)]
    B_vbuf = [S.buf("vbuf%d" % i, True) for i in range(NKV)]
    kvi = [0]
    NPB = 4
    pbuf_t = sb("pbuf", [128, NPB, 1024], BF16)
    B_pb8 = [S.buf("pb8_%d" % i) for i in range(8)]
    B_pbuf = [[B_pb8[2 * i], B_pb8[2 * i + 1]] for i in range(NPB)]
    pbi = [0]
    gsel_v = pbuf_t.ap().rearrange("p a b -> p (a b)").bitcast(F32).rearrange("p (r i c) -> p r i c", r=2, i=2)
    cosT = sb("cosT", [128, 32, 32]); sinT = sb("sinT", [128, 32, 32]); rhoT = sb("rhoT", [128, 32, 32])
    B_tab = S.buf("tab")
    scon = sb("scon", [128, 8, 32]); B_scon = S.buf("scon")
    hend = sb("hend", [128, 2, 32]); B_hend = S.buf("hend")
    hmid = sb("hmid", [128, 2, 32]); B_hmid = S.buf("hmid")
    stmp = sb("stmp", [128, 8, 16]); B_stmp = S.buf("stmp")
    hbuf = sb("hbuf", [128, 2, 32, 65], BF16); B_hbuf = S.buf("hbuf")
    wsb_v = [kbuf_t.ap().rearrange("p a b -> p (a b)").rearrange("p (s r c) -> p s r c", s=8, r=2),
             vbuf_t.ap().rearrange("p a b -> p (a b)").rearrange("p (s r c) -> p s r c", s=8, r=2)]
    BL_wsb = [B_kbuf, B_vbuf]
    wcb_v = [wobuf[i].ap().rearrange("p a b -> p (a b)").rearrange("p (j t r c) -> p j t r c", j=4, t=8, r=2) for i in range(2)]
    wkb = [sb("wkb%d" % i, [128, 8, 128], BF16) for i in range(2)]; B_wkb = [S.buf("wkb%d" % i, True) for i in range(2)]
    oti = [0]

    def A(e, fn, r=(), w=()):
        return S.op(e, fn, r, w)

    def sincos(x_ap, sin_out, cos_out, tmpk, tmpi, tmpm, rb, wb, tb, add16=False):
        rb = list(rb); wb = list(wb); tb = list(tb)
        if add16:
            A("dve", lambda e: e.tensor_scalar(out=x_ap, in0=x_ap, scalar1=float(16 * np.pi), scalar2=None, op0=ALU.add), rb, tb)
        A("dve", lambda e: e.tensor_scalar(out=tmpk, in0=x_ap, scalar1=float(1.0 / TWO_PI), scalar2=None, op0=ALU.mult), rb, tb)
        A("dve", lambda e: e.tensor_copy(out=tmpi, in_=tmpk), [], tb)
        A("dve", lambda e: e.tensor_copy(out=tmpk, in_=tmpi), [], tb)
        A("dve", lambda e: e.scalar_tensor_tensor(out=x_ap, in0=tmpk, scalar=-TWO_PI, in1=x_ap, op0=ALU.mult, op1=ALU.add), [], tb)
        A("dve", lambda e: e.tensor_scalar(out=tmpm, in0=x_ap, scalar1=PI, scalar2=-TWO_PI, op0=ALU.is_gt, op1=ALU.mult), [], tb)
        A("dve", lambda e: e.tensor_tensor(out=tmpk, in0=x_ap, in1=tmpm, op=ALU.add), [], tb)
        A("act", lambda e: e.activation(out=sin_out, in_=tmpk, func=AF.Sin), tb, wb)
        A("dve", lambda e: e.tensor_scalar(out=tmpk, in0=x_ap, scalar1=float(PI / 2), scalar2=None, op0=ALU.add), [], tb)
        A("dve", lambda e: e.tensor_scalar(out=tmpm, in0=tmpk, scalar1=PI, scalar2=-TWO_PI, op0=ALU.is_gt, op1=ALU.mult), [], tb)
        A("dve", lambda e: e.tensor_tensor(out=tmpk, in0=tmpk, in1=tmpm, op=ALU.add), [], tb)
        A("act", lambda e: e.activation(out=cos_out, in_=tmpk, func=AF.Sin), tb, wb)

    def cmul(o_re, o_im, a_re, a_im, b_re, b_im, t1, t2, rb, wb, tb, eng="dve"):
        rb = list(rb); wb = list(wb); tb = list(tb)
        A(eng, lambda e: e.tensor_tensor(out=t1, in0=a_re, in1=b_re, op=ALU.mult), rb, tb)
        A(eng, lambda e: e.tensor_tensor(out=t2, in0=a_im, in1=b_im, op=ALU.mult), rb, tb)
        A(eng, lambda e: e.tensor_tensor(out=o_re, in0=t1, in1=t2, op=ALU.subtract), tb, wb)
        A(eng, lambda e: e.tensor_tensor(out=t1, in0=a_re, in1=b_im, op=ALU.mult), rb, tb)
        A(eng, lambda e: e.tensor_tensor(out=t2, in0=a_im, in1=b_re, op=ALU.mult), rb, tb)
        A(eng, lambda e: e.tensor_tensor(out=o_im, in0=t1, in1=t2, op=ALU.add), tb, wb)

    bar_dummies = {"act": lambda e: e.activation(out=dmy_a.ap(), in_=ident.ap()[:, 0:1], func=AF.Copy),
                   "dve": lambda e: e.memset(dmy_d.ap(), 0.0), "pool": lambda e: e.memset(dmy_p.ap(), 0.0)}
    Bz = B_arena_init

    r16f = R16.ap()
    tmpf = TMP.ap().rearrange("p a b -> p (a b)")
    pbf = pbuf_t.ap().rearrange("p a b -> p (a b)")
    cin = [r16f[:, 0:2048], r16f[:, 2048:4096], tmpf[:, 0:2048]]
    tmpb = tmpf[:, 2048:4096].bitcast(BF16)
    cout = [tmpb[:, 0:2048], tmpb[:, 2048:4096], pbf[:, 0:2048], pbf[:, 2048:4096]]
    B_cin = [S.buf("cin%d" % i, True) for i in range(3)]
    B_cout = [S.buf("cout%d" % i, True) for i in range(4)]
    jobs = []
    w_in_v = w_in.rearrange("(c p) j -> p c j", p=128)
    w_glu_v = w_glu.rearrange("(c p) j -> p c j", p=128)
    w_out_v = w_out.rearrange("(c p) j -> p c j", p=128)
    for jb in range(48):
        jobs.append((w_in_v[:, :, jb * 128:(jb + 1) * 128], w_in_b[jb], 8))
    for jb in range(8):
        jobs.append((w_glu_v[:, :, jb * 128:(jb + 1) * 128], w_glu_b[jb], 8))
    for jb in range(8):
        jobs.append((w_out_v[:, :, jb * 128:(jb + 1) * 128], w_out_b[jb], 16))
    precast_ops = []
    for n, (src, dst, nch) in enumerate(jobs):
        i = n % 3
        o = n % 4
        ne = nch * 128
        S.dma("sp", cin[i][:, 0:ne].rearrange("p (c j) -> p c j", c=nch), src, writes=[B_cin[i]], sem=B_cin[i])
        A("pool", (lambda e, i=i, o=o, ne=ne: e.tensor_copy(out=cout[o][:, 0:ne], in_=cin[i][:, 0:ne])), [B_cin[i]], [B_cout[o]])
        S.dma("pool", dst[:, 0:ne], cout[o][:, 0:ne], reads=[B_cout[o]], writes=[], sem=B_cout[o])

    precast_dmas = list(S.pending_dma)
    S.pending_dma = []

    a64_f = scr(20); bm_f = scr(22, 1024)
    S.dma("act", ident.ap(), ident_in, writes=[B_ident], sem=B_ident)
    S.dma("act", perm_f.ap(), perm_in, writes=[B_permf], sem=B_permf)
    S.dma("act", a64_f, a64_in, writes=[B_a64f], sem=B_a64f)
    S.dma("act", bm_f, bm_in, writes=[B_bmf], sem=B_bmf)
    S.dma("act", pos0.ap(), pos0_in, writes=[B_pos0], sem=B_pos0)
    S.dma("act", sel_f.ap(), sel_in, writes=[B_sel], sem=B_sel)
    S.dma("act", sm.ap(), smalls, writes=[B_sm], sem=B_sm)
    S.dma("act", bown.ap(), base_own, writes=[B_bown], sem=B_bown)
    S.dma("act", gate_row.ap(), b_gate_row, writes=[B_grow], sem=B_grow)
    A("dve", lambda e: e.tensor_copy(out=perm_b.ap(), in_=perm_f.ap()), [B_permf], [B_perm])
    A("dve", lambda e: e.tensor_copy(out=a64_b.ap(), in_=a64_f), [B_a64f], [B_a64])
    A("dve", lambda e: e.tensor_copy(out=bm_b.ap(), in_=bm_f), [B_bmf], [B_bm])
    A("dve", lambda e: e.memset(ones_b.ap(), 1.0), [], [B_ones])
    A("dve", lambda e: e.memset(ones_f.ap(), 1.0), [], [B_onesf])
    A("dve", lambda e: e.memset(blk1_b.ap(), 0.0), [], [B_blk1])
    A("dve", lambda e: e.memset(blk1_b.ap()[0:64, 0:64], 1.0), [], [B_blk1])
    A("dve", lambda e: e.memset(blk1_b.ap()[64:128, 64:128], 1.0), [], [B_blk1])
    A("dve", lambda e: e.memset(hend.ap(), 0.0), [], [B_hend])
    A("dve", lambda e: e.memset(cols.ap(), 0.0), [], [B_cols])
    A("dve", lambda e: e.memset(hbuf.ap(), 0.0), [], [B_hbuf])

    lamt = scr(0, 256); lamp = scr(1, 128); lsum = scr(2, 4)
    B_lam = S.buf("lam", True)
    S.dma("act", lamt, lamv, writes=[B_lam], sem=B_lam)
    A("dve", lambda e: e.tensor_tensor(out=lamp[:, 0:64], in0=lamt[:, 0:64], in1=lamt[:, 64:128], op=ALU.mult), [B_lam], [Bz])
    A("dve", lambda e: e.tensor_tensor(out=lamp[:, 64:128], in0=lamt[:, 128:192], in1=lamt[:, 192:256], op=ALU.mult), [B_lam], [Bz])
    A("dve", lambda e: e.reduce_sum(out=lsum[:, 0:1], in_=lamp[:, 0:64], axis=AX.X), [], [Bz])
    A("dve", lambda e: e.reduce_sum(out=lsum[:, 1:2], in_=lamp[:, 64:128], axis=AX.X), [], [Bz])
    A("act", lambda e: e.activation(out=lsum[:, 2:4], in_=lsum[:, 0:2], func=AF.Exp), [Bz], [Bz])
    A("dve", lambda e: e.scalar_tensor_tensor(out=dc(D_NEGLAM), in0=lsum[:, 3:4], scalar=-LAM_INIT, in1=lsum[:, 2:3], op0=ALU.add, op1=ALU.subtract), [Bz], [B_cols])
    A("dve", lambda e: e.tensor_scalar(out=dc(D_GQ8), in0=smc(C_GQ), scalar1=0.125, scalar2=None, op0=ALU.mult), [B_sm], [B_cols])
    A("dve", lambda e: e.tensor_scalar(out=dc(D_GH8), in0=smc(C_GH), scalar1=float(1.0 - LAM_INIT), scalar2=None, op0=ALU.mult), [B_sm], [B_cols])

    A("act", lambda e: e.activation(out=dc(D_SC, 8), in_=smc(C_CT, 8), func=AF.Silu), [B_sm, B_cols], [B_cols])
    wa = [scr(4 + 2 * i, 1024).rearrange("p (c j) -> p c j", c=8) for i in range(3)]
    B_wa = [S.buf("wa%d" % i, True) for i in range(3)]
    w_ada_v = w_ada.rearrange("(c p) j -> p c j", p=128)
    bkA = gb()
    for jb in range(24):
        i = jb % 3
        S.dma("act", wa[i], w_ada_v[:, :, jb * 128:(jb + 1) * 128], writes=[B_wa[i]], sem=B_wa[i])
        for c in range(8):
            A("pe", (lambda e, i=i, c=c, jb=jb: e.matmul(bank(bkA, jb, jb + 1), lhsT=wa[i][:, c, :], rhs=dc(D_SC + c), start=(c == 0), stop=(c == 7))),
              [B_wa[i], B_cols], [PB[bkA]])
    A("dve", lambda e: e.tensor_tensor(out=dc(D_MOD, 24), in0=bank(bkA, 0, 24), in1=smc(C_BADA, 24), op=ALU.add), [PB[bkA], B_sm], [B_cols])
    A("dve", lambda e: e.scalar_tensor_tensor(out=dc(D_A1, 8), in0=dc(D_MOD + 8, 8), scalar=1.0, in1=smc(C_NG, 8), op0=ALU.add, op1=ALU.mult), [B_sm], [B_cols])
    A("dve", lambda e: e.tensor_copy(out=dc(D_SHIFT, 8), in_=dc(D_MOD, 8)), [], [B_cols])
    scbc = scr(12, 1024).rearrange("p (c j) -> p c j", c=8)
    for c in range(8):
        A("dve", (lambda e, c=c: e.tensor_copy(out=scbc[:, c, :], in_=dc(D_SC + c).to_broadcast([128, 128]))), [B_cols], [Bz])
    wg = [scr(14 + 2 * i, 1024) for i in range(3)]
    B_wg = [S.buf("wg%d" % i, True) for i in range(3)]
    bkG = [gb(), gb()]
    for c in range(8):
        i = c % 3
        S.dma("act", wg[i], w_ada[c * 128:(c + 1) * 128, 2048:3072], writes=[B_wg[i]], sem=B_wg[i])
        for hf in range(2):
            A("pe", (lambda e, i=i, c=c, hf=hf: e.matmul(bank(bkG[hf]), lhsT=scbc[:, c, :], rhs=wg[i][:, hf * 512:(hf + 1) * 512], start=(c == 0), stop=(c == 7))),
              [B_wg[i], Bz], [PB[bkG[hf]]])
    for hf in range(2):
        A("dve", (lambda e, hf=hf: e.tensor_tensor(out=gate_row.ap()[:, hf * 512:(hf + 1) * 512], in0=bank(bkG[hf]), in1=gate_row.ap()[:, hf * 512:(hf + 1) * 512], op=ALU.add)),
          [PB[bkG[hf]], B_grow], [B_grow])

    S.barrier(bar_dummies)
    S.pending_dma = precast_dmas + S.pending_dma

    B_T = S.buf("Tl", True)
    tl = [scr(2 * i, 1024) for i in range(4)]
    for i in range(4):
        S.dma("act", tl[i], ssmT[:, i, :], writes=[B_T, Bz], sem=B_T)
    dtT = scr(8, 1024); LrT = scr(10, 1024); thT = scr(12, 1024)
    tk = scr(14, 1024); tm = scr(16, 1024); ti = scr(18, 1024).bitcast(I32); tx = scr(20, 1024)
    A("act", lambda e: e.activation(out=dtT, in_=tl[2], func=AF.Exp), [B_T], [Bz])
    A("dve", lambda e: e.tensor_tensor(out=LrT, in0=tl[0], in1=dtT, op=ALU.mult), [B_T, Bz], [Bz])
    A("dve", lambda e: e.tensor_tensor(out=thT, in0=tl[1], in1=dtT, op=ALU.mult), [B_T, Bz], [Bz])

    def reduce_angle(x):
        n = x.shape[-1]
        A("dve", lambda e: e.tensor_scalar(out=x, in0=x, scalar1=float(16 * np.pi), scalar2=None, op0=ALU.add), [Bz], [Bz])
        A("dve", lambda e: e.tensor_scalar(out=tk[:, 0:n], in0=x, scalar1=float(1.0 / TWO_PI), scalar2=None, op0=ALU.mult), [Bz], [Bz])
        A("dve", lambda e: e.tensor_copy(out=ti[:, 0:n], in_=tk[:, 0:n]), [Bz], [Bz])
        A("dve", lambda e: e.tensor_copy(out=tk[:, 0:n], in_=ti[:, 0:n]), [Bz], [Bz])
        A("dve", lambda e: e.scalar_tensor_tensor(out=x, in0=tk[:, 0:n], scalar=-TWO_PI, in1=x, op0=ALU.mult, op1=ALU.add), [Bz], [Bz])

    def sincos_i(x, sin_out, cos_out, wb):
        n = x.shape[-1]
        sincos(x, sin_out, cos_out, tk[:, 0:n], ti[:, 0:n], tm[:, 0:n], [Bz], wb, [Bz], add16=True)

    reduce_angle(thT)
    A("dve", lambda e: e.tensor_scalar(out=tx, in0=thT, scalar1=8.0, scalar2=None, op0=ALU.mult), [Bz], [Bz])
    reduce_angle(tx)
    misc = scr(22)
    phc = misc[:, 0:32]; a32 = misc[:, 32:64]; c32 = misc[:, 64:96]; s32 = misc[:, 96:128]
    txv = tx.rearrange("p (a m) -> p a m", m=32)
    A("dve", lambda e: e.tensor_copy(out=phc, in_=txv[:, :, 0]), [Bz], [Bz])
    A("dve", lambda e: e.tensor_tensor(out=tx, in0=tx, in1=tl[3], op=ALU.mult), [Bz, B_T], [Bz])
    cosT_f = cosT.ap().rearrange("p a m -> p (a m)")
    sinT_f = sinT.ap().rearrange("p a m -> p (a m)")
    rhoT_f = rhoT.ap().rearrange("p a m -> p (a m)")
    sincos_i(tx, sinT_f, cosT_f, [B_tab, Bz])
    rho = dtT
    A("act", lambda e: e.activation(out=rho, in_=LrT, func=AF.Exp, scale=8.0), [Bz], [Bz])
    A("dve", lambda e: e.tensor_scalar(out=tm, in0=tl[3], scalar1=0.0, scalar2=None, op0=ALU.is_gt), [B_T, Bz], [Bz])
    A("dve", lambda e: e.tensor_tensor(out=rhoT_f, in0=rho, in1=tm, op=ALU.mult), [Bz], [B_tab, Bz])
    rhov = rho.rearrange("p (a m) -> p a m", m=32)
    A("dve", lambda e: e.tensor_tensor(out=scon.ap()[:, 0, :], in0=rhov[:, :, 1], in1=cosT.ap()[:, :, 1], op=ALU.mult), [Bz, B_tab], [B_scon, Bz])
    A("dve", lambda e: e.tensor_tensor(out=scon.ap()[:, 1, :], in0=rhov[:, :, 1], in1=sinT.ap()[:, :, 1], op=ALU.mult), [Bz, B_tab], [B_scon, Bz])
    A("dve", lambda e: e.tensor_scalar(out=a32, in0=phc, scalar1=32.0, scalar2=None, op0=ALU.mult), [Bz], [Bz])
    sincos_i(a32, s32, c32, [Bz])
    A("dve", lambda e: e.tensor_tensor(out=scon.ap()[:, 2, :], in0=rhov[:, :, 1], in1=c32, op=ALU.mult), [Bz], [B_scon, Bz])
    A("dve", lambda e: e.tensor_tensor(out=scon.ap()[:, 3, :], in0=rhov[:, :, 1], in1=s32, op=ALU.mult), [Bz], [B_scon, Bz])
    akre = scr(23)[:, 0:288].rearrange("p (k a) -> p k a", k=9)
    akim = scr(24)[:, 0:288].rearrange("p (k a) -> p k a", k=9)
    cmp_ = scr(25)
    Lrc = cmp_[:, 0:32]; thc = cmp_[:, 32:64]; magc = cmp_[:, 64:96]; a1s = cmp_[:, 96:128]; a1c = cmp_[:, 128:160]
    arc = cmp_[:, 160:192]; aic = cmp_[:, 192:224]; fre = cmp_[:, 224:256]; fim = cmp_[:, 256:288]
    t1c = cmp_[:, 288:320]; t2c = cmp_[:, 320:352]; nrc = cmp_[:, 352:384]; denc = cmp_[:, 384:416]; t3c = cmp_[:, 416:448]
    LrTv = LrT.rearrange("p (a m) -> p a m", m=32)
    thTv = thT.rearrange("p (a m) -> p a m", m=32)
    A("dve", lambda e: e.tensor_copy(out=Lrc, in_=LrTv[:, :, 0]), [Bz], [Bz])
    A("dve", lambda e: e.tensor_copy(out=thc, in_=thTv[:, :, 0]), [Bz], [Bz])
    A("dve", lambda e: e.tensor_copy(out=arc, in_=tl[0].rearrange("p (a m) -> p a m", m=32)[:, :, 0]), [B_T, Bz], [Bz])
    A("dve", lambda e: e.tensor_copy(out=aic, in_=tl[1].rearrange("p (a m) -> p a m", m=32)[:, :, 0]), [B_T, Bz], [Bz])
    A("act", lambda e: e.activation(out=magc, in_=Lrc, func=AF.Exp), [Bz], [Bz])
    sincos_i(thc, a1s, a1c, [Bz])
    A("dve", lambda e: e.tensor_tensor(out=akre[:, 1, :], in0=magc, in1=a1c, op=ALU.mult), [Bz], [Bz])
    A("dve", lambda e: e.tensor_tensor(out=akim[:, 1, :], in0=magc, in1=a1s, op=ALU.mult), [Bz], [Bz])
    A("dve", lambda e: e.memset(akre[:, 0, :], 1.0), [Bz], [Bz])
    A("dve", lambda e: e.memset(akim[:, 0, :], 0.0), [Bz], [Bz])
    for k in range(2, 9):
        cmul(akre[:, k, :], akim[:, k, :], akre[:, k - 1, :], akim[:, k - 1, :], akre[:, 1, :], akim[:, 1, :], t1c, t2c, [Bz], [Bz], [Bz])
    A("dve", lambda e: e.tensor_scalar(out=nrc, in0=akre[:, 1, :], scalar1=-1.0, scalar2=None, op0=ALU.add), [Bz], [Bz])
    A("dve", lambda e: e.tensor_tensor(out=t1c, in0=arc, in1=arc, op=ALU.mult), [Bz], [Bz])
    A("dve", lambda e: e.tensor_tensor(out=t2c, in0=aic, in1=aic, op=ALU.mult), [Bz], [Bz])
    A("dve", lambda e: e.tensor_tensor(out=denc, in0=t1c, in1=t2c, op=ALU.add), [Bz], [Bz])
    A("dve", lambda e: e.reciprocal(out=denc, in_=denc), [Bz], [Bz])
    A("dve", lambda e: e.tensor_tensor(out=t1c, in0=nrc, in1=arc, op=ALU.mult), [Bz], [Bz])
    A("dve", lambda e: e.tensor_tensor(out=t2c, in0=akim[:, 1, :], in1=aic, op=ALU.mult), [Bz], [Bz])
    A("dve", lambda e: e.tensor_tensor(out=t3c, in0=t1c, in1=t2c, op=ALU.add), [Bz], [Bz])
    A("dve", lambda e: e.tensor_tensor(out=fre, in0=t3c, in1=denc, op=ALU.mult), [Bz], [Bz])
    A("dve", lambda e: e.tensor_tensor(out=t1c, in0=akim[:, 1, :], in1=aic, op=ALU.mult), [Bz], [Bz])
    A("dve", lambda e: e.tensor_tensor(out=t2c, in0=nrc, in1=aic, op=ALU.mult), [Bz], [Bz])
    A("dve", lambda e: e.tensor_tensor(out=t3c, in0=t1c, in1=t2c, op=ALU.subtract), [Bz], [Bz])
    A("dve", lambda e: e.tensor_tensor(out=fim, in0=t3c, in1=denc, op=ALU.mult), [Bz], [Bz])

    kin = [[scr(2 * i)[:, 128 * q:128 * (q + 1)] for q in range(4)] for i in range(2)]
    B_kin = [S.buf("kin%d" % i, True) for i in range(2)]
    bbr = scr(8); bbi = scr(9)
    car_l = [scr(10), scr(16)]; cai_l = [scr(11), scr(17)]
    kt1 = scr(12)[:, 0:128]; kt2 = scr(12)[:, 128:256]; kp1 = scr(13)[:, 0:128]; kp2 = scr(13)[:, 128:256]
    akimn = scr(4)[:, 0:288].rearrange("p (k a) -> p k a", k=9)
    ddt = scr(14, 1024); B_dd = S.buf("dd", True)
    B_bb = S.buf("bb"); B_car_l = [S.buf("car0"), S.buf("car1")]; B_cai_l = [S.buf("cai0"), S.buf("cai1")]; B_kd = S.buf("kd"); B_kp = S.buf("kp"); B_akn = S.buf("akn")
    S.dma("act", ddt, ddiag, writes=[B_dd, Bz], sem=B_dd)
    A("dve", lambda e: e.tensor_scalar(out=akimn, in0=akim, scalar1=-1.0, scalar2=None, op0=ALU.mult), [Bz], [B_akn, Bz])
    A("dve", lambda e: e.memset(bbr, 0.0), [], [Bz, B_bb])
    A("dve", lambda e: e.memset(bbi, 0.0), [], [Bz, B_bb])
    for q_ in range(2):
        A("dve", (lambda e, q_=q_: e.memset(car_l[q_], 0.0)), [], [Bz, B_car_l[q_]])
        A("dve", (lambda e, q_=q_: e.memset(cai_l[q_], 0.0)), [], [Bz, B_cai_l[q_]])
    A("dve", lambda e: e.memset(kp1, 0.0), [], [Bz, B_kp])
    A("dve", lambda e: e.memset(kt1, 0.0), [], [Bz, B_kd])
    wcst = wcb_v
    wkst = [wkb[i].ap() for i in range(2)]

    def bc4(col_ap, n=32):
        return bass.AP(tensor=col_ap.tensor, offset=col_ap.offset, ap=[list(col_ap.ap[0]), [1, 4], [0, n]])

    def v4(ap128):
        return ap128.rearrange("p (j c) -> p j c", j=4)

    def dg(t512):
        return bass.AP(tensor=t512.tensor, offset=t512.offset, ap=[list(t512.ap[0]), [160, 4], [1, 32]])

    for cb in range(8):
        i = cb % 2
        for q in range(4):
            S.dma("act", kin[i][q], ssmK[:, q, cb * 128:(cb + 1) * 128], writes=[B_kin[i], Bz], sem=B_kin[i])
        bpr, bpi, cpr, cpi = [v4(kin[i][q]) for q in range(4)]
        fr_b = bc4(fre[:, 4 * cb:4 * cb + 4]); fi_b = bc4(fim[:, 4 * cb:4 * cb + 4])
        cmul(dg(bbr), dg(bbi), bpr, bpi, fr_b, fi_b, v4(kt1), v4(kt2), [B_kin[i], Bz], [B_bb], [B_kd])
        for lag in range(9):
            ar_b = bc4(akre[:, lag, 4 * cb:4 * cb + 4]); ai_b = bc4(akimn[:, lag, 4 * cb:4 * cb + 4])
            car = car_l[lag % 2]; cai = cai_l[lag % 2]; B_car = B_car_l[lag % 2]; B_cai = B_cai_l[lag % 2]
            A("dve", (lambda e, cpr=cpr, ar_b=ar_b: e.tensor_tensor(out=v4(kt1), in0=cpr, in1=ar_b, op=ALU.mult)), [B_kin[i], Bz], [B_kd])
            A("dve", (lambda e, cpi=cpi, ai_b=ai_b: e.tensor_tensor(out=v4(kt2), in0=cpi, in1=ai_b, op=ALU.mult)), [B_kin[i], B_akn], [B_kd])
            A("dve", (lambda e, car=car: e.tensor_tensor(out=dg(car), in0=v4(kt1), in1=v4(kt2), op=ALU.add)), [B_kd], [B_car])
            A("dve", (lambda e, cpr=cpr, ai_b=ai_b: e.tensor_tensor(out=v4(kp1), in0=cpr, in1=ai_b, op=ALU.mult)), [B_kin[i], B_akn], [B_kp])
            A("dve", (lambda e, cpi=cpi, ar_b=ar_b: e.tensor_tensor(out=v4(kp2), in0=cpi, in1=ar_b, op=ALU.mult)), [B_kin[i], Bz], [B_kp])
            A("dve", (lambda e, cai=cai: e.tensor_tensor(out=dg(cai), in0=v4(kp1), in1=v4(kp2), op=ALU.subtract)), [B_kp], [B_cai])
            if lag >= 1:
                A("act", (lambda e, i=i, lag=lag, car=car: e.activation(out=wcst[i][:, :, lag - 1, 0, :], in_=dg(car), func=AF.Copy)), [B_car], [B_wobuf[i]])
                A("act", (lambda e, i=i, lag=lag, cai=cai: e.activation(out=wcst[i][:, :, lag - 1, 1, :], in_=dg(cai), func=AF.Copy)), [B_cai], [B_wobuf[i]])
            if lag <= 7:
                bk = gb()
                for j in range(4):
                    A("pe", (lambda e, bk=bk, j=j, car=car: e.matmul(bank(bk, 0, 128), lhsT=bbr[:, j * 128:(j + 1) * 128], rhs=car[:, j * 128:(j + 1) * 128], start=(j == 0), stop=False)),
                      [B_bb, B_car], [PB[bk]])
                for j in range(4):
                    A("pe", (lambda e, bk=bk, j=j, cai=cai: e.matmul(bank(bk, 0, 128), lhsT=bbi[:, j * 128:(j + 1) * 128], rhs=cai[:, j * 128:(j + 1) * 128], start=False, stop=(j == 3))),
                      [B_bb, B_cai], [PB[bk]])
                if lag == 0:
                    A("dve", (lambda e, bk=bk, i=i, cb=cb: e.tensor_tensor(out=wkst[i][:, 0, :], in0=bank(bk, 0, 128), in1=ddt[:, cb * 128:(cb + 1) * 128], op=ALU.add)),
                      [PB[bk], B_dd], [B_wkb[i]])
                else:
                    A("act", (lambda e, bk=bk, i=i, lag=lag: e.activation(out=wkst[i][:, lag, :], in_=bank(bk, 0, 128), func=AF.Copy)), [PB[bk]], [B_wkb[i]])
        S.dma("act", WC_scr[cb], wcst[i], reads=[B_wobuf[i]], writes=[], sem=B_wobuf[i])
        S.dma("act", WK_scr[cb], wkst[i], reads=[B_wkb[i]], writes=[], sem=B_wkb[i])
    A("dve", lambda e: e.memset(dmy_d.ap(), 0.0), [], [Bz, B_bb, B_kd, B_kp, B_akn] + B_car_l + B_cai_l + B_kin)

    B_Sl = S.buf("Sl", True)
    sl = [scr(2 * i, 1024) for i in range(5)]
    for i in range(5):
        S.dma("act", sl[i], ssmS[:, i, :], writes=[B_Sl, Bz], sem=B_Sl)
    tA = scr(10, 1024); tB = scr(12, 1024); tC = scr(20, 1024); tD = scr(22, 1024)
    tE = scr(24, 1024); tF = scr(26, 1024); tG = scr(28, 1024); tH = scr(30, 1024)
    wsst = wsb_v
    A("act", lambda e: e.activation(out=tA, in_=sl[2], func=AF.Exp), [B_Sl, Bz], [Bz])
    A("dve", lambda e: e.tensor_tensor(out=tB, in0=sl[0], in1=tA, op=ALU.mult), [B_Sl, Bz], [Bz])
    A("dve", lambda e: e.tensor_tensor(out=tC, in0=sl[1], in1=tA, op=ALU.mult), [B_Sl, Bz], [Bz])
    A("act", lambda e: e.activation(out=tA, in_=tB, func=AF.Exp), [Bz], [Bz])
    sincos_i(tC, tD, tE, [Bz])
    a1re = tB; a1im = tC
    A("dve", lambda e: e.tensor_tensor(out=a1re, in0=tA, in1=tE, op=ALU.mult), [Bz], [Bz])
    A("dve", lambda e: e.tensor_tensor(out=a1im, in0=tA, in1=tD, op=ALU.mult), [Bz], [Bz])
    nr = tA; st1 = tD; st2 = tE; rden = tF; cre = tG; cim = tH
    A("dve", lambda e: e.tensor_scalar(out=nr, in0=a1re, scalar1=-1.0, scalar2=None, op0=ALU.add), [Bz], [Bz])
    A("dve", lambda e: e.tensor_tensor(out=st1, in0=sl[0], in1=sl[0], op=ALU.mult), [B_Sl, Bz], [Bz])
    A("dve", lambda e: e.tensor_tensor(out=st2, in0=sl[1], in1=sl[1], op=ALU.mult), [B_Sl, Bz], [Bz])
    A("dve", lambda e: e.tensor_tensor(out=rden, in0=st1, in1=st2, op=ALU.add), [Bz], [Bz])
    A("dve", lambda e: e.reciprocal(out=rden, in_=rden), [Bz], [Bz])
    A("dve", lambda e: e.tensor_tensor(out=st1, in0=nr, in1=sl[0], op=ALU.mult), [B_Sl, Bz], [Bz])
    A("dve", lambda e: e.tensor_tensor(out=st2, in0=a1im, in1=sl[1], op=ALU.mult), [B_Sl, Bz], [Bz])
    A("dve", lambda e: e.tensor_tensor(out=st1, in0=st1, in1=st2, op=ALU.add), [Bz], [Bz])
    A("dve", lambda e: e.tensor_tensor(out=cre, in0=st1, in1=rden, op=ALU.mult), [Bz], [Bz])
    A("dve", lambda e: e.tensor_tensor(out=st1, in0=a1im, in1=sl[0], op=ALU.mult), [B_Sl, Bz], [Bz])
    A("dve", lambda e: e.tensor_tensor(out=st2, in0=nr, in1=sl[1], op=ALU.mult), [B_Sl, Bz], [Bz])
    A("dve", lambda e: e.tensor_tensor(out=st1, in0=st1, in1=st2, op=ALU.subtract), [Bz], [Bz])
    A("dve", lambda e: e.tensor_tensor(out=cim, in0=st1, in1=rden, op=ALU.mult), [Bz], [Bz])
    cur = (cre, cim)
    nxt = (tA, tF)
    WS_v = WS_scr.rearrange("cb p s r c -> p cb s r c")
    c8 = lambda ap: ap.rearrange("p (cb c) -> p cb c", cb=8)
    wsst8 = [kbuf_t.ap().rearrange("p a b -> p (a b)").rearrange("p (cb r c) -> p cb r c", cb=8, r=2),
             vbuf_t.ap().rearrange("p a b -> p (a b)").rearrange("p (cb r c) -> p cb r c", cb=8, r=2)]
    for k in range(8):
        sg = 7 - k
        i = k % 2
        o_re = wsst8[i][:, :, 0, :]
        o_im = wsst8[i][:, :, 1, :]
        cmul(o_re, o_im, c8(cur[0]), c8(cur[1]), c8(sl[3]), c8(sl[4]), c8(st1), c8(st2), [B_Sl, Bz], BL_wsb[i], [Bz])
        S.dma("act", WS_v[:, :, sg, :, :], wsst8[i], reads=BL_wsb[i], writes=[], sem=BL_wsb[i][0])
        if k < 7:
            cmul(nxt[0], nxt[1], cur[0], cur[1], a1re, a1im, st1, st2, [Bz], [Bz], [Bz])
            cur, nxt = nxt, cur

    S.barrier(bar_dummies)

    B_K = [[S.buf("K%d_%d" % (t, h)) for h in range(8)] for t in range(NT)]
    B_V = [[S.buf("V%d_%d" % (t, h)) for h in range(8)] for t in range(NT)]
    final_ops = []

    def wload(src):
        i = wbi[0]
        wbi[0] = (i + 1) % NWB
        S.dma("sp", wbuf[i].ap().rearrange("p c j -> p (c j)"), src, writes=[B_wbuf[i]], sem=B_wbuf[i])
        return wbuf[i].ap(), B_wbuf[i]

    def front_a(src_rows):
        for sub in range(4):
            S.dma("sp", xt_v[:, sub, :], src_rows[sub * 128:(sub + 1) * 128, :], writes=[B_R[2 * sub], B_R[2 * sub + 1]], sem=B_R[2 * sub])
        for sub in range(4):
            A("act", (lambda e, sub=sub: e.activation(out=sqjunk.ap(), in_=xt_v[:, sub, :], func=AF.Square, accum_out=ss.ap()[:, sub:sub + 1])),
              [B_R[2 * sub], B_R[2 * sub + 1]], [B_junk, B_ss])
        A("act", lambda e: e.activation(out=ss.ap()[:, 4:8], in_=ss.ap()[:, 0:4], func=AF.Ln, scale=1.0 / 1024.0, bias=EPS), [B_ss], [B_ss])
        A("act", lambda e: e.activation(out=ss.ap()[:, 4:8], in_=ss.ap()[:, 4:8], func=AF.Exp, scale=-0.5), [B_ss], [B_ss])
        for sub in range(4):
            A("dve", (lambda e, sub=sub: e.tensor_scalar(out=xt_v[:, sub, :], in0=xt_v[:, sub, :], scalar1=ss.ap()[:, 4 + sub:5 + sub], scalar2=None, op0=ALU.mult)),
              [B_ss], [B_R[2 * sub], B_R[2 * sub + 1]])

    def front_b(base):
        for c in range(8):
            bk = gb()
            for sub in range(4):
                A("pe", (lambda e, bk=bk, sub=sub, c=c: e.transpose(bank(bk, sub * 128, (sub + 1) * 128), xt_v[:, sub, c * 128:(c + 1) * 128], ident.ap())),
                  [B_R[2 * sub], B_R[2 * sub + 1], B_ident], [PB[bk]])
            A("dve", (lambda e, bk=bk, c=c: e.tensor_scalar(out=HTC[:, c, :], in0=bank(bk), scalar1=dc(D_A1 + c), scalar2=dc(D_SHIFT + c), op0=ALU.mult, op1=ALU.add)),
              [PB[bk], B_cols], [B_htc])
        tb = B_T8[0:4]
        A("dve", lambda e: e.tensor_scalar(out=T(0), in0=pos0.ap(), scalar1=float(base), scalar2=smc(C_INVF), op0=ALU.add, op1=ALU.mult), [B_pos0, B_sm], tb)
        sincos(T(0), ropeS.ap(), ropeC.ap(), T(1), T(2).bitcast(I32), T(3), [], [B_rope], tb)

    qkset = [0]
    QS = [[T(0), T(1), T(2)], [T(3), T(4), T(5)], [T(6), T(7), qx.ap()]]
    BQS = [[B_T8[0], B_T8[1], B_T8[2]], [B_T8[3], B_T8[4], B_T8[5]], [B_T8[6], B_T8[7], B_qx]]

    def qknorm_a(bk, gcol):
        z = qkset[0]; qkset[0] = (z + 1) % 3
        qrs, qkn = QS[z][0], QS[z][1]
        B_qrs, B_qkn = BQS[z][0], BQS[z][1]
        qsq_, qknb_ = qsq2[z].ap(), qknb2[z].ap()
        Bq_, Bkb_ = B_qsq2[z], B_qknb2[z]
        A("act", lambda e: e.activation(out=qsq_, in_=bank(bk), func=AF.Square), [PB[bk]], [Bq_])
        b2 = gb()
        A("pe", lambda e: e.matmul(bank(b2), lhsT=blk1_b.ap(), rhs=qsq_, start=True, stop=True), [B_blk1, Bq_], [PB[b2]])
        A("act", lambda e: e.activation(out=qrs, in_=bank(b2), func=AF.Ln, scale=1.0 / 64.0, bias=EPS), [PB[b2]], [B_qrs])
        A("act", lambda e: e.activation(out=qrs, in_=qrs, func=AF.Exp, scale=-0.5), [], [B_qrs])
        A("dve", lambda e: e.scalar_tensor_tensor(out=qkn, in0=bank(bk), scalar=gcol, in1=qrs, op0=ALU.mult, op1=ALU.mult), [PB[bk], B_qrs, B_cols, B_sm], [B_qkn])
        A("act", lambda e: e.activation(out=qknb_, in_=qkn, func=AF.Copy), [B_qkn], [Bkb_])
        return (z, 0)

    def qknorm_b(stt, C, Sn, Bcs, out_ap, Bout):
        z, _ = stt
        qkn, qt1, qt2 = QS[z][1], QS[z][2], QS[z][0]
        B_qkn, B_qt1, B_qt2 = BQS[z][1], BQS[z][2], BQS[z][0]
        qknb_ = qknb2[z].ap(); Bkb_ = B_qknb2[z]
        b3 = gb()
        A("pe", lambda e: e.matmul(bank(b3), lhsT=perm_b.ap(), rhs=qknb_, start=True, stop=True), [B_perm, Bkb_], [PB[b3]])
        A("dve", lambda e: e.tensor_tensor(out=qt1, in0=qkn, in1=C, op=ALU.mult), [B_qkn, Bcs], [B_qt1])
        A("dve", lambda e: e.tensor_tensor(out=qt2, in0=bank(b3), in1=Sn, op=ALU.mult), [PB[b3], Bcs], [B_qt2])
        A("dve", lambda e: e.tensor_tensor(out=out_ap, in0=qt1, in1=qt2, op=ALU.add), [B_qt1, B_qt2], [Bout])

    def qk_pipeline(blk0, hT, BhT, gcol, C, Sn, Bcs, sink, heads=range(8)):
        heads = list(heads)
        nb = len(heads)
        banks_ = {}
        sta = {}
        for t_ in range(nb + 2):
            if t_ < nb:
                banks_[t_] = proj_fm(blk0 + heads[t_], hT, BhT)
            if 1 <= t_ <= nb:
                sta[t_ - 1] = qknorm_a(banks_[t_ - 1], gcol)
            if 2 <= t_ <= nb + 1:
                out_ap, Bout, post = sink(heads[t_ - 2])
                qknorm_b(sta[t_ - 2], C, Sn, Bcs, out_ap, Bout)
                if post is not None:
                    post()

    def proj_fm(blk, hT, BhT):
        w, Bw = wload(w_in_b[blk])
        bk = gb()
        for c in range(8):
            A("pe", (lambda e, bk=bk, c=c, w=w: e.matmul(bank(bk), lhsT=w[:, c, :], rhs=hT[:, c, :], start=(c == 0), stop=(c == 7))), [Bw, BhT], [PB[bk]])
        return bk

    for s in range(NSTEP):
        for half in range(2):
            g = 2 * s + half
            if half == 0:
                front_a(xc[g * 512:(g + 1) * 512, :])
            front_b(512.0 * g)
            if half == 0:
                front_a(xc[(g + 1) * 512:(g + 2) * 512, :])
            hc = HTC.rearrange("p c t -> p (c t)"); ho = HTO.rearrange("p c t -> p (c t)")
            if half == 0:
                A("dve", lambda e: e.tensor_scalar(out=ho, in0=hc, scalar1=smc(C_OMR), scalar2=None, op0=ALU.mult), [B_htc, B_sm], [B_hto])
                A("dve", lambda e: e.tensor_scalar(out=ropeCo.ap(), in0=ropeC.ap(), scalar1=smc(C_OMR), scalar2=None, op0=ALU.mult), [B_rope, B_sm], [B_ropeo])
                A("dve", lambda e: e.tensor_scalar(out=ropeSo.ap(), in0=ropeS.ap(), scalar1=smc(C_OMR), scalar2=None, op0=ALU.mult), [B_rope, B_sm], [B_ropeo])
            else:
                A("dve", lambda e: e.scalar_tensor_tensor(out=ho, in0=hc, scalar=smc(C_R), in1=ho, op0=ALU.mult, op1=ALU.add), [B_htc, B_sm], [B_hto])
                A("dve", lambda e: e.scalar_tensor_tensor(out=ropeCo.ap(), in0=ropeC.ap(), scalar=smc(C_R), in1=ropeCo.ap(), op0=ALU.mult, op1=ALU.add), [B_rope, B_sm], [B_ropeo])
                A("dve", lambda e: e.scalar_tensor_tensor(out=ropeSo.ap(), in0=ropeS.ap(), scalar=smc(C_R), in1=ropeSo.ap(), op0=ALU.mult, op1=ALU.add), [B_rope, B_sm], [B_ropeo])
            def ksink(h, g=g):
                i = ksi[0]; ksi[0] = (i + 1) % NKS
                return kst[i].ap(), B_kst[i], (lambda i=i, h=h, g=g: S.dma("pool", K_scr[g, h], kst[i].ap(), reads=[B_kst[i]], writes=[B_K[g][h]], sem=B_kst[i]))
            qk_pipeline(8, HTC, B_htc, smc(C_GK), ropeC.ap(), ropeS.ap(), B_rope, ksink)
            for h in range(8):
                w, Bw = wload(w_in_b[16 + h])
                bk = gb()
                for sub in range(4):
                    for c in range(8):
                        A("pe", (lambda e, bk=bk, c=c, w=w, sub=sub: e.matmul(bank(bk, sub * 128, (sub + 1) * 128), lhsT=HTC[:, c, sub * 128:(sub + 1) * 128], rhs=w[:, c, :], start=(c == 0), stop=(c == 7))),
                          [Bw, B_htc], [PB[bk]])
                i = vsi[0]; vsi[0] = (i + 1) % NKS
                A("act", (lambda e, bk=bk, i=i: e.activation(out=vst[i].ap(), in_=bank(bk), func=AF.Copy)), [PB[bk]], [B_vst[i]])
                S.dma("pool", V_scr[g, h], vst[i].ap(), reads=[B_vst[i]], writes=[B_V[g][h]], sem=B_vst[i])
            for cb in range(8):
                bk = proj_fm(32 + cb, HTC, B_htc)
                A("act", (lambda e, bk=bk, cb=cb, half=half: e.activation(out=UTC[:, cb, half * 512:(half + 1) * 512], in_=bank(bk), func=AF.Copy)), [PB[bk]], [B_utc[half]])
        A("dve", lambda e: e.tensor_scalar(out=UTO, in0=UTC[:, :, 0:512], scalar1=smc(C_OMR), scalar2=None, op0=ALU.mult), [B_utcl, B_sm], [B_uto])
        A("dve", lambda e: e.scalar_tensor_tensor(out=UTO, in0=UTC[:, :, 512:1024], scalar=smc(C_R), in1=UTO, op0=ALU.mult, op1=ALU.add), [B_utch, B_sm], [B_uto])

        for hf in range(2):
            for cq in range(4):
                cb = 4 * hf + cq
                i = cb % 2
                S.dma("sp", wsb_v[i], WS_scr[cb], writes=BL_wsb[i], sem=BL_wsb[i][0])
                bks = [(4 * (cb % 2) + j) for j in range(4)]
                for ri in range(2):
                    for sg_ in range(8):
                        for j in range(4):
                            A("pe", (lambda e, i=i, ri=ri, sg_=sg_, j=j, cb=cb, bks=bks: e.matmul(
                                bank(bks[j], ri * 128, (ri + 1) * 128), lhsT=wsb_v[i][32 * j:32 * j + 32, sg_, ri, :],
                                rhs=UTC[32 * j:32 * j + 32, cb, sg_:1024:8], start=(ri == 0 and sg_ == 0), stop=(ri == 1 and sg_ == 7),
                                tile_position=(32 * j, 0), skip_group_check=True)),
                              BL_wsb[i] + B_utc, [PB[bks[j]]])
                b0 = bks[0]
                s_re = bass.AP(tensor=PS, offset=512 * b0, ap=[[4096, 128], [512, 4], [32, 4], [1, 32]])
                s_im = bass.AP(tensor=PS, offset=512 * b0 + 128, ap=[[4096, 128], [512, 4], [32, 4], [1, 32]])
                c_t = bass.AP(tensor=cosT, offset=32 * 4 * cb, ap=[[1024, 128], [32, 4], [0, 4], [1, 32]])
                s_t = bass.AP(tensor=sinT, offset=32 * 4 * cb, ap=[[1024, 128], [32, 4], [0, 4], [1, 32]])
                rt = [T(q).rearrange("p (j s m) -> p j s m", j=4, s=4) for q in range(4)]
                pbs = [PB[b] for b in bks]
                A("dve", (lambda e, s_re=s_re, c_t=c_t, rt=rt: e.tensor_tensor(out=rt[0], in0=s_re, in1=c_t, op=ALU.mult)), pbs + [B_tab], [B_T8[0]])
                A("dve", (lambda e, s_im=s_im, s_t=s_t, rt=rt: e.tensor_tensor(out=rt[1], in0=s_im, in1=s_t, op=ALU.mult)), pbs + [B_tab], [B_T8[1]])
                A("dve", (lambda e, s_im=s_im, c_t=c_t, rt=rt: e.tensor_tensor(out=rt[2], in0=s_im, in1=c_t, op=ALU.mult)), pbs + [B_tab], [B_T8[2]])
                A("dve", (lambda e, s_re=s_re, s_t=s_t, rt=rt: e.tensor_tensor(out=rt[3], in0=s_re, in1=s_t, op=ALU.mult)), pbs + [B_tab], [B_T8[3]])
                g_re = bass.AP(tensor=R16, offset=0 * 2048 + 32 * 4 * cq, ap=[[4096, 128], [32, 4], [512, 4], [1, 32]])
                g_im = bass.AP(tensor=R16, offset=1 * 2048 + 32 * 4 * cq, ap=[[4096, 128], [32, 4], [512, 4], [1, 32]])
                A("pool", (lambda e, g_re=g_re, rt=rt: e.tensor_tensor(out=g_re, in0=rt[0], in1=rt[1], op=ALU.add)), [B_T8[0], B_T8[1]], B_gin[0])
                A("pool", (lambda e, g_im=g_im, rt=rt: e.tensor_tensor(out=g_im, in0=rt[2], in1=rt[3], op=ALU.subtract)), [B_T8[2], B_T8[3]], B_gin[1])
            qk_pipeline(0, HTO, B_hto, dc(D_GQ8), ropeCo.ap(), ropeSo.ap(), B_ropeo, (lambda h: (QT[:, h, :], B_qt[h], None)), heads=range(4 * hf, 4 * hf + 4))
            p0 = 16 * hf
            rho_h = rhoT.ap()[:, p0:p0 + 16, :].rearrange("p a m -> p (a m)")
            cos31 = cosT.ap()[:, p0:p0 + 16, 31]; sin31 = sinT.ap()[:, p0:p0 + 16, 31]
            st = [stmp.ap()[:, q, :] for q in range(8)]
            for sg in range(4):
                gre = gin_v[:, 0, sg, :]; gim = gin_v[:, 1, sg, :]
                gre3 = gre.rearrange("p (a m) -> p a m", m=32); gim3 = gim.rearrange("p (a m) -> p a m", m=32)
                Bg = [B_gin[0][sg], B_gin[1][sg]]
                if sg == 0:
                    pr_re = hend.ap()[:, 0, p0:p0 + 16]; pr_im = hend.ap()[:, 1, p0:p0 + 16]
                    m_re = scon.ap()[:, 0, p0:p0 + 16]; m_im = scon.ap()[:, 1, p0:p0 + 16]
                    rb = [B_hend, B_scon]
                else:
                    pr_re = gin_v[:, 0, sg - 1, :].rearrange("p (a m) -> p a m", m=32)[:, :, 31]
                    pr_im = gin_v[:, 1, sg - 1, :].rearrange("p (a m) -> p a m", m=32)[:, :, 31]
                    m_re = scon.ap()[:, 2, p0:p0 + 16]; m_im = scon.ap()[:, 3, p0:p0 + 16]
                    rb = [B_gin[0][sg - 1], B_gin[1][sg - 1], B_scon]
                cmul(st[0], st[1], pr_re, pr_im, m_re, m_im, st[2], st[3], rb, [B_stmp], [B_stmp])
                A("dve", (lambda e, gre3=gre3, st=st: e.tensor_tensor(out=gre3[:, :, 0], in0=gre3[:, :, 0], in1=st[0], op=ALU.add)), [B_stmp], [Bg[0]])
                A("dve", (lambda e, gim3=gim3, st=st: e.tensor_tensor(out=gim3[:, :, 0], in0=gim3[:, :, 0], in1=st[1], op=ALU.add)), [B_stmp], [Bg[1]])
                A("dve", (lambda e, gre=gre, rho_h=rho_h: e.tensor_tensor_scan(out=gre, data0=rho_h, data1=gre, initial=0.0, op0=ALU.mult, op1=ALU.add)), [B_tab], [Bg[0]])
                A("dve", (lambda e, gim=gim, rho_h=rho_h: e.tensor_tensor_scan(out=gim, data0=rho_h, data1=gim, initial=0.0, op0=ALU.mult, op1=ALU.add)), [B_tab], [Bg[1]])
                w_, i_ = sg // 2, sg % 2
                for ri, gsrc in ((0, gre), (1, gim)):
                    dst = gsel_v[:, ri, i_, :]
                    Bd = B_pbuf[2 * ri + i_]
                    if w_ == 0:
                        A("dve", (lambda e, dst=dst, gsrc=gsrc: e.tensor_scalar(out=dst, in0=gsrc, scalar1=smc(C_OMR), scalar2=None, op0=ALU.mult)), [Bg[ri], B_sm], Bd)
                    else:
                        A("dve", (lambda e, dst=dst, gsrc=gsrc: e.scalar_tensor_tensor(out=dst, in0=gsrc, scalar=smc(C_R), in1=dst, op0=ALU.mult, op1=ALU.add)), [Bg[ri], B_sm], Bd)
                if sg == 1 or sg == 3:
                    g31r = gre3[:, :, 31]; g31i = gim3[:, :, 31]
                    tgt = hmid if sg == 1 else hend
                    Btgt = B_hmid if sg == 1 else B_hend
                    if sg == 3:
                        for ri in range(2):
                            A("dve", (lambda e, ri=ri, st=st: e.tensor_scalar(out=st[4 + ri], in0=hend.ap()[:, ri, p0:p0 + 16], scalar1=smc(C_OMR), scalar2=None, op0=ALU.mult)), [B_hend, B_sm], [B_stmp])
                            A("dve", (lambda e, ri=ri, st=st: e.scalar_tensor_tensor(out=hbuf.ap()[:, ri, p0:p0 + 16, 0], in0=hmid.ap()[:, ri, p0:p0 + 16], scalar=smc(C_R), in1=st[4 + ri], op0=ALU.mult, op1=ALU.add)),
                              [B_hmid, B_stmp, B_sm], [B_hbuf])
                    cmul(tgt.ap()[:, 0, p0:p0 + 16], tgt.ap()[:, 1, p0:p0 + 16], g31r, g31i, cos31, sin31, st[2], st[3], [Bg[0], Bg[1], B_tab], [Btgt], [B_stmp])
            gs_re = gsel_v[:, 0, :, :].rearrange("p i (a m) -> p i a m", m=32)
            gs_im = gsel_v[:, 1, :, :].rearrange("p i (a m) -> p i a m", m=32)
            c_t = bass.AP(tensor=cosT, offset=32 * p0, ap=[[1024, 128], [0, 2], [32, 16], [1, 32]])
            s_t = bass.AP(tensor=sinT, offset=32 * p0, ap=[[1024, 128], [0, 2], [32, 16], [1, 32]])
            tmp4 = TMP.ap().rearrange("p a b -> p (a b)")
            rt = [tmp4[:, 1024 * q:1024 * (q + 1)].rearrange("p (i a m) -> p i a m", i=2, a=16) for q in range(4)]
            Brt = [[B_T8[2 * q], B_T8[2 * q + 1]] for q in range(4)]
            Bgs_re = B_pbuf[0] + B_pbuf[1]; Bgs_im = B_pbuf[2] + B_pbuf[3]
            A("dve", (lambda e, rt=rt, gs_re=gs_re, c_t=c_t: e.tensor_tensor(out=rt[0], in0=gs_re, in1=c_t, op=ALU.mult)), Bgs_re + [B_tab], Brt[0])
            A("dve", (lambda e, rt=rt, gs_im=gs_im, s_t=s_t: e.tensor_tensor(out=rt[1], in0=gs_im, in1=s_t, op=ALU.mult)), Bgs_im + [B_tab], Brt[1])
            A("dve", (lambda e, rt=rt, gs_im=gs_im, c_t=c_t: e.tensor_tensor(out=rt[2], in0=gs_im, in1=c_t, op=ALU.mult)), Bgs_im + [B_tab], Brt[2])
            A("dve", (lambda e, rt=rt, gs_re=gs_re, s_t=s_t: e.tensor_tensor(out=rt[3], in0=gs_re, in1=s_t, op=ALU.mult)), Bgs_re + [B_tab], Brt[3])
            h_re = bass.AP(tensor=hbuf, offset=0 * 32 * 65 + p0 * 65 + 1, ap=[[2 * 32 * 65, 128], [32, 2], [65, 16], [1, 32]])
            h_im = bass.AP(tensor=hbuf, offset=1 * 32 * 65 + p0 * 65 + 1, ap=[[2 * 32 * 65, 128], [32, 2], [65, 16], [1, 32]])
            A("pool", (lambda e, rt=rt, h_re=h_re: e.tensor_tensor(out=h_re, in0=rt[0], in1=rt[1], op=ALU.subtract)), Brt[0] + Brt[1], [B_hbuf])
            A("pool", (lambda e, rt=rt, h_im=h_im: e.tensor_tensor(out=h_im, in0=rt[2], in1=rt[3], op=ALU.add)), Brt[2] + Brt[3], [B_hbuf])

        for h in range(8):
            bk = proj_fm(24 + h, HTO, B_hto)
            A("act", (lambda e, bk=bk, h=h: e.activation(out=GAT[:, h, :], in_=bank(bk), func=AF.Silu)), [PB[bk]], [B_gat[h]])
        for cb in range(8):
            bk = proj_fm(40 + cb, HTO, B_hto)
            A("act", (lambda e, bk=bk, cb=cb: e.activation(out=GST[:, cb, :], in_=bank(bk), func=AF.Silu)), [PB[bk]], [B_gst[cb]])

        for cb in range(8):
            i = cb % 2
            S.dma("sp", wkb[i].ap(), WK_scr[cb], writes=[B_wkb[i]], sem=B_wkb[i])
            S.dma("sp", wcb_v[i], WC_scr[cb], writes=[B_wobuf[i]], sem=B_wobuf[i])
            bk = gb()
            first = True
            for tau in range(8):
                for sg_ in range(tau + 1):
                    A("pe", (lambda e, bk=bk, i=i, tau=tau, sg_=sg_, cb=cb, f=first: e.matmul(
                        PS.ap()[:, 512 * bk + tau:512 * bk + 512:8], lhsT=wkb[i].ap()[:, tau - sg_, :], rhs=UTO[:, cb, sg_:512:8], start=f, stop=False, skip_group_check=True)),
                      [B_wkb[i], B_uto], [PB[bk]])
                    first = False
            for tau in range(8):
                for j in range(4):
                    for ri in range(2):
                        last = (tau == 7 and j == 3 and ri == 1)
                        A("pe", (lambda e, bk=bk, i=i, tau=tau, j=j, ri=ri, cb=cb, last=last: e.matmul(
                            PS.ap()[32 * j:32 * j + 32, 512 * bk + tau:512 * bk + 512:8], lhsT=wcb_v[i][:, j, tau, ri, :],
                            rhs=hbuf.ap()[:, ri, 4 * cb + j, 0:64], start=False, stop=last, tile_position=(0, 32 * j), skip_group_check=True)),
                          [B_wobuf[i], B_hbuf], [PB[bk]])
            A("act", (lambda e, bk=bk, cb=cb: e.activation(out=ZG[:, cb, :], in_=bank(bk), func=AF.Gelu_apprx_tanh)), [PB[bk], B_utch], [B_zg])
        sig, gtmp = T(5), T(6)
        B_sig, B_gtmp = B_T8[5], B_T8[6]
        for nb in range(8):
            w, Bw = wload(w_glu_b[nb])
            bk = gb()
            for c in range(8):
                A("pe", (lambda e, bk=bk, c=c, w=w: e.matmul(bank(bk), lhsT=w[:, c, :], rhs=ZG[:, c, :], start=(c == 0), stop=(c == 7))), [Bw, B_zg], [PB[bk]])
            A("act", (lambda e, bk=bk, nb=nb: e.activation(out=sig, in_=bank(bk), func=AF.Sigmoid, bias=smc(C_BGLU + nb))), [PB[bk], B_sm], [B_sig])
            A("dve", (lambda e, nb=nb: e.tensor_tensor(out=gtmp, in0=sig, in1=ZG[:, nb, :], op=ALU.mult)), [B_sig, B_zg], [B_gtmp])
            A("dve", (lambda e, nb=nb: e.tensor_tensor(out=ZT[:, nb, :], in0=gtmp, in1=GST[:, nb, :], op=ALU.mult)), [B_gtmp, B_gst[nb]], [B_zt])

        nkt = 2 * s + 2
        fr0, fr1, fo0, fo1, frs = T(0), T(1), T(2), T(3), T(4)
        B_fr0, B_fr1, B_fo0, B_fo1, B_frs = B_T8[0], B_T8[1], B_T8[2], B_T8[3], B_T8[4]
        pb8 = pbuf_t.ap().rearrange("p a (b c) -> p (a b) c", b=2)
        units = [(h, kt, kb) for h in range(8) for kt in range(nkt) for kb in range(4)]
        kvslot = {}

        def issue_qk(n):
            h, kt, kb = units[n]
            if kb == 0:
                i = kvi[0]; kvi[0] = (i + 1) % NKV
                kvslot[(h, kt)] = i
                S.dma("sp", kbuf_t.ap()[:, i, :], K_scr[kt, h], reads=[B_K[kt][h]], writes=[B_kbuf[i]], sem=B_kbuf[i])
                S.dma("sp", vbuf_t.ap()[:, i, :], V_scr[kt, h], reads=[B_V[kt][h]], writes=[B_vbuf[i]], sem=B_vbuf[i])
            i = kvslot[(h, kt)]
            kb_ap = kbuf_t.ap()[:, i, :]
            mvar = 0 if kt == 2 * s else (1 if kt == 2 * s + 1 else None)
            st_ = n % 2
            for m in range(2):
                bk = 2 * st_ + m
                A("pe", (lambda e, kb_ap=kb_ap, m=m, kb=kb, bk=bk, h=h, mv=mvar: e.matmul(
                    bank(bk), lhsT=kb_ap[64 * m:64 * m + 64, kb * 128:(kb + 1) * 128], rhs=QT[64 * m:64 * m + 64, h, :],
                    start=True, stop=(mv is None))), [B_kbuf[i], B_qt[h]], [PB[bk]])
                if mvar is not None:
                    A("pe", (lambda e, m=m, kb=kb, bk=bk, mv=mvar: e.matmul(
                        bank(bk), lhsT=a64_b.ap()[64 * m:64 * m + 64, kb * 128:(kb + 1) * 128], rhs=bm_b.ap()[64 * m:64 * m + 64, mv * 512:(mv + 1) * 512],
                        start=False, stop=True)), [B_a64, B_bm], [PB[bk]])
            pp = pbi[0]; pbi[0] = (pp + 1) % 4
            A("act", (lambda e, pp=pp, st_=st_: e.activation(out=pbuf_t.ap()[:, pp, :], in_=PS.ap()[:, 1024 * st_:1024 * st_ + 1024], func=AF.Exp)),
              [PB[2 * st_], PB[2 * st_ + 1]], [B_pb8[2 * pp], B_pb8[2 * pp + 1]])
            slots = [2 * pp, 2 * pp + 1]
            return slots

        def issue_pv(n, slots):
            h, kt, kb = units[n]
            i = kvslot[(h, kt)]
            vb_ap = vbuf_t.ap()[:, i, :]
            st_ = (kt == 0 and kb == 0)
            sp_ = (kt == nkt - 1 and kb == 3)
            for m in range(2):
                pi_ = slots[m]
                A("pe", (lambda e, vb_ap=vb_ap, m=m, kb=kb, pi_=pi_, st_=st_, sp_=sp_: e.matmul(
                    bank(4 + m), lhsT=vb_ap[:, kb * 128:(kb + 1) * 128], rhs=pb8[:, pi_, :], start=st_, stop=sp_)),
                  [B_vbuf[i], B_pb8[pi_]], [PB[4 + m]])
                uidx = 4 * kt + kb
                if uidx % 3 == 2:
                    accp = saccp[m].ap(); Baccp = B_saccp[m]
                    if uidx == 2:
                        A("pool", (lambda e, accp=accp, pi_=pi_: e.tensor_copy(out=accp, in_=pb8[:, pi_, :])), [B_pb8[pi_]], [Baccp])
                    else:
                        A("pool", (lambda e, accp=accp, pi_=pi_: e.tensor_tensor(out=accp, in0=accp, in1=pb8[:, pi_, :], op=ALU.add)), [B_pb8[pi_]], [Baccp])
                else:
                    acc = sacc[h % 2][m].ap(); Bacc = B_sacc[h % 2][m]
                    if st_:
                        A("dve", (lambda e, acc=acc, pi_=pi_: e.tensor_copy(out=acc, in_=pb8[:, pi_, :])), [B_pb8[pi_]], [Bacc])
                    else:
                        A("dve", (lambda e, acc=acc, pi_=pi_: e.tensor_tensor(out=acc, in0=acc, in1=pb8[:, pi_, :], op=ALU.add)), [B_pb8[pi_]], [Bacc])
            if sp_:
                for m in range(2):
                    A("dve", (lambda e, m=m, hh=h % 2: e.tensor_tensor(out=sacc[hh][m].ap(), in0=sacc[hh][m].ap(), in1=saccp[m].ap(), op=ALU.add)), [B_saccp[m]], [B_sacc[h % 2][m]])
                    A("pe", (lambda e, m=m, hh=h % 2: e.matmul(bank(6 + m), lhsT=ones_f.ap(), rhs=sacc[hh][m].ap(), start=True, stop=True)), [B_onesf, B_sacc[h % 2][m]], [PB[6 + m]])
                A("dve", lambda e: e.reciprocal(out=fr0, in_=bank(6)), [PB[6]], [B_fr0])
                A("dve", lambda e: e.tensor_tensor(out=fo0, in0=bank(4), in1=fr0, op=ALU.mult), [PB[4], B_fr0], [B_fo0])
                A("dve", lambda e: e.reciprocal(out=fr1, in_=bank(7)), [PB[7]], [B_fr1])
                A("dve", lambda e: e.tensor_tensor(out=fo1, in0=bank(5), in1=fr1, op=ALU.mult), [PB[5], B_fr1], [B_fo1])
                A("dve", lambda e: e.scalar_tensor_tensor(out=fo0, in0=fo1, scalar=dc(D_NEGLAM), in1=fo0, op0=ALU.mult, op1=ALU.add), [B_fo1, B_cols], [B_fo0])
                A("act", lambda e: e.activation(out=qsq.ap(), in_=fo0, func=AF.Square), [B_fo0], [B_qsq])
                A("pe", lambda e: e.matmul(bank(6), lhsT=ones_b.ap(), rhs=qsq.ap(), start=True, stop=True), [B_ones, B_qsq], [PB[6]])
                A("act", lambda e: e.activation(out=frs, in_=bank(6), func=AF.Ln, scale=1.0 / 128.0, bias=EPS), [PB[6]], [B_frs])
                A("act", lambda e: e.activation(out=frs, in_=frs, func=AF.Exp, scale=-0.5), [], [B_frs])
                A("dve", lambda e: e.tensor_tensor(out=fo1, in0=fo0, in1=frs, op=ALU.mult), [B_fo0, B_frs], [B_fo1])
                A("dve", (lambda e, h=h: e.scalar_tensor_tensor(out=OT[:, h, :], in0=fo1, scalar=dc(D_GH8), in1=GAT[:, h, :], op0=ALU.mult, op1=ALU.mult)), [B_fo1, B_cols, B_gat[h]], [B_ot])

        pend = issue_qk(0)
        for n in range(len(units)):
            nxt = issue_qk(n + 1) if n + 1 < len(units) else None
            issue_pv(n, pend)
            pend = nxt

        for hf in range(2):
            S.dma("pool", xr_v, xo[s * 512:(s + 1) * 512, hf * 512:(hf + 1) * 512].rearrange("(sub p) c -> p sub c", p=128), writes=B_R[0:4], sem=B_R[0])
            obk = [gb() for _ in range(4)]
            for cq in range(4):
                blk = hf * 4 + cq
                i = blk % 2
                S.dma("sp", wobuf[i].ap().rearrange("p c j -> p (c j)"), w_out_b[blk], writes=[B_wobuf[i]], sem=B_wobuf[i])
                for sub in range(4):
                    for k in range(16):
                        src = OT[:, k, sub * 128:(sub + 1) * 128] if k < 8 else ZT[:, k - 8, sub * 128:(sub + 1) * 128]
                        Bsrc = B_ot if k < 8 else B_zt
                        A("pe", (lambda e, i=i, k=k, sub=sub, cq=cq, src=src, obk=obk: e.matmul(
                            bank(obk[sub], cq * 128, (cq + 1) * 128), lhsT=src, rhs=wobuf[i].ap()[:, k, :], start=(k == 0), stop=(k == 15))),
                          [B_wobuf[i], Bsrc], [PB[obk[sub]]])
            for sub in range(4):
                oi = oti[0]; oti[0] = (oi + 1) % 2
                A("dve", (lambda e, oi=oi, sub=sub, hf=hf, obk=obk: e.tensor_tensor(out=otile_v[oi], in0=bank(obk[sub]), in1=gate_row.ap()[:, hf * 512:(hf + 1) * 512], op=ALU.mult)),
                  [PB[obk[sub]], B_grow], [B_R[4 + oi]])
                A("dve", (lambda e, oi=oi, sub=sub: e.tensor_tensor(out=otile_v[oi], in0=otile_v[oi], in1=xr_v[:, sub, :], op=ALU.add)), B_R[0:4], [B_R[4 + oi]])
                o = S.dma("pool", out[s * 512 + sub * 128:s * 512 + (sub + 1) * 128, hf * 512:(hf + 1) * 512], otile_v[oi], reads=[B_R[4 + oi]], writes=[], sem=B_R[4 + oi])
                final_ops.append(o)

    if debug:
        for name, t, bufs in (("d_cols", cols, [B_cols]), ("d_hbuf", hbuf, [B_hbuf]),
                              ("d_arena", ARENA, [B_utcl, B_utch, B_htc, B_hto, B_uto] + B_qt + B_gat + B_gst),
                              ("d_cosT", cosT, [B_tab]), ("d_sinT", sinT, [B_tab]), ("d_rhoT", rhoT, [B_tab]), ("d_scon", scon, [B_scon]),
                              ("d_grow", gate_row, [B_grow])):
            bfl = S.buf(name, True)
            d = dbg_out(name, list(t.shape), t.dtype)
            final_ops.append(S.dma("sp", d, t.ap(), reads=bufs, writes=[], sem=bfl))

    with nc.Block() as block:
        S.emit(block, final_ops=final_ops)
    return nc, S


def _ssm_layouts(a_re, a_im, log_dt, b_re, b_im, c_re, c_im, d):
    f = np.float32
    AR_S = np.zeros((3, 128, 8, 2, 64), f)
    BT = np.zeros((2, 128, 8, 2, 64), f)
    for cb in range(8):
        for j in range(4):
            for g2 in range(2):
                g = 8 * cb + 2 * j + g2
                AR_S[0, 32 * j:32 * j + 32, cb, g2, :] = a_re[g][None, :]
                AR_S[1, 32 * j:32 * j + 32, cb, g2, :] = a_im[g][None, :]
                AR_S[2, 32 * j:32 * j + 32, cb, g2, :] = log_dt[g]
                BT[0, 32 * j + 16 * g2:32 * j + 16 * g2 + 16, cb, g2, :] = b_re[g].T
                BT[1, 32 * j + 16 * g2:32 * j + 16 * g2 + 16, cb, g2, :] = b_im[g].T
    ssmS = np.concatenate([AR_S.reshape(3, 128, 1024), BT.reshape(2, 128, 1024)], 0).transpose(1, 0, 2)
    K = np.zeros((4, 128, 32, 2, 16), f)
    T = np.zeros((4, 128, 32, 32), f)
    for pr in range(32):
        for g2 in range(2):
            g = 2 * pr + g2
            K[0, 64 * g2:64 * g2 + 64, pr, g2, :] = b_re[g]
            K[1, 64 * g2:64 * g2 + 64, pr, g2, :] = b_im[g]
            K[2, 64 * g2:64 * g2 + 64, pr, g2, :] = c_re[g].T
            K[3, 64 * g2:64 * g2 + 64, pr, g2, :] = c_im[g].T
            T[0, 64 * g2:64 * g2 + 64, pr, :] = a_re[g][:, None]
            T[1, 64 * g2:64 * g2 + 64, pr, :] = a_im[g][:, None]
            T[2, 64 * g2:64 * g2 + 64, pr, :] = log_dt[g]
    T[3] = np.arange(32, dtype=f)[None, None, :]
    ssmK = K.reshape(4, 128, 1024).transpose(1, 0, 2)
    ssmT = T.reshape(4, 128, 1024).transpose(1, 0, 2)
    dd = np.zeros((128, 8, 128), f)
    for cb in range(8):
        dd[np.arange(128), cb, np.arange(128)] = d[128 * cb:128 * (cb + 1)]
    return np.ascontiguousarray(ssmS), np.ascontiguousarray(ssmK), np.ascontiguousarray(ssmT), dd.reshape(128, 1024)


def _consts():
    f = np.float32
    ident = np.eye(128, dtype=f)
    perm = np.zeros((128, 128), f)
    for m in range(128):
        d = m % 64
        if d < 32:
            perm[m + 32, m] = -1.0
        else:
            perm[m - 32, m] = 1.0
    a64 = np.zeros((128, 512), f)
    for j in range(8):
        a64[j, 64 * j:64 * (j + 1)] = 1.0
        a64[64 + j, 64 * j:64 * (j + 1)] = 1.0
    diag = np.zeros((128, 512), f)
    none = np.zeros((128, 512), f)
    for j in range(8):
        row = np.where((np.arange(512) // 64) >= j, 0.0, NEG).astype(f)
        diag[j] = row
        diag[64 + j] = row
        none[j] = NEG
        none[64 + j] = NEG
    full = np.zeros((128, 512), f)
    pos0 = np.broadcast_to(np.arange(512, dtype=f)[None, :], (128, 512)).copy()
    invf = (1.0 / (np.float32(10000.0) ** (np.arange(0, 64, 2, dtype=f) / np.float32(64)))).astype(f)
    invf_col = np.tile(np.concatenate([invf, invf]), 2).astype(f)
    sel = np.zeros((128, 256), f)
    for m in range(2):
        sel[32 * m, 128 * m:128 * (m + 1)] = 1.0
        sel[64 + 32 * m, 128 * m:128 * (m + 1)] = 1.0
    return ident, perm, a64, (diag, none, full), pos0, invf_col, sel


def make_in_maps(inputs, NSTEP, nbatch):
    f = np.float32
    g = lambda k: np.asarray(inputs[k], dtype=f)
    x = g("x"); c = g("c")
    ident, perm, a64, (diag, none, full), pos0, invf_col, sel = _consts()
    ssmS, ssmK, ssmT, dd = _ssm_layouts(g("ssm_a_re")[0], g("ssm_a_im")[0], g("ssm_log_dt")[0], g("ssm_b_re")[0], g("ssm_b_im")[0],
                                        g("ssm_c_re")[0], g("ssm_c_im")[0], g("ssm_d")[0])
    lamv = np.broadcast_to(np.concatenate([g("lam_q1")[0], g("lam_k1")[0], g("lam_q2")[0], g("lam_k2")[0]])[None, :], (128, 256)).copy()
    b_ada = g("b_ada")[0]
    shared = {
        "w_ada": np.ascontiguousarray(g("w_ada")[0]), "w_in": np.ascontiguousarray(g("w_in")[0]),
        "w_glu": np.ascontiguousarray(g("w_glu")[0]), "w_out": np.ascontiguousarray(g("w_out")[0]),
        "lamv": lamv, "b_gate_row": np.broadcast_to(b_ada[None, 2048:3072], (128, 1024)).copy(),
        "ident": ident, "perm": perm, "sel": sel, "a64": a64, "pos0": pos0,
        "ssmS": ssmS, "ssmK": ssmK, "ssmT": ssmT, "ddiag": dd,
    }
    NT = 2 * NSTEP
    maps = []
    for core in range(2 * nbatch):
        b, r = core // 2, core % 2
        sm = np.zeros((128, 64), f)
        sm[:, 0:8] = c[b].reshape(8, 128).T
        sm[:, 8:32] = b_ada.reshape(24, 128).T
        sm[:, 32:40] = g("norm_g")[0].reshape(8, 128).T
        sm[:, 40] = np.tile(g("q_norm_g")[0], 2)
        sm[:, 41] = np.tile(g("k_norm_g")[0], 2)
        sm[:, 42] = g("head_norm_g")[0]
        sm[:, 43:51] = g("b_glu")[0].reshape(8, 128).T
        sm[:, 51] = invf_col
        sm[:, 52] = float(r)
        sm[:, 53] = float(1 - r)
        bo = np.zeros((128, 16), f)
        for s in range(NSTEP):
            bo[:, s] = 1024.0 * s + 512.0 * r
        bm = np.concatenate([diag, none] if r == 0 else [full, diag], axis=1)
        xb = np.ascontiguousarray(x[b][:NT * 512])
        m = dict(shared)
        m.update({"xc": xb, "xo": np.ascontiguousarray(xb.reshape(NT, 512, 1024)[r::2].reshape(-1, 1024)),
                  "smalls": sm, "base_own": bo, "bm": np.ascontiguousarray(bm)})
        maps.append(m)
    return maps


_NC_CACHE = {}


def run(inputs, NSTEP, nbatch, debug=False):
    key = (NSTEP, debug)
    if key not in _NC_CACHE:
        _NC_CACHE[key] = build_nc(NSTEP, debug)
    nc, S = _NC_CACHE[key]
    maps = make_in_maps(inputs, NSTEP, nbatch)
    res = run_bass_kernel_spmd(nc, maps, core_ids=list(range(2 * nbatch)))
    NT = 2 * NSTEP
    outp = np.zeros((nbatch, NT, 512, 1024), np.float32)
    for core in range(2 * nbatch):
        b, r = core // 2, core % 2
        outp[b, r::2] = np.asarray(res.results[core]["out"]).reshape(NSTEP, 512, 1024)
    return outp.reshape(nbatch, NT * 512, 1024), res


def kernel(**inputs):
    out, _ = run(inputs, 8, 4)
    return out
```

```python
import numpy as np
import concourse.bass as bass
import concourse.mybir as mybir
from concourse.bass_utils import run_bass_kernel_spmd

F32 = mybir.dt.float32
BF16 = mybir.dt.bfloat16
I32 = mybir.dt.int32
AF = mybir.ActivationFunctionType
ALU = mybir.AluOpType
AX = mybir.AxisListType
PI = float(np.pi)
TWO_PI = float(2 * np.pi)
EPS = 1e-6
NEG = -30000.0
LAM_INIT = 0.2


class Buf:
    __slots__ = ("name", "sem", "dcount", "last_w", "readers")

    def __init__(self, name, sem=None):
        self.name = name
        self.sem = sem
        self.dcount = 0
        self.last_w = None
        self.readers = []


class Op:
    __slots__ = ("eng", "fn", "deps", "is_dma", "sem", "val", "need_inc", "idx")


class Sched:
    ENG = ("pe", "act", "dve", "pool", "sp")

    def __init__(self, nc):
        self.nc = nc
        self.ops = {e: [] for e in self.ENG}
        self.esem = {e: nc.alloc_semaphore("es_" + e) for e in self.ENG}
        self.nbuf = 0
        self.pending_dma = []

    def buf(self, name, dma=False):
        self.nbuf += 1
        return Buf(name, self.nc.alloc_semaphore("ds%d" % self.nbuf) if dma else None)

    def _mk(self, eng, fn, reads, writes, is_dma=False):
        op = Op()
        op.eng = eng
        op.fn = fn
        op.is_dma = is_dma
        op.sem = None
        op.val = None
        op.need_inc = False
        cand = []
        for b in reads:
            if b.last_w is not None:
                cand.append(b.last_w)
        for b in writes:
            if b.last_w is not None:
                cand.append(b.last_w)
            cand.extend(b.readers)
        best = {}
        for d in cand:
            k = ("d", id(d.sem)) if d.is_dma else ("e", d.eng)
            cur = best.get(k)
            if cur is None or (d.val > cur.val if d.is_dma else d.idx > cur.idx):
                best[k] = d
        op.deps = list(best.values())
        for b in reads:
            if not is_dma:
                b.readers = [r for r in b.readers if (r.is_dma or r.eng != eng)]
            b.readers.append(op)
        for b in writes:
            b.last_w = op
            b.readers = []
        op.idx = len(self.ops[eng])
        self.ops[eng].append(op)
        return op

    def op(self, eng, fn, reads=(), writes=()):
        return self._mk(eng, fn, reads, writes)

    def dma(self, eng, out, in_, reads=(), writes=(), sem=None):
        assert sem is not None and sem.sem is not None
        sem.dcount += 1
        op = self._mk(eng, lambda e: e.dma_start(out=out, in_=in_), reads, writes, is_dma=True)
        op.sem = sem.sem
        op.val = 16 * sem.dcount
        self.pending_dma.append(op)
        return op

    def barrier(self, dummies):
        arr = []
        for e in ("act", "dve", "pool"):
            b = Buf("bar_" + e)
            op = self._mk(e, dummies[e], [], [b])
            op.deps = list(op.deps) + list(self.pending_dma)
            arr.append(b)
        self.pending_dma = []
        for e in self.ENG:
            self._mk(e, None, arr, [])

    def emit(self, block, final_ops=()):
        for e in self.ENG:
            for op in self.ops[e]:
                for d in op.deps:
                    if not d.is_dma:
                        if d.eng == "pe" and op.eng == "pe" and not op.is_dma:
                            continue
                        d.need_inc = True
        for op in final_ops:
            if not op.is_dma:
                op.need_inc = True
        self.stats = {}
        for e in self.ENG:
            c = 0
            for op in self.ops[e]:
                if op.is_dma:
                    continue
                if op.need_inc:
                    assert op.fn is not None
                    c += 1
                    op.sem = self.esem[e]
                    op.val = c
            self.stats[e] = (len(self.ops[e]), c)

        def run(e, eng):
            waited = {}
            for op in self.ops[e]:
                need = {}
                for d in op.deps:
                    if (not d.is_dma) and d.eng == "pe" and e == "pe" and not op.is_dma:
                        continue
                    k = id(d.sem)
                    if k not in need or need[k][1] < d.val:
                        need[k] = (d.sem, d.val)
                for k, (s, v) in need.items():
                    if waited.get(k, 0) < v:
                        eng.wait_ge(s, v)
                        waited[k] = v
                if op.fn is None:
                    continue
                ins = op.fn(eng)
                if op.is_dma:
                    ins.then_inc(op.sem, 16)
                elif op.need_inc:
                    ins.then_inc(op.sem, 1)
            if e == "sp":
                for op in final_ops:
                    eng.wait_ge(op.sem, op.val)

        block.tensor(lambda eng: run("pe", eng))
        block.scalar(lambda eng: run("act", eng))
        block.vector(lambda eng: run("dve", eng))
        block.gpsimd(lambda eng: run("pool", eng))
        block.sync(lambda eng: run("sp", eng))


def build_nc(NSTEP, debug=False):
    NT = 2 * NSTEP
    SLEN = NT * 512
    nc = bass.Bass("TRN2", target_bir_lowering=False)
    S = Sched(nc)
    dbg_kind = "ExternalOutput" if debug else "Internal"

    def din(name, shape, dt=F32):
        return nc.dram_tensor(name, list(shape), dt, kind="ExternalInput").ap()

    xc = din("xc", [SLEN, 1024])
    xo = din("xo", [NSTEP * 512, 1024])
    out = nc.dram_tensor("out", [NSTEP * 512, 1024], F32, kind="ExternalOutput").ap()
    w_ada = din("w_ada", [1024, 3072])
    w_in = din("w_in", [1024, 6144])
    w_glu = din("w_glu", [1024, 1024])
    w_out = din("w_out", [2048, 1024])
    smalls = din("smalls", [128, 64])
    lamv = din("lamv", [128, 256])
    b_gate_row = din("b_gate_row", [128, 1024])
    base_own = din("base_own", [128, 16])
    ident_in = din("ident", [128, 128])
    perm_in = din("perm", [128, 128])
    a64_in = din("a64", [128, 512])
    bm_in = din("bm", [128, 1024])
    pos0_in = din("pos0", [128, 512])
    sel_in = din("sel", [128, 256])
    ssmS = din("ssmS", [128, 5, 1024])
    ssmK = din("ssmK", [128, 4, 1024])
    ssmT = din("ssmT", [128, 4, 1024])
    ddiag = din("ddiag", [128, 1024])
    w_in_b = nc.dram_tensor("w_in_b", [48, 128, 1024], BF16).ap()
    w_glu_b = nc.dram_tensor("w_glu_b", [8, 128, 1024], BF16).ap()
    w_out_b = nc.dram_tensor("w_out_b", [8, 128, 2048], BF16).ap()
    K_scr = nc.dram_tensor("K_scr", [NT, 8, 128, 512], BF16, kind=dbg_kind).ap()
    V_scr = nc.dram_tensor("V_scr", [NT, 8, 128, 512], BF16, kind=dbg_kind).ap()
    WS_scr = nc.dram_tensor("WS_scr", [8, 128, 8, 2, 128], BF16, kind=dbg_kind).ap()
    WC_scr = nc.dram_tensor("WC_scr", [8, 128, 4, 8, 2, 32], BF16, kind=dbg_kind).ap()
    WK_scr = nc.dram_tensor("WK_scr", [8, 128, 8, 128], BF16, kind=dbg_kind).ap()
    dbg_outs = {}

    def dbg_out(name, shape, dt=F32):
        dbg_outs[name] = nc.dram_tensor(name, list(shape), dt, kind="ExternalOutput").ap()
        return dbg_outs[name]

    def sb(name, shape, dt=F32):
        return nc.alloc_sbuf_tensor("sb_" + name, list(shape), dt)

    PS = nc.alloc_psum_tensor("ps", [128, 4096], F32)
    PB = [S.buf("pb%d" % i) for i in range(8)]

    def bank(i, lo=0, hi=512):
        return PS.ap()[:, 512 * i + lo:512 * i + hi]

    gbc = [0]

    def gb():
        gbc[0] = (gbc[0] + 1) % 8
        return gbc[0]

    ident = sb("ident", [128, 128]); B_ident = S.buf("ident", True)
    perm_f = sb("perm_f", [128, 128]); B_permf = S.buf("permf", True)
    perm_b = sb("perm_b", [128, 128], BF16); B_perm = S.buf("perm")
    ones_b = sb("ones_b", [128, 128], BF16); B_ones = S.buf("ones")
    ones_f = sb("ones_f", [128, 128]); B_onesf = S.buf("onesf")
    sacc = [[sb("sacc%d%d" % (a_, m_), [128, 512]) for m_ in range(2)] for a_ in range(2)]
    B_sacc = [[S.buf("sacc%d%d" % (a_, m_)) for m_ in range(2)] for a_ in range(2)]
    saccp = [sb("saccp%d" % m_, [128, 512]) for m_ in range(2)]
    B_saccp = [S.buf("saccp%d" % m_) for m_ in range(2)]
    blk1_b = sb("blk1_b", [128, 128], BF16); B_blk1 = S.buf("blk1")
    B_a64f = S.buf("a64f", True)
    a64_b = sb("a64_b", [128, 512], BF16); B_a64 = S.buf("a64")
    B_bmf = S.buf("bmf", True)
    bm_b = sb("bm_b", [128, 1024], BF16); B_bm = S.buf("bm")
    pos0 = sb("pos0", [128, 512]); B_pos0 = S.buf("pos0", True)
    sel_f = sb("sel_f", [128, 256]); B_sel = S.buf("sel", True)
    sm = sb("smalls", [128, 64]); B_sm = S.buf("sm", True)
    bown = sb("bown", [128, 16]); B_bown = S.buf("bown", True)
    gate_row = sb("gate_row", [128, 1024]); B_grow = S.buf("grow", True)
    cols = sb("cols", [128, 64]); B_cols = S.buf("cols")
    dmy_a = sb("dmy_a", [128, 1]); dmy_d = sb("dmy_d", [128, 1]); dmy_p = sb("dmy_p", [128, 1])
    C_CT = 0
    C_BADA = 8
    C_NG = 32
    C_GQ = 40; C_GK = 41; C_GH = 42
    C_BGLU = 43
    C_INVF = 51; C_R = 52; C_OMR = 53
    D_A1 = 0
    D_SHIFT = 8
    D_GQ8 = 16; D_NEGLAM = 17; D_GH8 = 18
    D_SC = 20
    D_MOD = 28

    def smc(i, n=1):
        return sm.ap()[:, i:i + n]

    def dc(i, n=1):
        return cols.ap()[:, i:i + n]

    ARENA = sb("arena", [128, 32768], BF16)
    B_arena_init = S.buf("arena_init")
    arena_f = ARENA.ap().bitcast(F32)

    def scr(i, n=512):
        return arena_f[:, 512 * i:512 * i + n]

    def aview(off, n):
        return ARENA.ap()[:, off:off + n]

    HTC = aview(0, 4096).rearrange("p (c t) -> p c t", c=8); B_htc = S.buf("htc")
    HTO = aview(4096, 4096).rearrange("p (c t) -> p c t", c=8); B_hto = S.buf("hto")
    UTC = aview(8192, 8192).rearrange("p (c t) -> p c t", c=8); B_utcl = S.buf("utcl"); B_utch = S.buf("utch")
    B_utc = [B_utcl, B_utch]
    QT = aview(16384, 4096).rearrange("p (c t) -> p c t", c=8); B_qt = [S.buf("qt%d" % h) for h in range(8)]
    GAT = aview(20480, 4096).rearrange("p (c t) -> p c t", c=8); B_gat = [S.buf("gat%d" % h) for h in range(8)]
    UTO = aview(24576, 4096).rearrange("p (c t) -> p c t", c=8); B_uto = S.buf("uto")
    GST = aview(28672, 4096).rearrange("p (c t) -> p c t", c=8); B_gst = [S.buf("gst%d" % h) for h in range(8)]
    OT = HTC
    B_ot = B_htc
    ZG = UTC[:, :, 0:512]
    ZT = UTC[:, :, 512:1024]
    B_zg = B_utcl
    B_zt = B_utch

    R16 = sb("r16", [128, 4096]); B_R = [S.buf("r16_%d" % i, True) for i in range(8)]
    xt_v = R16.ap().rearrange("p (s c) -> p s c", s=4)
    gin_v = R16.ap().rearrange("p (r g c) -> p r g c", r=2, g=4)
    B_gin = [[B_R[4 * ri + sg] for sg in range(4)] for ri in range(2)]
    xr_v = R16.ap()[:, 0:2048].rearrange("p (s c) -> p s c", s=4)
    otile_v = [R16.ap()[:, 2048 + 512 * k:2048 + 512 * (k + 1)] for k in range(2)]
    TMP = sb("tmp", [128, 8, 512]); B_T8 = [S.buf("tmp%d" % i) for i in range(8)]

    def T(i):
        return TMP.ap()[:, i, :]

    sqjunk = sb("sqjunk", [128, 1024], BF16); B_junk = S.buf("junk")
    ss = sb("ss", [128, 8]); B_ss = S.buf("ss")
    ropeC = sb("ropeC", [128, 512]); ropeS = sb("ropeS", [128, 512]); B_rope = S.buf("rope")
    ropeCo = sb("ropeCo", [128, 512]); ropeSo = sb("ropeSo", [128, 512]); B_ropeo = S.buf("ropeo")
    NWB = 4
    wbuf = [sb("wbuf%d" % i, [128, 8, 128], BF16) for i in range(NWB)]
    B_wbuf = [S.buf("wbuf%d" % i, True) for i in range(NWB)]
    wbi = [0]
    wobuf = [sb("wobuf%d" % i, [128, 16, 128], BF16) for i in range(2)]
    B_wobuf = [S.buf("wobuf%d" % i, True) for i in range(2)]
    qsq2 = [sb("qsq%d" % i, [128, 512], BF16) for i in range(3)]; qknb2 = [sb("qknb%d" % i, [128, 512], BF16) for i in range(3)]
    B_qsq2 = [S.buf("qsq%d" % i) for i in range(3)]; B_qknb2 = [S.buf("qknb%d" % i) for i in range(3)]
    qx = sb("qx", [128, 512]); B_qx = S.buf("qx")
    qsq = qsq2[0]; B_qsq = B_qsq2[0]
    NKS = 4
    kst = [sb("kst%d" % i, [128, 512], BF16) for i in range(NKS)]
    B_kst = [S.buf("kst%d" % i, True) for i in range(NKS)]
    vst = [sb("vst%d" % i, [128, 512], BF16) for i in range(NKS)]
    B_vst = [S.buf("vst%d" % i, True) for i in range(NKS)]
    ksi = [0]; vsi = [0]
    NKV = 4
    kbuf_t = sb("kbuf", [128, NKV, 512], BF16); vbuf_t = sb("vbuf", [128, NKV, 512], BF16)
    B_kbuf = [S.buf("kbuf%d" % i, True) for i in range(NKV)]
    B_vbuf = [S.buf("vbuf%d" % i, True) for i in range(NKV)]
    kvi = [0]
    NPB = 4
    pbuf_t = sb("pbuf", [128, NPB, 1024], BF16)
    B_pb8 = [S.buf("pb8_%d" % i) for i in range(8)]
    B_pbuf = [[B_pb8[2 * i], B_pb8[2 * i + 1]] for i in range(NPB)]
    pbi = [0]
    gsel_v = pbuf_t.ap().rearrange("p a b -> p (a b)").bitcast(F32).rearrange("p (r i c) -> p r i c", r=2, i=2)
    cosT = sb("cosT", [128, 32, 32]); sinT = sb("sinT", [128, 32, 32]); rhoT = sb("rhoT", [128, 32, 32])
    B_tab = S.buf("tab")
    scon = sb("scon", [128, 8, 32]); B_scon = S.buf("scon")
    hend = sb("hend", [128, 2, 32]); B_hend = S.buf("hend")
    hmid = sb("hmid", [128, 2, 32]); B_hmid = S.buf("hmid")
    stmp = sb("stmp", [128, 8, 16]); B_stmp = S.buf("stmp")
    hbuf = sb("hbuf", [128, 2, 32, 65], BF16); B_hbuf = S.buf("hbuf")
    wsb_v = [kbuf_t.ap().rearrange("p a b -> p (a b)").rearrange("p (s r c) -> p s r c", s=8, r=2),
             vbuf_t.ap().rearrange("p a b -> p (a b)").rearrange("p (s r c) -> p s r c", s=8, r=2)]
    BL_wsb = [B_kbuf, B_vbuf]
    wcb_v = [wobuf[i].ap().rearrange("p a b -> p (a b)").rearrange("p (j t r c) -> p j t r c", j=4, t=8, r=2) for i in range(2)]
    wkb = [sb("wkb%d" % i, [128, 8, 128], BF16) for i in range(2)]; B_wkb = [S.buf("wkb%d" % i, True) for i in range(2)]
    oti = [0]

    def A(e, fn, r=(), w=()):
        return S.op(e, fn, r, w)

    def sincos(x_ap, sin_out, cos_out, tmpk, tmpi, tmpm, rb, wb, tb, add16=False):
        rb = list(rb); wb = list(wb); tb = list(tb)
        if add16:
            A("dve", lambda e: e.tensor_scalar(out=x_ap, in0=x_ap, scalar1=float(16 * np.pi), scalar2=None, op0=ALU.add), rb, tb)
        A("dve", lambda e: e.tensor_scalar(out=tmpk, in0=x_ap, scalar1=float(1.0 / TWO_PI), scalar2=None, op0=ALU.mult), rb, tb)
        A("dve", lambda e: e.tensor_copy(out=tmpi, in_=tmpk), [], tb)
        A("dve", lambda e: e.tensor_copy(out=tmpk, in_=tmpi), [], tb)
        A("dve", lambda e: e.scalar_tensor_tensor(out=x_ap, in0=tmpk, scalar=-TWO_PI, in1=x_ap, op0=ALU.mult, op1=ALU.add), [], tb)
        A("dve", lambda e: e.tensor_scalar(out=tmpm, in0=x_ap, scalar1=PI, scalar2=-TWO_PI, op0=ALU.is_gt, op1=ALU.mult), [], tb)
        A("dve", lambda e: e.tensor_tensor(out=tmpk, in0=x_ap, in1=tmpm, op=ALU.add), [], tb)
        A("act", lambda e: e.activation(out=sin_out, in_=tmpk, func=AF.Sin), tb, wb)
        A("dve", lambda e: e.tensor_scalar(out=tmpk, in0=x_ap, scalar1=float(PI / 2), scalar2=None, op0=ALU.add), [], tb)
        A("dve", lambda e: e.tensor_scalar(out=tmpm, in0=tmpk, scalar1=PI, scalar2=-TWO_PI, op0=ALU.is_gt, op1=ALU.mult), [], tb)
        A("dve", lambda e: e.tensor_tensor(out=tmpk, in0=tmpk, in1=tmpm, op=ALU.add), [], tb)
        A("act", lambda e: e.activation(out=cos_out, in_=tmpk, func=AF.Sin), tb, wb)

    def cmul(o_re, o_im, a_re, a_im, b_re, b_im, t1, t2, rb, wb, tb, eng="dve"):
        rb = list(rb); wb = list(wb); tb = list(tb)
        A(eng, lambda e: e.tensor_tensor(out=t1, in0=a_re, in1=b_re, op=ALU.mult), rb, tb)
        A(eng, lambda e: e.tensor_tensor(out=t2, in0=a_im, in1=b_im, op=ALU.mult), rb, tb)
        A(eng, lambda e: e.tensor_tensor(out=o_re, in0=t1, in1=t2, op=ALU.subtract), tb, wb)
        A(eng, lambda e: e.tensor_tensor(out=t1, in0=a_re, in1=b_im, op=ALU.mult), rb, tb)
        A(eng, lambda e: e.tensor_tensor(out=t2, in0=a_im, in1=b_re, op=ALU.mult), rb, tb)
        A(eng, lambda e: e.tensor_tensor(out=o_im, in0=t1, in1=t2, op=ALU.add), tb, wb)

    bar_dummies = {"act": lambda e: e.activation(out=dmy_a.ap(), in_=ident.ap()[:, 0:1], func=AF.Copy),
                   "dve": lambda e: e.memset(dmy_d.ap(), 0.0), "pool": lambda e: e.memset(dmy_p.ap(), 0.0)}
    Bz = B_arena_init

    r16f = R16.ap()
    tmpf = TMP.ap().rearrange("p a b -> p (a b)")
    pbf = pbuf_t.ap().rearrange("p a b -> p (a b)")
    cin = [r16f[:, 0:2048], r16f[:, 2048:4096], tmpf[:, 0:2048]]
    tmpb = tmpf[:, 2048:4096].bitcast(BF16)
    cout = [tmpb[:, 0:2048], tmpb[:, 2048:4096], pbf[:, 0:2048], pbf[:, 2048:4096]]
    B_cin = [S.buf("cin%d" % i, True) for i in range(3)]
    B_cout = [S.buf("cout%d" % i, True) for i in range(4)]
    jobs = []
    w_in_v = w_in.rearrange("(c p) j -> p c j", p=128)
    w_glu_v = w_glu.rearrange("(c p) j -> p c j", p=128)
    w_out_v = w_out.rearrange("(c p) j -> p c j", p=128)
    for jb in range(48):
        jobs.append((w_in_v[:, :, jb * 128:(jb + 1) * 128], w_in_b[jb], 8))
    for jb in range(8):
        jobs.append((w_glu_v[:, :, jb * 128:(jb + 1) * 128], w_glu_b[jb], 8))
    for jb in range(8):
        jobs.append((w_out_v[:, :, jb * 128:(jb + 1) * 128], w_out_b[jb], 16))
    precast_ops = []
    for n, (src, dst, nch) in enumerate(jobs):
        i = n % 3
        o = n % 4
        ne = nch * 128
        S.dma("sp", cin[i][:, 0:ne].rearrange("p (c j) -> p c j", c=nch), src, writes=[B_cin[i]], sem=B_cin[i])
        A("pool", (lambda e, i=i, o=o, ne=ne: e.tensor_copy(out=cout[o][:, 0:ne], in_=cin[i][:, 0:ne])), [B_cin[i]], [B_cout[o]])
        S.dma("pool", dst[:, 0:ne], cout[o][:, 0:ne], reads=[B_cout[o]], writes=[], sem=B_cout[o])

    precast_dmas = list(S.pending_dma)
    S.pending_dma = []

    a64_f = scr(20); bm_f = scr(22, 1024)
    S.dma("act", ident.ap(), ident_in, writes=[B_ident], sem=B_ident)
    S.dma("act", perm_f.ap(), perm_in, writes=[B_permf], sem=B_permf)
    S.dma("act", a64_f, a64_in, writes=[B_a64f], sem=B_a64f)
    S.dma("act", bm_f, bm_in, writes=[B_bmf], sem=B_bmf)
    S.dma("act", pos0.ap(), pos0_in, writes=[B_pos0], sem=B_pos0)
    S.dma("act", sel_f.ap(), sel_in, writes=[B_sel], sem=B_sel)
    S.dma("act", sm.ap(), smalls, writes=[B_sm], sem=B_sm)
    S.dma("act", bown.ap(), base_own, writes=[B_bown], sem=B_bown)
    S.dma("act", gate_row.ap(), b_gate_row, writes=[B_grow], sem=B_grow)
    A("dve", lambda e: e.tensor_copy(out=perm_b.ap(), in_=perm_f.ap()), [B_permf], [B_perm])
    A("dve", lambda e: e.tensor_copy(out=a64_b.ap(), in_=a64_f), [B_a64f], [B_a64])
    A("dve", lambda e: e.tensor_copy(out=bm_b.ap(), in_=bm_f), [B_bmf], [B_bm])
    A("dve", lambda e: e.memset(ones_b.ap(), 1.0), [], [B_ones])
    A("dve", lambda e: e.memset(ones_f.ap(), 1.0), [], [B_onesf])
    A("dve", lambda e: e.memset(blk1_b.ap(), 0.0), [], [B_blk1])
    A("dve", lambda e: e.memset(blk1_b.ap()[0:64, 0:64], 1.0), [], [B_blk1])
    A("dve", lambda e: e.memset(blk1_b.ap()[64:128, 64:128], 1.0), [], [B_blk1])
    A("dve", lambda e: e.memset(hend.ap(), 0.0), [], [B_hend])
    A("dve", lambda e: e.memset(cols.ap(), 0.0), [], [B_cols])
    A("dve", lambda e: e.memset(hbuf.ap(), 0.0), [], [B_hbuf])

    lamt = scr(0, 256); lamp = scr(1, 128); lsum = scr(2, 4)
    B_lam = S.buf("lam", True)
    S.dma("act", lamt, lamv, writes=[B_lam], sem=B_lam)
    A("dve", lambda e: e.tensor_tensor(out=lamp[:, 0:64], in0=lamt[:, 0:64], in1=lamt[:, 64:128], op=ALU.mult), [B_lam], [Bz])
    A("dve", lambda e: e.tensor_tensor(out=lamp[:, 64:128], in0=lamt[:, 128:192], in1=lamt[:, 192:256], op=ALU.mult), [B_lam], [Bz])
    A("dve", lambda e: e.reduce_sum(out=lsum[:, 0:1], in_=lamp[:, 0:64], axis=AX.X), [], [Bz])
    A("dve", lambda e: e.reduce_sum(out=lsum[:, 1:2], in_=lamp[:, 64:128], axis=AX.X), [], [Bz])
    A("act", lambda e: e.activation(out=lsum[:, 2:4], in_=lsum[:, 0:2], func=AF.Exp), [Bz], [Bz])
    A("dve", lambda e: e.scalar_tensor_tensor(out=dc(D_NEGLAM), in0=lsum[:, 3:4], scalar=-LAM_INIT, in1=lsum[:, 2:3], op0=ALU.add, op1=ALU.subtract), [Bz], [B_cols])
    A("dve", lambda e: e.tensor_scalar(out=dc(D_GQ8), in0=smc(C_GQ), scalar1=0.125, scalar2=None, op0=ALU.mult), [B_sm], [B_cols])
    A("dve", lambda e: e.tensor_scalar(out=dc(D_GH8), in0=smc(C_GH), scalar1=float(1.0 - LAM_INIT), scalar2=None, op0=ALU.mult), [B_sm], [B_cols])

    A("act", lambda e: e.activation(out=dc(D_SC, 8), in_=smc(C_CT, 8), func=AF.Silu), [B_sm, B_cols], [B_cols])
    wa = [scr(4 + 2 * i, 1024).rearrange("p (c j) -> p c j", c=8) for i in range(3)]
    B_wa = [S.buf("wa%d" % i, True) for i in range(3)]
    w_ada_v = w_ada.rearrange("(c p) j -> p c j", p=128)
    bkA = gb()
    for jb in range(24):
        i = jb % 3
        S.dma("act", wa[i], w_ada_v[:, :, jb * 128:(jb + 1) * 128], writes=[B_wa[i]], sem=B_wa[i])
        for c in range(8):
            A("pe", (lambda e, i=i, c=c, jb=jb: e.matmul(bank(bkA, jb, jb + 1), lhsT=wa[i][:, c, :], rhs=dc(D_SC + c), start=(c == 0), stop=(c == 7))),
              [B_wa[i], B_cols], [PB[bkA]])
    A("dve", lambda e: e.tensor_tensor(out=dc(D_MOD, 24), in0=bank(bkA, 0, 24), in1=smc(C_BADA, 24), op=ALU.add), [PB[bkA], B_sm], [B_cols])
    A("dve", lambda e: e.scalar_tensor_tensor(out=dc(D_A1, 8), in0=dc(D_MOD + 8, 8), scalar=1.0, in1=smc(C_NG, 8), op0=ALU.add, op1=ALU.mult), [B_sm], [B_cols])
    A("dve", lambda e: e.tensor_copy(out=dc(D_SHIFT, 8), in_=dc(D_MOD, 8)), [], [B_cols])
    scbc = scr(12, 1024).rearrange("p (c j) -> p c j", c=8)
    for c in range(8):
        A("dve", (lambda e, c=c: e.tensor_copy(out=scbc[:, c, :], in_=dc(D_SC + c).to_broadcast([128, 128]))), [B_cols], [Bz])
    wg = [scr(14 + 2 * i, 1024) for i in range(3)]
    B_wg = [S.buf("wg%d" % i, True) for i in range(3)]
    bkG = [gb(), gb()]
    for c in range(8):
        i = c % 3
        S.dma("act", wg[i], w_ada[c * 128:(c + 1) * 128, 2048:3072], writes=[B_wg[i]], sem=B_wg[i])
        for hf in range(2):
            A("pe", (lambda e, i=i, c=c, hf=hf: e.matmul(bank(bkG[hf]), lhsT=scbc[:, c, :], rhs=wg[i][:, hf * 512:(hf + 1) * 512], start=(c == 0), stop=(c == 7))),
              [B_wg[i], Bz], [PB[bkG[hf]]])
    for hf in range(2):
        A("dve", (lambda e, hf=hf: e.tensor_tensor(out=gate_row.ap()[:, hf * 512:(hf + 1) * 512], in0=bank(bkG[hf]), in1=gate_row.ap()[:, hf * 512:(hf + 1) * 512], op=ALU.add)),
          [PB[bkG[hf]], B_grow], [B_grow])

    S.barrier(bar_dummies)
    S.pending_dma = precast_dmas + S.pending_dma

    B_T = S.buf("Tl", True)
    tl = [scr(2 * i, 1024) for i in range(4)]
    for i in range(4):
        S.dma("act", tl[i], ssmT[:, i, :], writes=[B_T, Bz], sem=B_T)
    dtT = scr(8, 1024); LrT = scr(10, 1024); thT = scr(12, 1024)
    tk = scr(14, 1024); tm = scr(16, 1024); ti = scr(18, 1024).bitcast(I32); tx = scr(20, 1024)
    A("act", lambda e: e.activation(out=dtT, in_=tl[2], func=AF.Exp), [B_T], [Bz])
    A("dve", lambda e: e.tensor_tensor(out=LrT, in0=tl[0], in1=dtT, op=ALU.mult), [B_T, Bz], [Bz])
    A("dve", lambda e: e.tensor_tensor(out=thT, in0=tl[1], in1=dtT, op=ALU.mult), [B_T, Bz], [Bz])

    def reduce_angle(x):
        n = x.shape[-1]
        A("dve", lambda e: e.tensor_scalar(out=x, in0=x, scalar1=float(16 * np.pi), scalar2=None, op0=ALU.add), [Bz], [Bz])
        A("dve", lambda e: e.tensor_scalar(out=tk[:, 0:n], in0=x, scalar1=float(1.0 / TWO_PI), scalar2=None, op0=ALU.mult), [Bz], [Bz])
        A("dve", lambda e: e.tensor_copy(out=ti[:, 0:n], in_=tk[:, 0:n]), [Bz], [Bz])
        A("dve", lambda e: e.tensor_copy(out=tk[:, 0:n], in_=ti[:, 0:n]), [Bz], [Bz])
        A("dve", lambda e: e.scalar_tensor_tensor(out=x, in0=tk[:, 0:n], scalar=-TWO_PI, in1=x, op0=ALU.mult, op1=ALU.add), [Bz], [Bz])

    def sincos_i(x, sin_out, cos_out, wb):
        n = x.shape[-1]
        sincos(x, sin_out, cos_out, tk[:, 0:n], ti[:, 0:n], tm[:, 0:n], [Bz], wb, [Bz], add16=True)

    reduce_angle(thT)
    A("dve", lambda e: e.tensor_scalar(out=tx, in0=thT, scalar1=8.0, scalar2=None, op0=ALU.mult), [Bz], [Bz])
    reduce_angle(tx)
    misc = scr(22)
    phc = misc[:, 0:32]; a32 = misc[:, 32:64]; c32 = misc[:, 64:96]; s32 = misc[:, 96:128]
    txv = tx.rearrange("p (a m) -> p a m", m=32)
    A("dve", lambda e: e.tensor_copy(out=phc, in_=txv[:, :, 0]), [Bz], [Bz])
    A("dve", lambda e: e.tensor_tensor(out=tx, in0=tx, in1=tl[3], op=ALU.mult), [Bz, B_T], [Bz])
    cosT_f = cosT.ap().rearrange("p a m -> p (a m)")
    sinT_f = sinT.ap().rearrange("p a m -> p (a m)")
    rhoT_f = rhoT.ap().rearrange("p a m -> p (a m)")
    sincos_i(tx, sinT_f, cosT_f, [B_tab, Bz])
    rho = dtT
    A("act", lambda e: e.activation(out=rho, in_=LrT, func=AF.Exp, scale=8.0), [Bz], [Bz])
    A("dve", lambda e: e.tensor_scalar(out=tm, in0=tl[3], scalar1=0.0, scalar2=None, op0=ALU.is_gt), [B_T, Bz], [Bz])
    A("dve", lambda e: e.tensor_tensor(out=rhoT_f, in0=rho, in1=tm, op=ALU.mult), [Bz], [B_tab, Bz])
    rhov = rho.rearrange("p (a m) -> p a m", m=32)
    A("dve", lambda e: e.tensor_tensor(out=scon.ap()[:, 0, :], in0=rhov[:, :, 1], in1=cosT.ap()[:, :, 1], op=ALU.mult), [Bz, B_tab], [B_scon, Bz])
    A("dve", lambda e: e.tensor_tensor(out=scon.ap()[:, 1, :], in0=rhov[:, :, 1], in1=sinT.ap()[:, :, 1], op=ALU.mult), [Bz, B_tab], [B_scon, Bz])
    A("dve", lambda e: e.tensor_scalar(out=a32, in0=phc, scalar1=32.0, scalar2=None, op0=ALU.mult), [Bz], [Bz])
    sincos_i(a32, s32, c32, [Bz])
    A("dve", lambda e: e.tensor_tensor(out=scon.ap()[:, 2, :], in0=rhov[:, :, 1], in1=c32, op=ALU.mult), [Bz], [B_scon, Bz])
    A("dve", lambda e: e.tensor_tensor(out=scon.ap()[:, 3, :], in0=rhov[:, :, 1], in1=s32, op=ALU.mult), [Bz], [B_scon, Bz])
    akre = scr(23)[:, 0:288].rearrange("p (k a) -> p k a", k=9)
    akim = scr(24)[:, 0:288].rearrange("p (k a) -> p k a", k=9)
    cmp_ = scr(25)
    Lrc = cmp_[:, 0:32]; thc = cmp_[:, 32:64]; magc = cmp_[:, 64:96]; a1s = cmp_[:, 96:128]; a1c = cmp_[:, 128:160]
    arc = cmp_[:, 160:192]; aic = cmp_[:, 192:224]; fre = cmp_[:, 224:256]; fim = cmp_[:, 256:288]
    t1c = cmp_[:, 288:320]; t2c = cmp_[:, 320:352]; nrc = cmp_[:, 352:384]; denc = cmp_[:, 384:416]; t3c = cmp_[:, 416:448]
    LrTv = LrT.rearrange("p (a m) -> p a m", m=32)
    thTv = thT.rearrange("p (a m) -> p a m", m=32)
    A("dve", lambda e: e.tensor_copy(out=Lrc, in_=LrTv[:, :, 0]), [Bz], [Bz])
    A("dve", lambda e: e.tensor_copy(out=thc, in_=thTv[:, :, 0]), [Bz], [Bz])
    A("dve", lambda e: e.tensor_copy(out=arc, in_=tl[0].rearrange("p (a m) -> p a m", m=32)[:, :, 0]), [B_T, Bz], [Bz])
    A("dve", lambda e: e.tensor_copy(out=aic, in_=tl[1].rearrange("p (a m) -> p a m", m=32)[:, :, 0]), [B_T, Bz], [Bz])
    A("act", lambda e: e.activation(out=magc, in_=Lrc, func=AF.Exp), [Bz], [Bz])
    sincos_i(thc, a1s, a1c, [Bz])
    A("dve", lambda e: e.tensor_tensor(out=akre[:, 1, :], in0=magc, in1=a1c, op=ALU.mult), [Bz], [Bz])
    A("dve", lambda e: e.tensor_tensor(out=akim[:, 1, :], in0=magc, in1=a1s, op=ALU.mult), [Bz], [Bz])
    A("dve", lambda e: e.memset(akre[:, 0, :], 1.0), [Bz], [Bz])
    A("dve", lambda e: e.memset(akim[:, 0, :], 0.0), [Bz], [Bz])
    for k in range(2, 9):
        cmul(akre[:, k, :], akim[:, k, :], akre[:, k - 1, :], akim[:, k - 1, :], akre[:, 1, :], akim[:, 1, :], t1c, t2c, [Bz], [Bz], [Bz])
    A("dve", lambda e: e.tensor_scalar(out=nrc, in0=akre[:, 1, :], scalar1=-1.0, scalar2=None, op0=ALU.add), [Bz], [Bz])
    A("dve", lambda e: e.tensor_tensor(out=t1c, in0=arc, in1=arc, op=ALU.mult), [Bz], [Bz])
    A("dve", lambda e: e.tensor_tensor(out=t2c, in0=aic, in1=aic, op=ALU.mult), [Bz], [Bz])
    A("dve", lambda e: e.tensor_tensor(out=denc, in0=t1c, in1=t2c, op=ALU.add), [Bz], [Bz])
    A("dve", lambda e: e.reciprocal(out=denc, in_=denc), [Bz], [Bz])
    A("dve", lambda e: e.tensor_tensor(out=t1c, in0=nrc, in1=arc, op=ALU.mult), [Bz], [Bz])
    A("dve", lambda e: e.tensor_tensor(out=t2c, in0=akim[:, 1, :], in1=aic, op=ALU.mult), [Bz], [Bz])
    A("dve", lambda e: e.tensor_tensor(out=t3c, in0=t1c, in1=t2c, op=ALU.add), [Bz], [Bz])
    A("dve", lambda e: e.tensor_tensor(out=fre, in0=t3c, in1=denc, op=ALU.mult), [Bz], [Bz])
    A("dve", lambda e: e.tensor_tensor(out=t1c, in0=akim[:, 1, :], in1=aic, op=ALU.mult), [Bz], [Bz])
    A("dve", lambda e: e.tensor_tensor(out=t2c, in0=nrc, in1=aic, op=ALU.mult), [Bz], [Bz])
    A("dve", lambda e: e.tensor_tensor(out=t3c, in0=t1c, in1=t2c, op=ALU.subtract), [Bz], [Bz])
    A("dve", lambda e: e.tensor_tensor(out=fim, in0=t3c, in1=denc, op=ALU.mult), [Bz], [Bz])

    kin = [[scr(2 * i)[:, 128 * q:128 * (q + 1)] for q in range(4)] for i in range(2)]
    B_kin = [S.buf("kin%d" % i, True) for i in range(2)]
    bbr = scr(8); bbi = scr(9)
    car_l = [scr(10), scr(16)]; cai_l = [scr(11), scr(17)]
    kt1 = scr(12)[:, 0:128]; kt2 = scr(12)[:, 128:256]; kp1 = scr(13)[:, 0:128]; kp2 = scr(13)[:, 128:256]
    akimn = scr(4)[:, 0:288].rearrange("p (k a) -> p k a", k=9)
    ddt = scr(14, 1024); B_dd = S.buf("dd", True)
    B_bb = S.buf("bb"); B_car_l = [S.buf("car0"), S.buf("car1")]; B_cai_l = [S.buf("cai0"), S.buf("cai1")]; B_kd = S.buf("kd"); B_kp = S.buf("kp"); B_akn = S.buf("akn")
    S.dma("act", ddt, ddiag, writes=[B_dd, Bz], sem=B_dd)
    A("dve", lambda e: e.tensor_scalar(out=akimn, in0=akim, scalar1=-1.0, scalar2=None, op0=ALU.mult), [Bz], [B_akn, Bz])
    A("dve", lambda e: e.memset(bbr, 0.0), [], [Bz, B_bb])
    A("dve", lambda e: e.memset(bbi, 0.0), [], [Bz, B_bb])
    for q_ in range(2):
        A("dve", (lambda e, q_=q_: e.memset(car_l[q_], 0.0)), [], [Bz, B_car_l[q_]])
        A("dve", (lambda e, q_=q_: e.memset(cai_l[q_], 0.0)), [], [Bz, B_cai_l[q_]])
    A("dve", lambda e: e.memset(kp1, 0.0), [], [Bz, B_kp])
    A("dve", lambda e: e.memset(kt1, 0.0), [], [Bz, B_kd])
    wcst = wcb_v
    wkst = [wkb[i].ap() for i in range(2)]

    def bc4(col_ap, n=32):
        return bass.AP(tensor=col_ap.tensor, offset=col_ap.offset, ap=[list(col_ap.ap[0]), [1, 4], [0, n]])

    def v4(ap128):
        return ap128.rearrange("p (j c) -> p j c", j=4)

    def dg(t512):
        return bass.AP(tensor=t512.tensor, offset=t512.offset, ap=[list(t512.ap[0]), [160, 4], [1, 32]])

    for cb in range(8):
        i = cb % 2
        for q in range(4):
            S.dma("act", kin[i][q], ssmK[:, q, cb * 128:(cb + 1) * 128], writes=[B_kin[i], Bz], sem=B_kin[i])
        bpr, bpi, cpr, cpi = [v4(kin[i][q]) for q in range(4)]
        fr_b = bc4(fre[:, 4 * cb:4 * cb + 4]); fi_b = bc4(fim[:, 4 * cb:4 * cb + 4])
        cmul(dg(bbr), dg(bbi), bpr, bpi, fr_b, fi_b, v4(kt1), v4(kt2), [B_kin[i], Bz], [B_bb], [B_kd])
        for lag in range(9):
            ar_b = bc4(akre[:, lag, 4 * cb:4 * cb + 4]); ai_b = bc4(akimn[:, lag, 4 * cb:4 * cb + 4])
            car = car_l[lag % 2]; cai = cai_l[lag % 2]; B_car = B_car_l[lag % 2]; B_cai = B_cai_l[lag % 2]
            A("dve", (lambda e, cpr=cpr, ar_b=ar_b: e.tensor_tensor(out=v4(kt1), in0=cpr, in1=ar_b, op=ALU.mult)), [B_kin[i], Bz], [B_kd])
            A("dve", (lambda e, cpi=cpi, ai_b=ai_b: e.tensor_tensor(out=v4(kt2), in0=cpi, in1=ai_b, op=ALU.mult)), [B_kin[i], B_akn], [B_kd])
            A("dve", (lambda e, car=car: e.tensor_tensor(out=dg(car), in0=v4(kt1), in1=v4(kt2), op=ALU.add)), [B_kd], [B_car])
            A("dve", (lambda e, cpr=cpr, ai_b=ai_b: e.tensor_tensor(out=v4(kp1), in0=cpr, in1=ai_b, op=ALU.mult)), [B_kin[i], B_akn], [B_kp])
            A("dve", (lambda e, cpi=cpi, ar_b=ar_b: e.tensor_tensor(out=v4(kp2), in0=cpi, in1=ar_b, op=ALU.mult)), [B_kin[i], Bz], [B_kp])
            A("dve", (lambda e, cai=cai: e.tensor_tensor(out=dg(cai), in0=v4(kp1), in1=v4(kp2), op=ALU.subtract)), [B_kp], [B_cai])
            if lag >= 1:
                A("act", (lambda e, i=i, lag=lag, car=car: e.activation(out=wcst[i][:, :, lag - 1, 0, :], in_=dg(car), func=AF.Copy)), [B_car], [B_wobuf[i]])
                A("act", (lambda e, i=i, lag=lag, cai=cai: e.activation(out=wcst[i][:, :, lag - 1, 1, :], in_=dg(cai), func=AF.Copy)), [B_cai], [B_wobuf[i]])
            if lag <= 7:
                bk = gb()
                for j in range(4):
                    A("pe", (lambda e, bk=bk, j=j, car=car: e.matmul(bank(bk, 0, 128), lhsT=bbr[:, j * 128:(j + 1) * 128], rhs=car[:, j * 128:(j + 1) * 128], start=(j == 0), stop=False)),
                      [B_bb, B_car], [PB[bk]])
                for j in range(4):
                    A("pe", (lambda e, bk=bk, j=j, cai=cai: e.matmul(bank(bk, 0, 128), lhsT=bbi[:, j * 128:(j + 1) * 128], rhs=cai[:, j * 128:(j + 1) * 128], start=False, stop=(j == 3))),
                      [B_bb, B_cai], [PB[bk]])
                if lag == 0:
                    A("dve", (lambda e, bk=bk, i=i, cb=cb: e.tensor_tensor(out=wkst[i][:, 0, :], in0=bank(bk, 0, 128), in1=ddt[:, cb * 128:(cb + 1) * 128], op=ALU.add)),
                      [PB[bk], B_dd], [B_wkb[i]])
                else:
                    A("act", (lambda e, bk=bk, i=i, lag=lag: e.activation(out=wkst[i][:, lag, :], in_=bank(bk, 0, 128), func=AF.Copy)), [PB[bk]], [B_wkb[i]])
        S.dma("act", WC_scr[cb], wcst[i], reads=[B_wobuf[i]], writes=[], sem=B_wobuf[i])
        S.dma("act", WK_scr[cb], wkst[i], reads=[B_wkb[i]], writes=[], sem=B_wkb[i])
    A("dve", lambda e: e.memset(dmy_d.ap(), 0.0), [], [Bz, B_bb, B_kd, B_kp, B_akn] + B_car_l + B_cai_l + B_kin)

    B_Sl = S.buf("Sl", True)
    sl = [scr(2 * i, 1024) for i in range(5)]
    for i in range(5):
        S.dma("act", sl[i], ssmS[:, i, :], writes=[B_Sl, Bz], sem=B_Sl)
    tA = scr(10, 1024); tB = scr(12, 1024); tC = scr(20, 1024); tD = scr(22, 1024)
    tE = scr(24, 1024); tF = scr(26, 1024); tG = scr(28, 1024); tH = scr(30, 1024)
    wsst = wsb_v
    A("act", lambda e: e.activation(out=tA, in_=sl[2], func=AF.Exp), [B_Sl, Bz], [Bz])
    A("dve", lambda e: e.tensor_tensor(out=tB, in0=sl[0], in1=tA, op=ALU.mult), [B_Sl, Bz], [Bz])
    A("dve", lambda e: e.tensor_tensor(out=tC, in0=sl[1], in1=tA, op=ALU.mult), [B_Sl, Bz], [Bz])
    A("act", lambda e: e.activation(out=tA, in_=tB, func=AF.Exp), [Bz], [Bz])
    sincos_i(tC, tD, tE, [Bz])
    a1re = tB; a1im = tC
    A("dve", lambda e: e.tensor_tensor(out=a1re, in0=tA, in1=tE, op=ALU.mult), [Bz], [Bz])
    A("dve", lambda e: e.tensor_tensor(out=a1im, in0=tA, in1=tD, op=ALU.mult), [Bz], [Bz])
    nr = tA; st1 = tD; st2 = tE; rden = tF; cre = tG; cim = tH
    A("dve", lambda e: e.tensor_scalar(out=nr, in0=a1re, scalar1=-1.0, scalar2=None, op0=ALU.add), [Bz], [Bz])
    A("dve", lambda e: e.tensor_tensor(out=st1, in0=sl[0], in1=sl[0], op=ALU.mult), [B_Sl, Bz], [Bz])
    A("dve", lambda e: e.tensor_tensor(out=st2, in0=sl[1], in1=sl[1], op=ALU.mult), [B_Sl, Bz], [Bz])
    A("dve", lambda e: e.tensor_tensor(out=rden, in0=st1, in1=st2, op=ALU.add), [Bz], [Bz])
    A("dve", lambda e: e.reciprocal(out=rden, in_=rden), [Bz], [Bz])
    A("dve", lambda e: e.tensor_tensor(out=st1, in0=nr, in1=sl[0], op=ALU.mult), [B_Sl, Bz], [Bz])
    A("dve", lambda e: e.tensor_tensor(out=st2, in0=a1im, in1=sl[1], op=ALU.mult), [B_Sl, Bz], [Bz])
    A("dve", lambda e: e.tensor_tensor(out=st1, in0=st1, in1=st2, op=ALU.add), [Bz], [Bz])
    A("dve", lambda e: e.tensor_tensor(out=cre, in0=st1, in1=rden, op=ALU.mult), [Bz], [Bz])
    A("dve", lambda e: e.tensor_tensor(out=st1, in0=a1im, in1=sl[0], op=ALU.mult), [B_Sl, Bz], [Bz])
    A("dve", lambda e: e.tensor_tensor(out=st2, in0=nr, in1=sl[1], op=ALU.mult), [B_Sl, Bz], [Bz])
    A("dve", lambda e: e.tensor_tensor(out=st1, in0=st1, in1=st2, op=ALU.subtract), [Bz], [Bz])
    A("dve", lambda e: e.tensor_tensor(out=cim, in0=st1, in1=rden, op=ALU.mult), [Bz], [Bz])
    cur = (cre, cim)
    nxt = (tA, tF)
    WS_v = WS_scr.rearrange("cb p s r c -> p cb s r c")
    c8 = lambda ap: ap.rearrange("p (cb c) -> p cb c", cb=8)
    wsst8 = [kbuf_t.ap().rearrange("p a b -> p (a b)").rearrange("p (cb r c) -> p cb r c", cb=8, r=2),
             vbuf_t.ap().rearrange("p a b -> p (a b)").rearrange("p (cb r c) -> p cb r c", cb=8, r=2)]
    for k in range(8):
        sg = 7 - k
        i = k % 2
        o_re = wsst8[i][:, :, 0, :]
        o_im = wsst8[i][:, :, 1, :]
        cmul(o_re, o_im, c8(cur[0]), c8(cur[1]), c8(sl[3]), c8(sl[4]), c8(st1), c8(st2), [B_Sl, Bz], BL_wsb[i], [Bz])
        S.dma("act", WS_v[:, :, sg, :, :], wsst8[i], reads=BL_wsb[i], writes=[], sem=BL_wsb[i][0])
        if k < 7:
            cmul(nxt[0], nxt[1], cur[0], cur[1], a1re, a1im, st1, st2, [Bz], [Bz], [Bz])
            cur, nxt = nxt, cur

    S.barrier(bar_dummies)

    B_K = [[S.buf("K%d_%d" % (t, h)) for h in range(8)] for t in range(NT)]
    B_V = [[S.buf("V%d_%d" % (t, h)) for h in range(8)] for t in range(NT)]
    final_ops = []

    def wload(src):
        i = wbi[0]
        wbi[0] = (i + 1) % NWB
        S.dma("sp", wbuf[i].ap().rearrange("p c j -> p (c j)"), src, writes=[B_wbuf[i]], sem=B_wbuf[i])
        return wbuf[i].ap(), B_wbuf[i]

    def front_a(src_rows):
        for sub in range(4):
            S.dma("sp", xt_v[:, sub, :], src_rows[sub * 128:(sub + 1) * 128, :], writes=[B_R[2 * sub], B_R[2 * sub + 1]], sem=B_R[2 * sub])
        for sub in range(4):
            A("act", (lambda e, sub=sub: e.activation(out=sqjunk.ap(), in_=xt_v[:, sub, :], func=AF.Square, accum_out=ss.ap()[:, sub:sub + 1])),
              [B_R[2 * sub], B_R[2 * sub + 1]], [B_junk, B_ss])
        A("act", lambda e: e.activation(out=ss.ap()[:, 4:8], in_=ss.ap()[:, 0:4], func=AF.Ln, scale=1.0 / 1024.0, bias=EPS), [B_ss], [B_ss])
        A("act", lambda e: e.activation(out=ss.ap()[:, 4:8], in_=ss.ap()[:, 4:8], func=AF.Exp, scale=-0.5), [B_ss], [B_ss])
        for sub in range(4):
            A("dve", (lambda e, sub=sub: e.tensor_scalar(out=xt_v[:, sub, :], in0=xt_v[:, sub, :], scalar1=ss.ap()[:, 4 + sub:5 + sub], scalar2=None, op0=ALU.mult)),
              [B_ss], [B_R[2 * sub], B_R[2 * sub + 1]])

    def front_b(base):
        for c in range(8):
            bk = gb()
            for sub in range(4):
                A("pe", (lambda e, bk=bk, sub=sub, c=c: e.transpose(bank(bk, sub * 128, (sub + 1) * 128), xt_v[:, sub, c * 128:(c + 1) * 128], ident.ap())),
                  [B_R[2 * sub], B_R[2 * sub + 1], B_ident], [PB[bk]])
            A("dve", (lambda e, bk=bk, c=c: e.tensor_scalar(out=HTC[:, c, :], in0=bank(bk), scalar1=dc(D_A1 + c), scalar2=dc(D_SHIFT + c), op0=ALU.mult, op1=ALU.add)),
              [PB[bk], B_cols], [B_htc])
        tb = B_T8[0:4]
        A("dve", lambda e: e.tensor_scalar(out=T(0), in0=pos0.ap(), scalar1=float(base), scalar2=smc(C_INVF), op0=ALU.add, op1=ALU.mult), [B_pos0, B_sm], tb)
        sincos(T(0), ropeS.ap(), ropeC.ap(), T(1), T(2).bitcast(I32), T(3), [], [B_rope], tb)

    qkset = [0]
    QS = [[T(0), T(1), T(2)], [T(3), T(4), T(5)], [T(6), T(7), qx.ap()]]
    BQS = [[B_T8[0], B_T8[1], B_T8[2]], [B_T8[3], B_T8[4], B_T8[5]], [B_T8[6], B_T8[7], B_qx]]

    def qknorm_a(bk, gcol):
        z = qkset[0]; qkset[0] = (z + 1) % 3
        qrs, qkn = QS[z][0], QS[z][1]
        B_qrs, B_qkn = BQS[z][0], BQS[z][1]
        qsq_, qknb_ = qsq2[z].ap(), qknb2[z].ap()
        Bq_, Bkb_ = B_qsq2[z], B_qknb2[z]
        A("act", lambda e: e.activation(out=qsq_, in_=bank(bk), func=AF.Square), [PB[bk]], [Bq_])
        b2 = gb()
        A("pe", lambda e: e.matmul(bank(b2), lhsT=blk1_b.ap(), rhs=qsq_, start=True, stop=True), [B_blk1, Bq_], [PB[b2]])
        A("act", lambda e: e.activation(out=qrs, in_=bank(b2), func=AF.Ln, scale=1.0 / 64.0, bias=EPS), [PB[b2]], [B_qrs])
        A("act", lambda e: e.activation(out=qrs, in_=qrs, func=AF.Exp, scale=-0.5), [], [B_qrs])
        A("dve", lambda e: e.scalar_tensor_tensor(out=qkn, in0=bank(bk), scalar=gcol, in1=qrs, op0=ALU.mult, op1=ALU.mult), [PB[bk], B_qrs, B_cols, B_sm], [B_qkn])
        A("act", lambda e: e.activation(out=qknb_, in_=qkn, func=AF.Copy), [B_qkn], [Bkb_])
        return (z, 0)

    def qknorm_b(stt, C, Sn, Bcs, out_ap, Bout):
        z, _ = stt
        qkn, qt1, qt2 = QS[z][1], QS[z][2], QS[z][0]
        B_qkn, B_qt1, B_qt2 = BQS[z][1], BQS[z][2], BQS[z][0]
        qknb_ = qknb2[z].ap(); Bkb_ = B_qknb2[z]
        b3 = gb()
        A("pe", lambda e: e.matmul(bank(b3), lhsT=perm_b.ap(), rhs=qknb_, start=True, stop=True), [B_perm, Bkb_], [PB[b3]])
        A("dve", lambda e: e.tensor_tensor(out=qt1, in0=qkn, in1=C, op=ALU.mult), [B_qkn, Bcs], [B_qt1])
        A("dve", lambda e: e.tensor_tensor(out=qt2, in0=bank(b3), in1=Sn, op=ALU.mult), [PB[b3], Bcs], [B_qt2])
        A("dve", lambda e: e.tensor_tensor(out=out_ap, in0=qt1, in1=qt2, op=ALU.add), [B_qt1, B_qt2], [Bout])

    def qk_pipeline(blk0, hT, BhT, gcol, C, Sn, Bcs, sink, heads=range(8)):
        heads = list(heads)
        nb = len(heads)
        banks_ = {}
        sta = {}
        for t_ in range(nb + 2):
            if t_ < nb:
                banks_[t_] = proj_fm(blk0 + heads[t_], hT, BhT)
            if 1 <= t_ <= nb:
                sta[t_ - 1] = qknorm_a(banks_[t_ - 1], gcol)
            if 2 <= t_ <= nb + 1:
                out_ap, Bout, post = sink(heads[t_ - 2])
                qknorm_b(sta[t_ - 2], C, Sn, Bcs, out_ap, Bout)
                if post is not None:
                    post()

    def proj_fm(blk, hT, BhT):
        w, Bw = wload(w_in_b[blk])
        bk = gb()
        for c in range(8):
            A("pe", (lambda e, bk=bk, c=c, w=w: e.matmul(bank(bk), lhsT=w[:, c, :], rhs=hT[:, c, :], start=(c == 0), stop=(c == 7))), [Bw, BhT], [PB[bk]])
        return bk

    for s in range(NSTEP):
        for half in range(2):
            g = 2 * s + half
            if half == 0:
                front_a(xc[g * 512:(g + 1) * 512, :])
            front_b(512.0 * g)
            if half == 0:
                front_a(xc[(g + 1) * 512:(g + 2) * 512, :])
            hc = HTC.rearrange("p c t -> p (c t)"); ho = HTO.rearrange("p c t -> p (c t)")
            if half == 0:
                A("dve", lambda e: e.tensor_scalar(out=ho, in0=hc, scalar1=smc(C_OMR), scalar2=None, op0=ALU.mult), [B_htc, B_sm], [B_hto])
                A("dve", lambda e: e.tensor_scalar(out=ropeCo.ap(), in0=ropeC.ap(), scalar1=smc(C_OMR), scalar2=None, op0=ALU.mult), [B_rope, B_sm], [B_ropeo])
                A("dve", lambda e: e.tensor_scalar(out=ropeSo.ap(), in0=ropeS.ap(), scalar1=smc(C_OMR), scalar2=None, op0=ALU.mult), [B_rope, B_sm], [B_ropeo])
            else:
                A("dve", lambda e: e.scalar_tensor_tensor(out=ho, in0=hc, scalar=smc(C_R), in1=ho, op0=ALU.mult, op1=ALU.add), [B_htc, B_sm], [B_hto])
                A("dve", lambda e: e.scalar_tensor_tensor(out=ropeCo.ap(), in0=ropeC.ap(), scalar=smc(C_R), in1=ropeCo.ap(), op0=ALU.mult, op1=ALU.add), [B_rope, B_sm], [B_ropeo])
                A("dve", lambda e: e.scalar_tensor_tensor(out=ropeSo.ap(), in0=ropeS.ap(), scalar=smc(C_R), in1=ropeSo.ap(), op0=ALU.mult, op1=ALU.add), [B_rope, B_sm], [B_ropeo])
            def ksink(h, g=g):
                i = ksi[0]; ksi[0] = (i + 1) % NKS
                return kst[i].ap(), B_kst[i], (lambda i=i, h=h, g=g: S.dma("pool", K_scr[g, h], kst[i].ap(), reads=[B_kst[i]], writes=[B_K[g][h]], sem=B_kst[i]))
            qk_pipeline(8, HTC, B_htc, smc(C_GK), ropeC.ap(), ropeS.ap(), B_rope, ksink)
            for h in range(8):
                w, Bw = wload(w_in_b[16 + h])
                bk = gb()
                for sub in range(4):
                    for c in range(8):
                        A("pe", (lambda e, bk=bk, c=c, w=w, sub=sub: e.matmul(bank(bk, sub * 128, (sub + 1) * 128), lhsT=HTC[:, c, sub * 128:(sub + 1) * 128], rhs=w[:, c, :], start=(c == 0), stop=(c == 7))),
                          [Bw, B_htc], [PB[bk]])
                i = vsi[0]; vsi[0] = (i + 1) % NKS
                A("act", (lambda e, bk=bk, i=i: e.activation(out=vst[i].ap(), in_=bank(bk), func=AF.Copy)), [PB[bk]], [B_vst[i]])
                S.dma("pool", V_scr[g, h], vst[i].ap(), reads=[B_vst[i]], writes=[B_V[g][h]], sem=B_vst[i])
            for cb in range(8):
                bk = proj_fm(32 + cb, HTC, B_htc)
                A("act", (lambda e, bk=bk, cb=cb, half=half: e.activation(out=UTC[:, cb, half * 512:(half + 1) * 512], in_=bank(bk), func=AF.Copy)), [PB[bk]], [B_utc[half]])
        A("dve", lambda e: e.tensor_scalar(out=UTO, in0=UTC[:, :, 0:512], scalar1=smc(C_OMR), scalar2=None, op0=ALU.mult), [B_utcl, B_sm], [B_uto])
        A("dve", lambda e: e.scalar_tensor_tensor(out=UTO, in0=UTC[:, :, 512:1024], scalar=smc(C_R), in1=UTO, op0=ALU.mult, op1=ALU.add), [B_utch, B_sm], [B_uto])

        for hf in range(2):
            for cq in range(4):
                cb = 4 * hf + cq
                i = cb % 2
                S.dma("sp", wsb_v[i], WS_scr[cb], writes=BL_wsb[i], sem=BL_wsb[i][0])
                bks = [(4 * (cb % 2) + j) for j in range(4)]
                for ri in range(2):
                    for sg_ in range(8):
                        for j in range(4):
                            A("pe", (lambda e, i=i, ri=ri, sg_=sg_, j=j, cb=cb, bks=bks: e.matmul(
                                bank(bks[j], ri * 128, (ri + 1) * 128), lhsT=wsb_v[i][32 * j:32 * j + 32, sg_, ri, :],
                                rhs=UTC[32 * j:32 * j + 32, cb, sg_:1024:8], start=(ri == 0 and sg_ == 0), stop=(ri == 1 and sg_ == 7),
                                tile_position=(32 * j, 0), skip_group_check=True)),
                              BL_wsb[i] + B_utc, [PB[bks[j]]])
                b0 = bks[0]
                s_re = bass.AP(tensor=PS, offset=512 * b0, ap=[[4096, 128], [512, 4], [32, 4], [1, 32]])
                s_im = bass.AP(tensor=PS, offset=512 * b0 + 128, ap=[[4096, 128], [512, 4], [32, 4], [1, 32]])
                c_t = bass.AP(tensor=cosT, offset=32 * 4 * cb, ap=[[1024, 128], [32, 4], [0, 4], [1, 32]])
                s_t = bass.AP(tensor=sinT, offset=32 * 4 * cb, ap=[[1024, 128], [32, 4], [0, 4], [1, 32]])
                rt = [T(q).rearrange("p (j s m) -> p j s m", j=4, s=4) for q in range(4)]
                pbs = [PB[b] for b in bks]
                A("dve", (lambda e, s_re=s_re, c_t=c_t, rt=rt: e.tensor_tensor(out=rt[0], in0=s_re, in1=c_t, op=ALU.mult)), pbs + [B_tab], [B_T8[0]])
                A("dve", (lambda e, s_im=s_im, s_t=s_t, rt=rt: e.tensor_tensor(out=rt[1], in0=s_im, in1=s_t, op=ALU.mult)), pbs + [B_tab], [B_T8[1]])
                A("dve", (lambda e, s_im=s_im, c_t=c_t, rt=rt: e.tensor_tensor(out=rt[2], in0=s_im, in1=c_t, op=ALU.mult)), pbs + [B_tab], [B_T8[2]])
                A("dve", (lambda e, s_re=s_re, s_t=s_t, rt=rt: e.tensor_tensor(out=rt[3], in0=s_re, in1=s_t, op=ALU.mult)), pbs + [B_tab], [B_T8[3]])
                g_re = bass.AP(tensor=R16, offset=0 * 2048 + 32 * 4 * cq, ap=[[4096, 128], [32, 4], [512, 4], [1, 32]])
                g_im = bass.AP(tensor=R16, offset=1 * 2048 + 32 * 4 * cq, ap=[[4096, 128], [32, 4], [512, 4], [1, 32]])
                A("pool", (lambda e, g_re=g_re, rt=rt: e.tensor_tensor(out=g_re, in0=rt[0], in1=rt[1], op=ALU.add)), [B_T8[0], B_T8[1]], B_gin[0])
                A("pool", (lambda e, g_im=g_im, rt=rt: e.tensor_tensor(out=g_im, in0=rt[2], in1=rt[3], op=ALU.subtract)), [B_T8[2], B_T8[3]], B_gin[1])
            qk_pipeline(0, HTO, B_hto, dc(D_GQ8), ropeCo.ap(), ropeSo.ap(), B_ropeo, (lambda h: (QT[:, h, :], B_qt[h], None)), heads=range(4 * hf, 4 * hf + 4))
            p0 = 16 * hf
            rho_h = rhoT.ap()[:, p0:p0 + 16, :].rearrange("p a m -> p (a m)")
            cos31 = cosT.ap()[:, p0:p0 + 16, 31]; sin31 = sinT.ap()[:, p0:p0 + 16, 31]
            st = [stmp.ap()[:, q, :] for q in range(8)]
            for sg in range(4):
                gre = gin_v[:, 0, sg, :]; gim = gin_v[:, 1, sg, :]
                gre3 = gre.rearrange("p (a m) -> p a m", m=32); gim3 = gim.rearrange("p (a m) -> p a m", m=32)
                Bg = [B_gin[0][sg], B_gin[1][sg]]
                if sg == 0:
                    pr_re = hend.ap()[:, 0, p0:p0 + 16]; pr_im = hend.ap()[:, 1, p0:p0 + 16]
                    m_re = scon.ap()[:, 0, p0:p0 + 16]; m_im = scon.ap()[:, 1, p0:p0 + 16]
                    rb = [B_hend, B_scon]
                else:
                    pr_re = gin_v[:, 0, sg - 1, :].rearrange("p (a m) -> p a m", m=32)[:, :, 31]
                    pr_im = gin_v[:, 1, sg - 1, :].rearrange("p (a m) -> p a m", m=32)[:, :, 31]
                    m_re = scon.ap()[:, 2, p0:p0 + 16]; m_im = scon.ap()[:, 3, p0:p0 + 16]
                    rb = [B_gin[0][sg - 1], B_gin[1][sg - 1], B_scon]
                cmul(st[0], st[1], pr_re, pr_im, m_re, m_im, st[2], st[3], rb, [B_stmp], [B_stmp])
                A("dve", (lambda e, gre3=gre3, st=st: e.tensor_tensor(out=gre3[:, :, 0], in0=gre3[:, :, 0], in1=st[0], op=ALU.add)), [B_stmp], [Bg[0]])
                A("dve", (lambda e, gim3=gim3, st=st: e.tensor_tensor(out=gim3[:, :, 0], in0=gim3[:, :, 0], in1=st[1], op=ALU.add)), [B_stmp], [Bg[1]])
                A("dve", (lambda e, gre=gre, rho_h=rho_h: e.tensor_tensor_scan(out=gre, data0=rho_h, data1=gre, initial=0.0, op0=ALU.mult, op1=ALU.add)), [B_tab], [Bg[0]])
                A("dve", (lambda e, gim=gim, rho_h=rho_h: e.tensor_tensor_scan(out=gim, data0=rho_h, data1=gim, initial=0.0, op0=ALU.mult, op1=ALU.add)), [B_tab], [Bg[1]])
                w_, i_ = sg // 2, sg % 2
                for ri, gsrc in ((0, gre), (1, gim)):
                    dst = gsel_v[:, ri, i_, :]
                    Bd = B_pbuf[2 * ri + i_]
                    if w_ == 0:
                        A("dve", (lambda e, dst=dst, gsrc=gsrc: e.tensor_scalar(out=dst, in0=gsrc, scalar1=smc(C_OMR), scalar2=None, op0=ALU.mult)), [Bg[ri], B_sm], Bd)
                    else:
                        A("dve", (lambda e, dst=dst, gsrc=gsrc: e.scalar_tensor_tensor(out=dst, in0=gsrc, scalar=smc(C_R), in1=dst, op0=ALU.mult, op1=ALU.add)), [Bg[ri], B_sm], Bd)
                if sg == 1 or sg == 3:
                    g31r = gre3[:, :, 31]; g31i = gim3[:, :, 31]
                    tgt = hmid if sg == 1 else hend
                    Btgt = B_hmid if sg == 1 else B_hend
                    if sg == 3:
                        for ri in range(2):
                            A("dve", (lambda e, ri=ri, st=st: e.tensor_scalar(out=st[4 + ri], in0=hend.ap()[:, ri, p0:p0 + 16], scalar1=smc(C_OMR), scalar2=None, op0=ALU.mult)), [B_hend, B_sm], [B_stmp])
                            A("dve", (lambda e, ri=ri, st=st: e.scalar_tensor_tensor(out=hbuf.ap()[:, ri, p0:p0 + 16, 0], in0=hmid.ap()[:, ri, p0:p0 + 16], scalar=smc(C_R), in1=st[4 + ri], op0=ALU.mult, op1=ALU.add)),
                              [B_hmid, B_stmp, B_sm], [B_hbuf])
                    cmul(tgt.ap()[:, 0, p0:p0 + 16], tgt.ap()[:, 1, p0:p0 + 16], g31r, g31i, cos31, sin31, st[2], st[3], [Bg[0], Bg[1], B_tab], [Btgt], [B_stmp])
            gs_re = gsel_v[:, 0, :, :].rearrange("p i (a m) -> p i a m", m=32)
            gs_im = gsel_v[:, 1, :, :].rearrange("p i (a m) -> p i a m", m=32)
            c_t = bass.AP(tensor=cosT, offset=32 * p0, ap=[[1024, 128], [0, 2], [32, 16], [1, 32]])
            s_t = bass.AP(tensor=sinT, offset=32 * p0, ap=[[1024, 128], [0, 2], [32, 16], [1, 32]])
            tmp4 = TMP.ap().rearrange("p a b -> p (a b)")
            rt = [tmp4[:, 1024 * q:1024 * (q + 1)].rearrange("p (i a m) -> p i a m", i=2, a=16) for q in range(4)]
            Brt = [[B_T8[2 * q], B_T8[2 * q + 1]] for q in range(4)]
            Bgs_re = B_pbuf[0] + B_pbuf[1]; Bgs_im = B_pbuf[2] + B_pbuf[3]
            A("dve", (lambda e, rt=rt, gs_re=gs_re, c_t=c_t: e.tensor_tensor(out=rt[0], in0=gs_re, in1=c_t, op=ALU.mult)), Bgs_re + [B_tab], Brt[0])
            A("dve", (lambda e, rt=rt, gs_im=gs_im, s_t=s_t: e.tensor_tensor(out=rt[1], in0=gs_im, in1=s_t, op=ALU.mult)), Bgs_im + [B_tab], Brt[1])
            A("dve", (lambda e, rt=rt, gs_im=gs_im, c_t=c_t: e.tensor_tensor(out=rt[2], in0=gs_im, in1=c_t, op=ALU.mult)), Bgs_im + [B_tab], Brt[2])
            A("dve", (lambda e, rt=rt, gs_re=gs_re, s_t=s_t: e.tensor_tensor(out=rt[3], in0=gs_re, in1=s_t, op=ALU.mult)), Bgs_re + [B_tab], Brt[3])
            h_re = bass.AP(tensor=hbuf, offset=0 * 32 * 65 + p0 * 65 + 1, ap=[[2 * 32 * 65, 128], [32, 2], [65, 16], [1, 32]])
            h_im = bass.AP(tensor=hbuf, offset=1 * 32 * 65 + p0 * 65 + 1, ap=[[2 * 32 * 65, 128], [32, 2], [65, 16], [1, 32]])
            A("pool", (lambda e, rt=rt, h_re=h_re: e.tensor_tensor(out=h_re, in0=rt[0], in1=rt[1], op=ALU.subtract)), Brt[0] + Brt[1], [B_hbuf])
            A("pool", (lambda e, rt=rt, h_im=h_im: e.tensor_tensor(out=h_im, in0=rt[2], in1=rt[3], op=ALU.add)), Brt[2] + Brt[3], [B_hbuf])

        for h in range(8):
            bk = proj_fm(24 + h, HTO, B_hto)
            A("act", (lambda e, bk=bk, h=h: e.activation(out=GAT[:, h, :], in_=bank(bk), func=AF.Silu)), [PB[bk]], [B_gat[h]])
        for cb in range(8):
            bk = proj_fm(40 + cb, HTO, B_hto)
            A("act", (lambda e, bk=bk, cb=cb: e.activation(out=GST[:, cb, :], in_=bank(bk), func=AF.Silu)), [PB[bk]], [B_gst[cb]])

        for cb in range(8):
            i = cb % 2
            S.dma("sp", wkb[i].ap(), WK_scr[cb], writes=[B_wkb[i]], sem=B_wkb[i])
            S.dma("sp", wcb_v[i], WC_scr[cb], writes=[B_wobuf[i]], sem=B_wobuf[i])
            bk = gb()
            first = True
            for tau in range(8):
                for sg_ in range(tau + 1):
                    A("pe", (lambda e, bk=bk, i=i, tau=tau, sg_=sg_, cb=cb, f=first: e.matmul(
                        PS.ap()[:, 512 * bk + tau:512 * bk + 512:8], lhsT=wkb[i].ap()[:, tau - sg_, :], rhs=UTO[:, cb, sg_:512:8], start=f, stop=False, skip_group_check=True)),
                      [B_wkb[i], B_uto], [PB[bk]])
                    first = False
            for tau in range(8):
                for j in range(4):
                    for ri in range(2):
                        last = (tau == 7 and j == 3 and ri == 1)
                        A("pe", (lambda e, bk=bk, i=i, tau=tau, j=j, ri=ri, cb=cb, last=last: e.matmul(
                            PS.ap()[32 * j:32 * j + 32, 512 * bk + tau:512 * bk + 512:8], lhsT=wcb_v[i][:, j, tau, ri, :],
                            rhs=hbuf.ap()[:, ri, 4 * cb + j, 0:64], start=False, stop=last, tile_position=(0, 32 * j), skip_group_check=True)),
                          [B_wobuf[i], B_hbuf], [PB[bk]])
            A("act", (lambda e, bk=bk, cb=cb: e.activation(out=ZG[:, cb, :], in_=bank(bk), func=AF.Gelu_apprx_tanh)), [PB[bk], B_utch], [B_zg])
        sig, gtmp = T(5), T(6)
        B_sig, B_gtmp = B_T8[5], B_T8[6]
        for nb in range(8):
            w, Bw = wload(w_glu_b[nb])
            bk = gb()
            for c in range(8):
                A("pe", (lambda e, bk=bk, c=c, w=w: e.matmul(bank(bk), lhsT=w[:, c, :], rhs=ZG[:, c, :], start=(c == 0), stop=(c == 7))), [Bw, B_zg], [PB[bk]])
            A("act", (lambda e, bk=bk, nb=nb: e.activation(out=sig, in_=bank(bk), func=AF.Sigmoid, bias=smc(C_BGLU + nb))), [PB[bk], B_sm], [B_sig])
            A("dve", (lambda e, nb=nb: e.tensor_tensor(out=gtmp, in0=sig, in1=ZG[:, nb, :], op=ALU.mult)), [B_sig, B_zg], [B_gtmp])
            A("dve", (lambda e, nb=nb: e.tensor_tensor(out=ZT[:, nb, :], in0=gtmp, in1=GST[:, nb, :], op=ALU.mult)), [B_gtmp, B_gst[nb]], [B_zt])

        nkt = 2 * s + 2
        fr0, fr1, fo0, fo1, frs = T(0), T(1), T(2), T(3), T(4)
        B_fr0, B_fr1, B_fo0, B_fo1, B_frs = B_T8[0], B_T8[1], B_T8[2], B_T8[3], B_T8[4]
        pb8 = pbuf_t.ap().rearrange("p a (b c) -> p (a b) c", b=2)
        units = [(h, kt, kb) for h in range(8) for kt in range(nkt) for kb in range(4)]
        kvslot = {}

        def issue_qk(n):
            h, kt, kb = units[n]
            if kb == 0:
                i = kvi[0]; kvi[0] = (i + 1) % NKV
                kvslot[(h, kt)] = i
                S.dma("sp", kbuf_t.ap()[:, i, :], K_scr[kt, h], reads=[B_K[kt][h]], writes=[B_kbuf[i]], sem=B_kbuf[i])
                S.dma("sp", vbuf_t.ap()[:, i, :], V_scr[kt, h], reads=[B_V[kt][h]], writes=[B_vbuf[i]], sem=B_vbuf[i])
            i = kvslot[(h, kt)]
            kb_ap = kbuf_t.ap()[:, i, :]
            mvar = 0 if kt == 2 * s else (1 if kt == 2 * s + 1 else None)
            st_ = n % 2
            for m in range(2):
                bk = 2 * st_ + m
                A("pe", (lambda e, kb_ap=kb_ap, m=m, kb=kb, bk=bk, h=h, mv=mvar: e.matmul(
                    bank(bk), lhsT=kb_ap[64 * m:64 * m + 64, kb * 128:(kb + 1) * 128], rhs=QT[64 * m:64 * m + 64, h, :],
                    start=True, stop=(mv is None))), [B_kbuf[i], B_qt[h]], [PB[bk]])
                if mvar is not None:
                    A("pe", (lambda e, m=m, kb=kb, bk=bk, mv=mvar: e.matmul(
                        bank(bk), lhsT=a64_b.ap()[64 * m:64 * m + 64, kb * 128:(kb + 1) * 128], rhs=bm_b.ap()[64 * m:64 * m + 64, mv * 512:(mv + 1) * 512],
                        start=False, stop=True)), [B_a64, B_bm], [PB[bk]])
            pp = pbi[0]; pbi[0] = (pp + 1) % 4
            A("act", (lambda e, pp=pp, st_=st_: e.activation(out=pbuf_t.ap()[:, pp, :], in_=PS.ap()[:, 1024 * st_:1024 * st_ + 1024], func=AF.Exp)),
              [PB[2 * st_], PB[2 * st_ + 1]], [B_pb8[2 * pp], B_pb8[2 * pp + 1]])
            slots = [2 * pp, 2 * pp + 1]
            return slots

        def issue_pv(n, slots):
            h, kt, kb = units[n]
            i = kvslot[(h, kt)]
            vb_ap = vbuf_t.ap()[:, i, :]
            st_ = (kt == 0 and kb == 0)
            sp_ = (kt == nkt - 1 and kb == 3)
            for m in range(2):
                pi_ = slots[m]
                A("pe", (lambda e, vb_ap=vb_ap, m=m, kb=kb, pi_=pi_, st_=st_, sp_=sp_: e.matmul(
                    bank(4 + m), lhsT=vb_ap[:, kb * 128:(kb + 1) * 128], rhs=pb8[:, pi_, :], start=st_, stop=sp_)),
                  [B_vbuf[i], B_pb8[pi_]], [PB[4 + m]])
                uidx = 4 * kt + kb
                if uidx % 4 == 1:
                    A("pe", (lambda e, m=m, pi_=pi_, f=(uidx == 1): e.matmul(bank(6 + m), lhsT=ones_b.ap(), rhs=pb8[:, pi_, :], start=f, stop=False)),
                      [B_ones, B_pb8[pi_]], [PB[6 + m]])
                else:
                    acc = sacc[h % 2][m].ap(); Bacc = B_sacc[h % 2][m]
                    if st_:
                        A("dve", (lambda e, acc=acc, pi_=pi_: e.tensor_copy(out=acc, in_=pb8[:, pi_, :])), [B_pb8[pi_]], [Bacc])
                    else:
                        A("dve", (lambda e, acc=acc, pi_=pi_: e.tensor_tensor(out=acc, in0=acc, in1=pb8[:, pi_, :], op=ALU.add)), [B_pb8[pi_]], [Bacc])
            if sp_:
                for m in range(2):
                    A("pe", (lambda e, m=m, hh=h % 2: e.matmul(bank(6 + m), lhsT=ones_f.ap(), rhs=sacc[hh][m].ap(), start=False, stop=True)), [B_onesf, B_sacc[h % 2][m]], [PB[6 + m]])
                A("dve", lambda e: e.reciprocal(out=fr0, in_=bank(6)), [PB[6]], [B_fr0])
                A("dve", lambda e: e.tensor_tensor(out=fo0, in0=bank(4), in1=fr0, op=ALU.mult), [PB[4], B_fr0], [B_fo0])
                A("dve", lambda e: e.reciprocal(out=fr1, in_=bank(7)), [PB[7]], [B_fr1])
                A("dve", lambda e: e.tensor_tensor(out=fo1, in0=bank(5), in1=fr1, op=ALU.mult), [PB[5], B_fr1], [B_fo1])
                A("dve", lambda e: e.scalar_tensor_tensor(out=fo0, in0=fo1, scalar=dc(D_NEGLAM), in1=fo0, op0=ALU.mult, op1=ALU.add), [B_fo1, B_cols], [B_fo0])
                A("act", lambda e: e.activation(out=qsq.ap(), in_=fo0, func=AF.Square), [B_fo0], [B_qsq])
                A("pe", lambda e: e.matmul(bank(6), lhsT=ones_b.ap(), rhs=qsq.ap(), start=True, stop=True), [B_ones, B_qsq], [PB[6]])
                A("act", lambda e: e.activation(out=frs, in_=bank(6), func=AF.Ln, scale=1.0 / 128.0, bias=EPS), [PB[6]], [B_frs])
                A("act", lambda e: e.activation(out=frs, in_=frs, func=AF.Exp, scale=-0.5), [], [B_frs])
                A("dve", lambda e: e.tensor_tensor(out=fo1, in0=fo0, in1=frs, op=ALU.mult), [B_fo0, B_frs], [B_fo1])
                A("dve", (lambda e, h=h: e.scalar_tensor_tensor(out=OT[:, h, :], in0=fo1, scalar=dc(D_GH8), in1=GAT[:, h, :], op0=ALU.mult, op1=ALU.mult)), [B_fo1, B_cols, B_gat[h]], [B_ot])

        pend = issue_qk(0)
        for n in range(len(units)):
            nxt = issue_qk(n + 1) if n + 1 < len(units) else None
            issue_pv(n, pend)
            pend = nxt

        for hf in range(2):
            S.dma("pool", xr_v, xo[s * 512:(s + 1) * 512, hf * 512:(hf + 1) * 512].rearrange("(sub p) c -> p sub c", p=128), writes=B_R[0:4], sem=B_R[0])
            obk = [gb() for _ in range(4)]
            for cq in range(4):
                blk = hf * 4 + cq
                i = blk % 2
                S.dma("sp", wobuf[i].ap().rearrange("p c j -> p (c j)"), w_out_b[blk], writes=[B_wobuf[i]], sem=B_wobuf[i])
                for sub in range(4):
                    for k in range(16):
                        src = OT[:, k, sub * 128:(sub + 1) * 128] if k < 8 else ZT[:, k - 8, sub * 128:(sub + 1) * 128]
                        Bsrc = B_ot if k < 8 else B_zt
                        A("pe", (lambda e, i=i, k=k, sub=sub, cq=cq, src=src, obk=obk: e.matmul(
                            bank(obk[sub], cq * 128, (cq + 1) * 128), lhsT=src, rhs=wobuf[i].ap()[:, k, :], start=(k == 0), stop=(k == 15))),
                          [B_wobuf[i], Bsrc], [PB[obk[sub]]])
            for sub in range(4):
                oi = oti[0]; oti[0] = (oi + 1) % 2
                A("dve", (lambda e, oi=oi, sub=sub, hf=hf, obk=obk: e.tensor_tensor(out=otile_v[oi], in0=bank(obk[sub]), in1=gate_row.ap()[:, hf * 512:(hf + 1) * 512], op=ALU.mult)),
                  [PB[obk[sub]], B_grow], [B_R[4 + oi]])
                A("dve", (lambda e, oi=oi, sub=sub: e.tensor_tensor(out=otile_v[oi], in0=otile_v[oi], in1=xr_v[:, sub, :], op=ALU.add)), B_R[0:4], [B_R[4 + oi]])
                o = S.dma("pool", out[s * 512 + sub * 128:s * 512 + (sub + 1) * 128, hf * 512:(hf + 1) * 512], otile_v[oi], reads=[B_R[4 + oi]], writes=[], sem=B_R[4 + oi])
                final_ops.append(o)

    if debug:
        for name, t, bufs in (("d_cols", cols, [B_cols]), ("d_hbuf", hbuf, [B_hbuf]),
                              ("d_arena", ARENA, [B_utcl, B_utch, B_htc, B_hto, B_uto] + B_qt + B_gat + B_gst),
                              ("d_cosT", cosT, [B_tab]), ("d_sinT", sinT, [B_tab]), ("d_rhoT", rhoT, [B_tab]), ("d_scon", scon, [B_scon]),
                              ("d_grow", gate_row, [B_grow])):
            bfl = S.buf(name, True)
            d = dbg_out(name, list(t.shape), t.dtype)
            final_ops.append(S.dma("sp", d, t.ap(), reads=bufs, writes=[], sem=bfl))

    with nc.Block() as block:
        S.emit(block, final_ops=final_ops)
    return nc, S


def _ssm_layouts(a_re, a_im, log_dt, b_re, b_im, c_re, c_im, d):
    f = np.float32
    AR_S = np.zeros((3, 128, 8, 2, 64), f)
    BT = np.zeros((2, 128, 8, 2, 64), f)
    for cb in range(8):
        for j in range(4):
            for g2 in range(2):
                g = 8 * cb + 2 * j + g2
                AR_S[0, 32 * j:32 * j + 32, cb, g2, :] = a_re[g][None, :]
                AR_S[1, 32 * j:32 * j + 32, cb, g2, :] = a_im[g][None, :]
                AR_S[2, 32 * j:32 * j + 32, cb, g2, :] = log_dt[g]
                BT[0, 32 * j + 16 * g2:32 * j + 16 * g2 + 16, cb, g2, :] = b_re[g].T
                BT[1, 32 * j + 16 * g2:32 * j + 16 * g2 + 16, cb, g2, :] = b_im[g].T
    ssmS = np.concatenate([AR_S.reshape(3, 128, 1024), BT.reshape(2, 128, 1024)], 0).transpose(1, 0, 2)
    K = np.zeros((4, 128, 32, 2, 16), f)
    T = np.zeros((4, 128, 32, 32), f)
    for pr in range(32):
        for g2 in range(2):
            g = 2 * pr + g2
            K[0, 64 * g2:64 * g2 + 64, pr, g2, :] = b_re[g]
            K[1, 64 * g2:64 * g2 + 64, pr, g2, :] = b_im[g]
            K[2, 64 * g2:64 * g2 + 64, pr, g2, :] = c_re[g].T
            K[3, 64 * g2:64 * g2 + 64, pr, g2, :] = c_im[g].T
            T[0, 64 * g2:64 * g2 + 64, pr, :] = a_re[g][:, None]
            T[1, 64 * g2:64 * g2 + 64, pr, :] = a_im[g][:, None]
            T[2, 64 * g2:64 * g2 + 64, pr, :] = log_dt[g]
    T[3] = np.arange(32, dtype=f)[None, None, :]
    ssmK = K.reshape(4, 128, 1024).transpose(1, 0, 2)
    ssmT = T.reshape(4, 128, 1024).transpose(1, 0, 2)
    dd = np.zeros((128, 8, 128), f)
    for cb in range(8):
        dd[np.arange(128), cb, np.arange(128)] = d[128 * cb:128 * (cb + 1)]
    return np.ascontiguousarray(ssmS), np.ascontiguousarray(ssmK), np.ascontiguousarray(ssmT), dd.reshape(128, 1024)


def _consts():
    f = np.float32
    ident = np.eye(128, dtype=f)
    perm = np.zeros((128, 128), f)
    for m in range(128):
        d = m % 64
        if d < 32:
            perm[m + 32, m] = -1.0
        else:
            perm[m - 32, m] = 1.0
    a64 = np.zeros((128, 512), f)
    for j in range(8):
        a64[j, 64 * j:64 * (j + 1)] = 1.0
        a64[64 + j, 64 * j:64 * (j + 1)] = 1.0
    diag = np.zeros((128, 512), f)
    none = np.zeros((128, 512), f)
    for j in range(8):
        row = np.where((np.arange(512) // 64) >= j, 0.0, NEG).astype(f)
        diag[j] = row
        diag[64 + j] = row
        none[j] = NEG
        none[64 + j] = NEG
    full = np.zeros((128, 512), f)
    pos0 = np.broadcast_to(np.arange(512, dtype=f)[None, :], (128, 512)).copy()
    invf = (1.0 / (np.float32(10000.0) ** (np.arange(0, 64, 2, dtype=f) / np.float32(64)))).astype(f)
    invf_col = np.tile(np.concatenate([invf, invf]), 2).astype(f)
    sel = np.zeros((128, 256), f)
    for m in range(2):
        sel[32 * m, 128 * m:128 * (m + 1)] = 1.0
        sel[64 + 32 * m, 128 * m:128 * (m + 1)] = 1.0
    return ident, perm, a64, (diag, none, full), pos0, invf_col, sel


def make_in_maps(inputs, NSTEP, nbatch):
    f = np.float32
    g = lambda k: np.asarray(inputs[k], dtype=f)
    x = g("x"); c = g("c")
    ident, perm, a64, (diag, none, full), pos0, invf_col, sel = _consts()
    ssmS, ssmK, ssmT, dd = _ssm_layouts(g("ssm_a_re")[0], g("ssm_a_im")[0], g("ssm_log_dt")[0], g("ssm_b_re")[0], g("ssm_b_im")[0],
                                        g("ssm_c_re")[0], g("ssm_c_im")[0], g("ssm_d")[0])
    lamv = np.broadcast_to(np.concatenate([g("lam_q1")[0], g("lam_k1")[0], g("lam_q2")[0], g("lam_k2")[0]])[None, :], (128, 256)).copy()
    b_ada = g("b_ada")[0]
    shared = {
        "w_ada": np.ascontiguousarray(g("w_ada")[0]), "w_in": np.ascontiguousarray(g("w_in")[0]),
        "w_glu": np.ascontiguousarray(g("w_glu")[0]), "w_out": np.ascontiguousarray(g("w_out")[0]),
        "lamv": lamv, "b_gate_row": np.broadcast_to(b_ada[None, 2048:3072], (128, 1024)).copy(),
        "ident": ident, "perm": perm, "sel": sel, "a64": a64, "pos0": pos0,
        "ssmS": ssmS, "ssmK": ssmK, "ssmT": ssmT, "ddiag": dd,
    }
    NT = 2 * NSTEP
    maps = []
    for core in range(2 * nbatch):
        b, r = core // 2, core % 2
        sm = np.zeros((128, 64), f)
        sm[:, 0:8] = c[b].reshape(8, 128).T
        sm[:, 8:32] = b_ada.reshape(24, 128).T
        sm[:, 32:40] = g("norm_g")[0].reshape(8, 128).T
        sm[:, 40] = np.tile(g("q_norm_g")[0], 2)
        sm[:, 41] = np.tile(g("k_norm_g")[0], 2)
        sm[:, 42] = g("head_norm_g")[0]
        sm[:, 43:51] = g("b_glu")[0].reshape(8, 128).T
        sm[:, 51] = invf_col
        sm[:, 52] = float(r)
        sm[:, 53] = float(1 - r)
        bo = np.zeros((128, 16), f)
        for s in range(NSTEP):
            bo[:, s] = 1024.0 * s + 512.0 * r
        bm = np.concatenate([diag, none] if r == 0 else [full, diag], axis=1)
        xb = np.ascontiguousarray(x[b][:NT * 512])
        m = dict(shared)
        m.update({"xc": xb, "xo": np.ascontiguousarray(xb.reshape(NT, 512, 1024)[r::2].reshape(-1, 1024)),
                  "smalls": sm, "base_own": bo, "bm": np.ascontiguousarray(bm)})
        maps.append(m)
    return maps


_NC_CACHE = {}


def run(inputs, NSTEP, nbatch, debug=False):
    key = (NSTEP, debug)
    if key not in _NC_CACHE:
        _NC_CACHE[key] = build_nc(NSTEP, debug)
    nc, S = _NC_CACHE[key]
    maps = make_in_maps(inputs, NSTEP, nbatch)
    res = run_bass_kernel_spmd(nc, maps, core_ids=list(range(2 * nbatch)))
    NT = 2 * NSTEP
    outp = np.zeros((nbatch, NT, 512, 1024), np.float32)
    for core in range(2 * nbatch):
        b, r = core // 2, core % 2
        outp[b, r::2] = np.asarray(res.results[core]["out"]).reshape(NSTEP, 512, 1024)
    return outp.reshape(nbatch, NT * 512, 1024), res


def kernel(**inputs):
    out, _ = run(inputs, 8, 4)
    return out
```

```python
import numpy as np
import concourse.bass as bass
import concourse.mybir as mybir
from concourse.bass_utils import run_bass_kernel_spmd

F32 = mybir.dt.float32
BF16 = mybir.dt.bfloat16
I32 = mybir.dt.int32
AF = mybir.ActivationFunctionType
ALU = mybir.AluOpType
AX = mybir.AxisListType
PI = float(np.pi)
TWO_PI = float(2 * np.pi)
EPS = 1e-6
NEG = -30000.0
LAM_INIT = 0.2


class Buf:
    __slots__ = ("name", "sem", "dcount", "last_w", "readers")

    def __init__(self, name, sem=None):
        self.name = name
        self.sem = sem
        self.dcount = 0
        self.last_w = None
        self.readers = []


class Op:
    __slots__ = ("eng", "fn", "deps", "is_dma", "sem", "val", "need_inc", "idx")


class Sched:
    ENG = ("pe", "act", "dve", "pool", "sp")

    def __init__(self, nc):
        self.nc = nc
        self.ops = {e: [] for e in self.ENG}
        self.esem = {e: nc.alloc_semaphore("es_" + e) for e in self.ENG}
        self.nbuf = 0
        self.pending_dma = []

    def buf(self, name, dma=False):
        self.nbuf += 1
        return Buf(name, self.nc.alloc_semaphore("ds%d" % self.nbuf) if dma else None)

    def _mk(self, eng, fn, reads, writes, is_dma=False):
        op = Op()
        op.eng = eng
        op.fn = fn
        op.is_dma = is_dma
        op.sem = None
        op.val = None
        op.need_inc = False
        cand = []
        for b in reads:
            if b.last_w is not None:
                cand.append(b.last_w)
        for b in writes:
            if b.last_w is not None:
                cand.append(b.last_w)
            cand.extend(b.readers)
        best = {}
        for d in cand:
            k = ("d", id(d.sem)) if d.is_dma else ("e", d.eng)
            cur = best.get(k)
            if cur is None or (d.val > cur.val if d.is_dma else d.idx > cur.idx):
                best[k] = d
        op.deps = list(best.values())
        for b in reads:
            if not is_dma:
                b.readers = [r for r in b.readers if (r.is_dma or r.eng != eng)]
            b.readers.append(op)
        for b in writes:
            b.last_w = op
            b.readers = []
        op.idx = len(self.ops[eng])
        self.ops[eng].append(op)
        return op

    def op(self, eng, fn, reads=(), writes=()):
        return self._mk(eng, fn, reads, writes)

    def dma(self, eng, out, in_, reads=(), writes=(), sem=None):
        assert sem is not None and sem.sem is not None
        sem.dcount += 1
        op = self._mk(eng, lambda e: e.dma_start(out=out, in_=in_), reads, writes, is_dma=True)
        op.sem = sem.sem
        op.val = 16 * sem.dcount
        self.pending_dma.append(op)
        return op

    def barrier(self, dummies):
        arr = []
        for e in ("act", "dve", "pool"):
            b = Buf("bar_" + e)
            op = self._mk(e, dummies[e], [], [b])
            op.deps = list(op.deps) + list(self.pending_dma)
            arr.append(b)
        self.pending_dma = []
        for e in self.ENG:
            self._mk(e, None, arr, [])

    def emit(self, block, final_ops=()):
        for e in self.ENG:
            for op in self.ops[e]:
                for d in op.deps:
                    if not d.is_dma:
                        if d.eng == "pe" and op.eng == "pe" and not op.is_dma:
                            continue
                        d.need_inc = True
        for op in final_ops:
            if not op.is_dma:
                op.need_inc = True
        self.stats = {}
        for e in self.ENG:
            c = 0
            for op in self.ops[e]:
                if op.is_dma:
                    continue
                if op.need_inc:
                    assert op.fn is not None
                    c += 1
                    op.sem = self.esem[e]
                    op.val = c
            self.stats[e] = (len(self.ops[e]), c)

        def run(e, eng):
            waited = {}
            for op in self.ops[e]:
                need = {}
                for d in op.deps:
                    if (not d.is_dma) and d.eng == "pe" and e == "pe" and not op.is_dma:
                        continue
                    k = id(d.sem)
                    if k not in need or need[k][1] < d.val:
                        need[k] = (d.sem, d.val)
                for k, (s, v) in need.items():
                    if waited.get(k, 0) < v:
                        eng.wait_ge(s, v)
                        waited[k] = v
                if op.fn is None:
                    continue
                ins = op.fn(eng)
                if op.is_dma:
                    ins.then_inc(op.sem, 16)
                elif op.need_inc:
                    ins.then_inc(op.sem, 1)
            if e == "sp":
                for op in final_ops:
                    eng.wait_ge(op.sem, op.val)

        block.tensor(lambda eng: run("pe", eng))
        block.scalar(lambda eng: run("act", eng))
        block.vector(lambda eng: run("dve", eng))
        block.gpsimd(lambda eng: run("pool", eng))
        block.sync(lambda eng: run("sp", eng))


def build_nc(NSTEP, debug=False):
    NT = 2 * NSTEP
    SLEN = NT * 512
    nc = bass.Bass("TRN2", target_bir_lowering=False)
    S = Sched(nc)
    dbg_kind = "ExternalOutput" if debug else "Internal"

    def din(name, shape, dt=F32):
        return nc.dram_tensor(name, list(shape), dt, kind="ExternalInput").ap()

    xc = din("xc", [SLEN, 1024])
    xo = din("xo", [NSTEP * 512, 1024])
    out = nc.dram_tensor("out", [NSTEP * 512, 1024], F32, kind="ExternalOutput").ap()
    w_ada = din("w_ada", [1024, 3072])
    w_in = din("w_in", [1024, 6144])
    w_glu = din("w_glu", [1024, 1024])
    w_out = din("w_out", [2048, 1024])
    smalls = din("smalls", [128, 64])
    lamv = din("lamv", [128, 256])
    b_gate_row = din("b_gate_row", [128, 1024])
    base_own = din("base_own", [128, 16])
    ident_in = din("ident", [128, 128])
    perm_in = din("perm", [128, 128])
    a64_in = din("a64", [128, 512])
    bm_in = din("bm", [128, 1024])
    pos0_in = din("pos0", [128, 512])
    sel_in = din("sel", [128, 256])
    ssmS = din("ssmS", [128, 5, 1024])
    ssmK = din("ssmK", [128, 4, 1024])
    ssmT = din("ssmT", [128, 4, 1024])
    ddiag = din("ddiag", [128, 1024])
    w_in_b = nc.dram_tensor("w_in_b", [48, 128, 1024], BF16).ap()
    w_glu_b = nc.dram_tensor("w_glu_b", [8, 128, 1024], BF16).ap()
    w_out_b = nc.dram_tensor("w_out_b", [8, 128, 2048], BF16).ap()
    K_scr = nc.dram_tensor("K_scr", [NT, 8, 128, 512], BF16, kind=dbg_kind).ap()
    V_scr = nc.dram_tensor("V_scr", [NT, 8, 128, 512], BF16, kind=dbg_kind).ap()
    WS_scr = nc.dram_tensor("WS_scr", [8, 128, 8, 2, 128], BF16, kind=dbg_kind).ap()
    WC_scr = nc.dram_tensor("WC_scr", [8, 128, 4, 8, 2, 32], BF16, kind=dbg_kind).ap()
    WK_scr = nc.dram_tensor("WK_scr", [8, 128, 8, 128], BF16, kind=dbg_kind).ap()
    dbg_outs = {}

    def dbg_out(name, shape, dt=F32):
        dbg_outs[name] = nc.dram_tensor(name, list(shape), dt, kind="ExternalOutput").ap()
        return dbg_outs[name]

    def sb(name, shape, dt=F32):
        return nc.alloc_sbuf_tensor("sb_" + name, list(shape), dt)

    PS = nc.alloc_psum_tensor("ps", [128, 4096], F32)
    PB = [S.buf("pb%d" % i) for i in range(8)]

    def bank(i, lo=0, hi=512):
        return PS.ap()[:, 512 * i + lo:512 * i + hi]

    gbc = [0]

    def gb():
        gbc[0] = (gbc[0] + 1) % 8
        return gbc[0]

    ident = sb("ident", [128, 128]); B_ident = S.buf("ident", True)
    perm_f = sb("perm_f", [128, 128]); B_permf = S.buf("permf", True)
    perm_b = sb("perm_b", [128, 128], BF16); B_perm = S.buf("perm")
    ones_b = sb("ones_b", [128, 128], BF16); B_ones = S.buf("ones")
    ones_f = sb("ones_f", [128, 128]); B_onesf = S.buf("onesf")
    sacc = [[sb("sacc%d%d" % (a_, m_), [128, 512]) for m_ in range(2)] for a_ in range(2)]
    B_sacc = [[S.buf("sacc%d%d" % (a_, m_)) for m_ in range(2)] for a_ in range(2)]
    saccp = [sb("saccp%d" % m_, [128, 512]) for m_ in range(2)]
    B_saccp = [S.buf("saccp%d" % m_) for m_ in range(2)]
    blk1_b = sb("blk1_b", [128, 128], BF16); B_blk1 = S.buf("blk1")
    B_a64f = S.buf("a64f", True)
    a64_b = sb("a64_b", [128, 512], BF16); B_a64 = S.buf("a64")
    B_bmf = S.buf("bmf", True)
    bm_b = sb("bm_b", [128, 1024], BF16); B_bm = S.buf("bm")
    pos0 = sb("pos0", [128, 512]); B_pos0 = S.buf("pos0", True)
    sel_f = sb("sel_f", [128, 256]); B_sel = S.buf("sel", True)
    sm = sb("smalls", [128, 64]); B_sm = S.buf("sm", True)
    bown = sb("bown", [128, 16]); B_bown = S.buf("bown", True)
    gate_row = sb("gate_row", [128, 1024]); B_grow = S.buf("grow", True)
    cols = sb("cols", [128, 64]); B_cols = S.buf("cols")
    dmy_a = sb("dmy_a", [128, 1]); dmy_d = sb("dmy_d", [128, 1]); dmy_p = sb("dmy_p", [128, 1])
    C_CT = 0
    C_BADA = 8
    C_NG = 32
    C_GQ = 40; C_GK = 41; C_GH = 42
    C_BGLU = 43
    C_INVF = 51; C_R = 52; C_OMR = 53
    D_A1 = 0
    D_SHIFT = 8
    D_GQ8 = 16; D_NEGLAM = 17; D_GH8 = 18
    D_SC = 20
    D_MOD = 28

    def smc(i, n=1):
        return sm.ap()[:, i:i + n]

    def dc(i, n=1):
        return cols.ap()[:, i:i + n]

    ARENA = sb("arena", [128, 32768], BF16)
    B_arena_init = S.buf("arena_init")
    arena_f = ARENA.ap().bitcast(F32)

    def scr(i, n=512):
        return arena_f[:, 512 * i:512 * i + n]

    def aview(off, n):
        return ARENA.ap()[:, off:off + n]

    HTC = aview(0, 4096).rearrange("p (c t) -> p c t", c=8); B_htc = S.buf("htc")
    HTO = aview(4096, 4096).rearrange("p (c t) -> p c t", c=8); B_hto = S.buf("hto")
    UTC = aview(8192, 8192).rearrange("p (c t) -> p c t", c=8); B_utcl = S.buf("utcl"); B_utch = S.buf("utch")
    B_utc = [B_utcl, B_utch]
    QT = aview(16384, 4096).rearrange("p (c t) -> p c t", c=8); B_qt = [S.buf("qt%d" % h) for h in range(8)]
    GAT = aview(20480, 4096).rearrange("p (c t) -> p c t", c=8); B_gat = [S.buf("gat%d" % h) for h in range(8)]
    UTO = aview(24576, 4096).rearrange("p (c t) -> p c t", c=8); B_uto = S.buf("uto")
    GST = aview(28672, 4096).rearrange("p (c t) -> p c t", c=8); B_gst = [S.buf("gst%d" % h) for h in range(8)]
    OT = HTC
    B_ot = B_htc
    ZG = UTC[:, :, 0:512]
    ZT = UTC[:, :, 512:1024]
    B_zg = B_utcl
    B_zt = B_utch

    R16 = sb("r16", [128, 4096]); B_R = [S.buf("r16_%d" % i, True) for i in range(8)]
    xt_v = R16.ap().rearrange("p (s c) -> p s c", s=4)
    gin_v = R16.ap().rearrange("p (r g c) -> p r g c", r=2, g=4)
    B_gin = [[B_R[4 * ri + sg] for sg in range(4)] for ri in range(2)]
    xr_v = R16.ap()[:, 0:2048].rearrange("p (s c) -> p s c", s=4)
    otile_v = [R16.ap()[:, 2048 + 512 * k:2048 + 512 * (k + 1)] for k in range(2)]
    TMP = sb("tmp", [128, 8, 512]); B_T8 = [S.buf("tmp%d" % i) for i in range(8)]

    def T(i):
        return TMP.ap()[:, i, :]

    sqjunk = sb("sqjunk", [128, 1024], BF16); B_junk = S.buf("junk")
    ss = sb("ss", [128, 8]); B_ss = S.buf("ss")
    ropeC = sb("ropeC", [128, 512]); ropeS = sb("ropeS", [128, 512]); B_rope = S.buf("rope")
    ropeCo = sb("ropeCo", [128, 512]); ropeSo = sb("ropeSo", [128, 512]); B_ropeo = S.buf("ropeo")
    NWB = 4
    wbuf = [sb("wbuf%d" % i, [128, 8, 128], BF16) for i in range(NWB)]
    B_wbuf = [S.buf("wbuf%d" % i, True) for i in range(NWB)]
    wbi = [0]
    wobuf = [sb("wobuf%d" % i, [128, 16, 128], BF16) for i in range(2)]
    B_wobuf = [S.buf("wobuf%d" % i, True) for i in range(2)]
    qsq2 = [sb("qsq%d" % i, [128, 512], BF16) for i in range(3)]; qknb2 = [sb("qknb%d" % i, [128, 512], BF16) for i in range(3)]
    B_qsq2 = [S.buf("qsq%d" % i) for i in range(3)]; B_qknb2 = [S.buf("qknb%d" % i) for i in range(3)]
    qx = sb("qx", [128, 512]); B_qx = S.buf("qx")
    qsq = qsq2[0]; B_qsq = B_qsq2[0]
    NKS = 4
    kst = [sb("kst%d" % i, [128, 512], BF16) for i in range(NKS)]
    B_kst = [S.buf("kst%d" % i, True) for i in range(NKS)]
    vst = [sb("vst%d" % i, [128, 512], BF16) for i in range(NKS)]
    B_vst = [S.buf("vst%d" % i, True) for i in range(NKS)]
    ksi = [0]; vsi = [0]
    NKV = 4
    kbuf_t = sb("kbuf", [128, NKV, 512], BF16); vbuf_t = sb("vbuf", [128, NKV, 512], BF16)
    B_kbuf = [S.buf("kbuf%d" % i, True) for i in range(NKV)]
    B_vbuf = [S.buf("vbuf%d" % i, True) for i in range(NKV)]
    kvi = [0]
    NPB = 4
    pbuf_t = sb("pbuf", [128, NPB, 1024], BF16)
    B_pb8 = [S.buf("pb8_%d" % i) for i in range(8)]
    B_pbuf = [[B_pb8[2 * i], B_pb8[2 * i + 1]] for i in range(NPB)]
    pbi = [0]
    gsel_v = pbuf_t.ap().rearrange("p a b -> p (a b)").bitcast(F32).rearrange("p (r i c) -> p r i c", r=2, i=2)
    cosT = sb("cosT", [128, 32, 32]); sinT = sb("sinT", [128, 32, 32]); rhoT = sb("rhoT", [128, 32, 32])
    B_tab = S.buf("tab")
    scon = sb("scon", [128, 8, 32]); B_scon = S.buf("scon")
    hend = sb("hend", [128, 2, 32]); B_hend = S.buf("hend")
    hmid = sb("hmid", [128, 2, 32]); B_hmid = S.buf("hmid")
    stmp = sb("stmp", [128, 8, 16]); B_stmp = S.buf("stmp")
    hbuf = sb("hbuf", [128, 2, 32, 65], BF16); B_hbuf = S.buf("hbuf")
    wsb_v = [kbuf_t.ap().rearrange("p a b -> p (a b)").rearrange("p (s r c) -> p s r c", s=8, r=2),
             vbuf_t.ap().rearrange("p a b -> p (a b)").rearrange("p (s r c) -> p s r c", s=8, r=2)]
    BL_wsb = [B_kbuf, B_vbuf]
    wcb_v = [wobuf[i].ap().rearrange("p a b -> p (a b)").rearrange("p (j t r c) -> p j t r c", j=4, t=8, r=2) for i in range(2)]
    wkb = [sb("wkb%d" % i, [128, 8, 128], BF16) for i in range(2)]; B_wkb = [S.buf("wkb%d" % i, True) for i in range(2)]
    oti = [0]

    def A(e, fn, r=(), w=()):
        return S.op(e, fn, r, w)

    def sincos(x_ap, sin_out, cos_out, tmpk, tmpi, tmpm, rb, wb, tb, add16=False):
        rb = list(rb); wb = list(wb); tb = list(tb)
        if add16:
            A("dve", lambda e: e.tensor_scalar(out=x_ap, in0=x_ap, scalar1=float(16 * np.pi), scalar2=None, op0=ALU.add), rb, tb)
        A("dve", lambda e: e.tensor_scalar(out=tmpk, in0=x_ap, scalar1=float(1.0 / TWO_PI), scalar2=None, op0=ALU.mult), rb, tb)
        A("dve", lambda e: e.tensor_copy(out=tmpi, in_=tmpk), [], tb)
        A("dve", lambda e: e.tensor_copy(out=tmpk, in_=tmpi), [], tb)
        A("dve", lambda e: e.scalar_tensor_tensor(out=x_ap, in0=tmpk, scalar=-TWO_PI, in1=x_ap, op0=ALU.mult, op1=ALU.add), [], tb)
        A("dve", lambda e: e.tensor_scalar(out=tmpm, in0=x_ap, scalar1=PI, scalar2=-TWO_PI, op0=ALU.is_gt, op1=ALU.mult), [], tb)
        A("dve", lambda e: e.tensor_tensor(out=tmpk, in0=x_ap, in1=tmpm, op=ALU.add), [], tb)
        A("act", lambda e: e.activation(out=sin_out, in_=tmpk, func=AF.Sin), tb, wb)
        A("dve", lambda e: e.tensor_scalar(out=tmpk, in0=x_ap, scalar1=float(PI / 2), scalar2=None, op0=ALU.add), [], tb)
        A("dve", lambda e: e.tensor_scalar(out=tmpm, in0=tmpk, scalar1=PI, scalar2=-TWO_PI, op0=ALU.is_gt, op1=ALU.mult), [], tb)
        A("dve", lambda e: e.tensor_tensor(out=tmpk, in0=tmpk, in1=tmpm, op=ALU.add), [], tb)
        A("act", lambda e: e.activation(out=cos_out, in_=tmpk, func=AF.Sin), tb, wb)

    def cmul(o_re, o_im, a_re, a_im, b_re, b_im, t1, t2, rb, wb, tb, eng="dve"):
        rb = list(rb); wb = list(wb); tb = list(tb)
        A(eng, lambda e: e.tensor_tensor(out=t1, in0=a_re, in1=b_re, op=ALU.mult), rb, tb)
        A(eng, lambda e: e.tensor_tensor(out=t2, in0=a_im, in1=b_im, op=ALU.mult), rb, tb)
        A(eng, lambda e: e.tensor_tensor(out=o_re, in0=t1, in1=t2, op=ALU.subtract), tb, wb)
        A(eng, lambda e: e.tensor_tensor(out=t1, in0=a_re, in1=b_im, op=ALU.mult), rb, tb)
        A(eng, lambda e: e.tensor_tensor(out=t2, in0=a_im, in1=b_re, op=ALU.mult), rb, tb)
        A(eng, lambda e: e.tensor_tensor(out=o_im, in0=t1, in1=t2, op=ALU.add), tb, wb)

    bar_dummies = {"act": lambda e: e.activation(out=dmy_a.ap(), in_=ident.ap()[:, 0:1], func=AF.Copy),
                   "dve": lambda e: e.memset(dmy_d.ap(), 0.0), "pool": lambda e: e.memset(dmy_p.ap(), 0.0)}
    Bz = B_arena_init

    r16f = R16.ap()
    tmpf = TMP.ap().rearrange("p a b -> p (a b)")
    pbf = pbuf_t.ap().rearrange("p a b -> p (a b)")
    cin = [r16f[:, 0:2048], r16f[:, 2048:4096], tmpf[:, 0:2048]]
    tmpb = tmpf[:, 2048:4096].bitcast(BF16)
    cout = [tmpb[:, 0:2048], tmpb[:, 2048:4096], pbf[:, 0:2048], pbf[:, 2048:4096]]
    B_cin = [S.buf("cin%d" % i, True) for i in range(3)]
    B_cout = [S.buf("cout%d" % i, True) for i in range(4)]
    jobs = []
    w_in_v = w_in.rearrange("(c p) j -> p c j", p=128)
    w_glu_v = w_glu.rearrange("(c p) j -> p c j", p=128)
    w_out_v = w_out.rearrange("(c p) j -> p c j", p=128)
    for jb in range(48):
        jobs.append((w_in_v[:, :, jb * 128:(jb + 1) * 128], w_in_b[jb], 8))
    for jb in range(8):
        jobs.append((w_glu_v[:, :, jb * 128:(jb + 1) * 128], w_glu_b[jb], 8))
    for jb in range(8):
        jobs.append((w_out_v[:, :, jb * 128:(jb + 1) * 128], w_out_b[jb], 16))
    precast_ops = []
    for n, (src, dst, nch) in enumerate(jobs):
        i = n % 3
        o = n % 4
        ne = nch * 128
        S.dma("sp", cin[i][:, 0:ne].rearrange("p (c j) -> p c j", c=nch), src, writes=[B_cin[i]], sem=B_cin[i])
        A("pool", (lambda e, i=i, o=o, ne=ne: e.tensor_copy(out=cout[o][:, 0:ne], in_=cin[i][:, 0:ne])), [B_cin[i]], [B_cout[o]])
        S.dma("pool", dst[:, 0:ne], cout[o][:, 0:ne], reads=[B_cout[o]], writes=[], sem=B_cout[o])

    precast_dmas = list(S.pending_dma)
    S.pending_dma = []

    a64_f = scr(20); bm_f = scr(22, 1024)
    S.dma("act", ident.ap(), ident_in, writes=[B_ident], sem=B_ident)
    S.dma("act", perm_f.ap(), perm_in, writes=[B_permf], sem=B_permf)
    S.dma("act", a64_f, a64_in, writes=[B_a64f], sem=B_a64f)
    S.dma("act", bm_f, bm_in, writes=[B_bmf], sem=B_bmf)
    S.dma("act", pos0.ap(), pos0_in, writes=[B_pos0], sem=B_pos0)
    S.dma("act", sel_f.ap(), sel_in, writes=[B_sel], sem=B_sel)
    S.dma("act", sm.ap(), smalls, writes=[B_sm], sem=B_sm)
    S.dma("act", bown.ap(), base_own, writes=[B_bown], sem=B_bown)
    S.dma("act", gate_row.ap(), b_gate_row, writes=[B_grow], sem=B_grow)
    A("dve", lambda e: e.tensor_copy(out=perm_b.ap(), in_=perm_f.ap()), [B_permf], [B_perm])
    A("dve", lambda e: e.tensor_copy(out=a64_b.ap(), in_=a64_f), [B_a64f], [B_a64])
    A("dve", lambda e: e.tensor_copy(out=bm_b.ap(), in_=bm_f), [B_bmf], [B_bm])
    A("dve", lambda e: e.memset(ones_b.ap(), 1.0), [], [B_ones])
    A("dve", lambda e: e.memset(ones_f.ap(), 1.0), [], [B_onesf])
    A("dve", lambda e: e.memset(blk1_b.ap(), 0.0), [], [B_blk1])
    A("dve", lambda e: e.memset(blk1_b.ap()[0:64, 0:64], 1.0), [], [B_blk1])
    A("dve", lambda e: e.memset(blk1_b.ap()[64:128, 64:128], 1.0), [], [B_blk1])
    A("dve", lambda e: e.memset(hend.ap(), 0.0), [], [B_hend])
    A("dve", lambda e: e.memset(cols.ap(), 0.0), [], [B_cols])
    A("dve", lambda e: e.memset(hbuf.ap(), 0.0), [], [B_hbuf])

    lamt = scr(0, 256); lamp = scr(1, 128); lsum = scr(2, 4)
    B_lam = S.buf("lam", True)
    S.dma("act", lamt, lamv, writes=[B_lam], sem=B_lam)
    A("dve", lambda e: e.tensor_tensor(out=lamp[:, 0:64], in0=lamt[:, 0:64], in1=lamt[:, 64:128], op=ALU.mult), [B_lam], [Bz])
    A("dve", lambda e: e.tensor_tensor(out=lamp[:, 64:128], in0=lamt[:, 128:192], in1=lamt[:, 192:256], op=ALU.mult), [B_lam], [Bz])
    A("dve", lambda e: e.reduce_sum(out=lsum[:, 0:1], in_=lamp[:, 0:64], axis=AX.X), [], [Bz])
    A("dve", lambda e: e.reduce_sum(out=lsum[:, 1:2], in_=lamp[:, 64:128], axis=AX.X), [], [Bz])
    A("act", lambda e: e.activation(out=lsum[:, 2:4], in_=lsum[:, 0:2], func=AF.Exp), [Bz], [Bz])
    A("dve", lambda e: e.scalar_tensor_tensor(out=dc(D_NEGLAM), in0=lsum[:, 3:4], scalar=-LAM_INIT, in1=lsum[:, 2:3], op0=ALU.add, op1=ALU.subtract), [Bz], [B_cols])
    A("dve", lambda e: e.tensor_scalar(out=dc(D_GQ8), in0=smc(C_GQ), scalar1=0.125, scalar2=None, op0=ALU.mult), [B_sm], [B_cols])
    A("dve", lambda e: e.tensor_scalar(out=dc(D_GH8), in0=smc(C_GH), scalar1=float(1.0 - LAM_INIT), scalar2=None, op0=ALU.mult), [B_sm], [B_cols])

    A("act", lambda e: e.activation(out=dc(D_SC, 8), in_=smc(C_CT, 8), func=AF.Silu), [B_sm, B_cols], [B_cols])
    wa = [scr(4 + 2 * i, 1024).rearrange("p (c j) -> p c j", c=8) for i in range(3)]
    B_wa = [S.buf("wa%d" % i, True) for i in range(3)]
    w_ada_v = w_ada.rearrange("(c p) j -> p c j", p=128)
    bkA = gb()
    for jb in range(24):
        i = jb % 3
        S.dma("act", wa[i], w_ada_v[:, :, jb * 128:(jb + 1) * 128], writes=[B_wa[i]], sem=B_wa[i])
        for c in range(8):
            A("pe", (lambda e, i=i, c=c, jb=jb: e.matmul(bank(bkA, jb, jb + 1), lhsT=wa[i][:, c, :], rhs=dc(D_SC + c), start=(c == 0), stop=(c == 7))),
              [B_wa[i], B_cols], [PB[bkA]])
    A("dve", lambda e: e.tensor_tensor(out=dc(D_MOD, 24), in0=bank(bkA, 0, 24), in1=smc(C_BADA, 24), op=ALU.add), [PB[bkA], B_sm], [B_cols])
    A("dve", lambda e: e.scalar_tensor_tensor(out=dc(D_A1, 8), in0=dc(D_MOD + 8, 8), scalar=1.0, in1=smc(C_NG, 8), op0=ALU.add, op1=ALU.mult), [B_sm], [B_cols])
    A("dve", lambda e: e.tensor_copy(out=dc(D_SHIFT, 8), in_=dc(D_MOD, 8)), [], [B_cols])
    scbc = scr(12, 1024).rearrange("p (c j) -> p c j", c=8)
    for c in range(8):
        A("dve", (lambda e, c=c: e.tensor_copy(out=scbc[:, c, :], in_=dc(D_SC + c).to_broadcast([128, 128]))), [B_cols], [Bz])
    wg = [scr(14 + 2 * i, 1024) for i in range(3)]
    B_wg = [S.buf("wg%d" % i, True) for i in range(3)]
    bkG = [gb(), gb()]
    for c in range(8):
        i = c % 3
        S.dma("act", wg[i], w_ada[c * 128:(c + 1) * 128, 2048:3072], writes=[B_wg[i]], sem=B_wg[i])
        for hf in range(2):
            A("pe", (lambda e, i=i, c=c, hf=hf: e.matmul(bank(bkG[hf]), lhsT=scbc[:, c, :], rhs=wg[i][:, hf * 512:(hf + 1) * 512], start=(c == 0), stop=(c == 7))),
              [B_wg[i], Bz], [PB[bkG[hf]]])
    for hf in range(2):
        A("dve", (lambda e, hf=hf: e.tensor_tensor(out=gate_row.ap()[:, hf * 512:(hf + 1) * 512], in0=bank(bkG[hf]), in1=gate_row.ap()[:, hf * 512:(hf + 1) * 512], op=ALU.add)),
          [PB[bkG[hf]], B_grow], [B_grow])

    S.barrier(bar_dummies)
    S.pending_dma = precast_dmas + S.pending_dma

    B_T = S.buf("Tl", True)
    tl = [scr(2 * i, 1024) for i in range(4)]
    for i in range(4):
        S.dma("act", tl[i], ssmT[:, i, :], writes=[B_T, Bz], sem=B_T)
    dtT = scr(8, 1024); LrT = scr(10, 1024); thT = scr(12, 1024)
    tk = scr(14, 1024); tm = scr(16, 1024); ti = scr(18, 1024).bitcast(I32); tx = scr(20, 1024)
    A("act", lambda e: e.activation(out=dtT, in_=tl[2], func=AF.Exp), [B_T], [Bz])
    A("dve", lambda e: e.tensor_tensor(out=LrT, in0=tl[0], in1=dtT, op=ALU.mult), [B_T, Bz], [Bz])
    A("dve", lambda e: e.tensor_tensor(out=thT, in0=tl[1], in1=dtT, op=ALU.mult), [B_T, Bz], [Bz])

    def reduce_angle(x):
        n = x.shape[-1]
        A("dve", lambda e: e.tensor_scalar(out=x, in0=x, scalar1=float(16 * np.pi), scalar2=None, op0=ALU.add), [Bz], [Bz])
        A("dve", lambda e: e.tensor_scalar(out=tk[:, 0:n], in0=x, scalar1=float(1.0 / TWO_PI), scalar2=None, op0=ALU.mult), [Bz], [Bz])
        A("dve", lambda e: e.tensor_copy(out=ti[:, 0:n], in_=tk[:, 0:n]), [Bz], [Bz])
        A("dve", lambda e: e.tensor_copy(out=tk[:, 0:n], in_=ti[:, 0:n]), [Bz], [Bz])
        A("dve", lambda e: e.scalar_tensor_tensor(out=x, in0=tk[:, 0:n], scalar=-TWO_PI, in1=x, op0=ALU.mult, op1=ALU.add), [Bz], [Bz])

    def sincos_i(x, sin_out, cos_out, wb):
        n = x.shape[-1]
        sincos(x, sin_out, cos_out, tk[:, 0:n], ti[:, 0:n], tm[:, 0:n], [Bz], wb, [Bz], add16=True)

    reduce_angle(thT)
    A("dve", lambda e: e.tensor_scalar(out=tx, in0=thT, scalar1=8.0, scalar2=None, op0=ALU.mult), [Bz], [Bz])
    reduce_angle(tx)
    misc = scr(22)
    phc = misc[:, 0:32]; a32 = misc[:, 32:64]; c32 = misc[:, 64:96]; s32 = misc[:, 96:128]
    txv = tx.rearrange("p (a m) -> p a m", m=32)
    A("dve", lambda e: e.tensor_copy(out=phc, in_=txv[:, :, 0]), [Bz], [Bz])
    A("dve", lambda e: e.tensor_tensor(out=tx, in0=tx, in1=tl[3], op=ALU.mult), [Bz, B_T], [Bz])
    cosT_f = cosT.ap().rearrange("p a m -> p (a m)")
    sinT_f = sinT.ap().rearrange("p a m -> p (a m)")
    rhoT_f = rhoT.ap().rearrange("p a m -> p (a m)")
    sincos_i(tx, sinT_f, cosT_f, [B_tab, Bz])
    rho = dtT
    A("act", lambda e: e.activation(out=rho, in_=LrT, func=AF.Exp, scale=8.0), [Bz], [Bz])
    A("dve", lambda e: e.tensor_scalar(out=tm, in0=tl[3], scalar1=0.0, scalar2=None, op0=ALU.is_gt), [B_T, Bz], [Bz])
    A("dve", lambda e: e.tensor_tensor(out=rhoT_f, in0=rho, in1=tm, op=ALU.mult), [Bz], [B_tab, Bz])
    rhov = rho.rearrange("p (a m) -> p a m", m=32)
    A("dve", lambda e: e.tensor_tensor(out=scon.ap()[:, 0, :], in0=rhov[:, :, 1], in1=cosT.ap()[:, :, 1], op=ALU.mult), [Bz, B_tab], [B_scon, Bz])
    A("dve", lambda e: e.tensor_tensor(out=scon.ap()[:, 1, :], in0=rhov[:, :, 1], in1=sinT.ap()[:, :, 1], op=ALU.mult), [Bz, B_tab], [B_scon, Bz])
    A("dve", lambda e: e.tensor_scalar(out=a32, in0=phc, scalar1=32.0, scalar2=None, op0=ALU.mult), [Bz], [Bz])
    sincos_i(a32, s32, c32, [Bz])
    A("dve", lambda e: e.tensor_tensor(out=scon.ap()[:, 2, :], in0=rhov[:, :, 1], in1=c32, op=ALU.mult), [Bz], [B_scon, Bz])
    A("dve", lambda e: e.tensor_tensor(out=scon.ap()[:, 3, :], in0=rhov[:, :, 1], in1=s32, op=ALU.mult), [Bz], [B_scon, Bz])
    akre = scr(23)[:, 0:288].rearrange("p (k a) -> p k a", k=9)
    akim = scr(24)[:, 0:288].rearrange("p (k a) -> p k a", k=9)
    cmp_ = scr(25)
    Lrc = cmp_[:, 0:32]; thc = cmp_[:, 32:64]; magc = cmp_[:, 64:96]; a1s = cmp_[:, 96:128]; a1c = cmp_[:, 128:160]
    arc = cmp_[:, 160:192]; aic = cmp_[:, 192:224]; fre = cmp_[:, 224:256]; fim = cmp_[:, 256:288]
    t1c = cmp_[:, 288:320]; t2c = cmp_[:, 320:352]; nrc = cmp_[:, 352:384]; denc = cmp_[:, 384:416]; t3c = cmp_[:, 416:448]
    LrTv = LrT.rearrange("p (a m) -> p a m", m=32)
    thTv = thT.rearrange("p (a m) -> p a m", m=32)
    A("dve", lambda e: e.tensor_copy(out=Lrc, in_=LrTv[:, :, 0]), [Bz], [Bz])
    A("dve", lambda e: e.tensor_copy(out=thc, in_=thTv[:, :, 0]), [Bz], [Bz])
    A("dve", lambda e: e.tensor_copy(out=arc, in_=tl[0].rearrange("p (a m) -> p a m", m=32)[:, :, 0]), [B_T, Bz], [Bz])
    A("dve", lambda e: e.tensor_copy(out=aic, in_=tl[1].rearrange("p (a m) -> p a m", m=32)[:, :, 0]), [B_T, Bz], [Bz])
    A("act", lambda e: e.activation(out=magc, in_=Lrc, func=AF.Exp), [Bz], [Bz])
    sincos_i(thc, a1s, a1c, [Bz])
    A("dve", lambda e: e.tensor_tensor(out=akre[:, 1, :], in0=magc, in1=a1c, op=ALU.mult), [Bz], [Bz])
    A("dve", lambda e: e.tensor_tensor(out=akim[:, 1, :], in0=magc, in1=a1s, op=ALU.mult), [Bz], [Bz])
    A("dve", lambda e: e.memset(akre[:, 0, :], 1.0), [Bz], [Bz])
    A("dve", lambda e: e.memset(akim[:, 0, :], 0.0), [Bz], [Bz])
    for k in range(2, 9):
        cmul(akre[:, k, :], akim[:, k, :], akre[:, k - 1, :], akim[:, k - 1, :], akre[:, 1, :], akim[:, 1, :], t1c, t2c, [Bz], [Bz], [Bz])
    A("dve", lambda e: e.tensor_scalar(out=nrc, in0=akre[:, 1, :], scalar1=-1.0, scalar2=None, op0=ALU.add), [Bz], [Bz])
    A("dve", lambda e: e.tensor_tensor(out=t1c, in0=arc, in1=arc, op=ALU.mult), [Bz], [Bz])
    A("dve", lambda e: e.tensor_tensor(out=t2c, in0=aic, in1=aic, op=ALU.mult), [Bz], [Bz])
    A("dve", lambda e: e.tensor_tensor(out=denc, in0=t1c, in1=t2c, op=ALU.add), [Bz], [Bz])
    A("dve", lambda e: e.reciprocal(out=denc, in_=denc), [Bz], [Bz])
    A("dve", lambda e: e.tensor_tensor(out=t1c, in0=nrc, in1=arc, op=ALU.mult), [Bz], [Bz])
    A("dve", lambda e: e.tensor_tensor(out=t2c, in0=akim[:, 1, :], in1=aic, op=ALU.mult), [Bz], [Bz])
    A("dve", lambda e: e.tensor_tensor(out=t3c, in0=t1c, in1=t2c, op=ALU.add), [Bz], [Bz])
    A("dve", lambda e: e.tensor_tensor(out=fre, in0=t3c, in1=denc, op=ALU.mult), [Bz], [Bz])
    A("dve", lambda e: e.tensor_tensor(out=t1c, in0=akim[:, 1, :], in1=aic, op=ALU.mult), [Bz], [Bz])
    A("dve", lambda e: e.tensor_tensor(out=t2c, in0=nrc, in1=aic, op=ALU.mult), [Bz], [Bz])
    A("dve", lambda e: e.tensor_tensor(out=t3c, in0=t1c, in1=t2c, op=ALU.subtract), [Bz], [Bz])
    A("dve", lambda e: e.tensor_tensor(out=fim, in0=t3c, in1=denc, op=ALU.mult), [Bz], [Bz])

    kin = [[scr(2 * i)[:, 128 * q:128 * (q + 1)] for q in range(4)] for i in range(2)]
    B_kin = [S.buf("kin%d" % i, True) for i in range(2)]
    bbr = scr(8); bbi = scr(9)
    car_l = [scr(10), scr(16)]; cai_l = [scr(11), scr(17)]
    kt1 = scr(12)[:, 0:128]; kt2 = scr(12)[:, 128:256]; kp1 = scr(13)[:, 0:128]; kp2 = scr(13)[:, 128:256]
    akimn = scr(4)[:, 0:288].rearrange("p (k a) -> p k a", k=9)
    ddt = scr(14, 1024); B_dd = S.buf("dd", True)
    B_bb = S.buf("bb"); B_car_l = [S.buf("car0"), S.buf("car1")]; B_cai_l = [S.buf("cai0"), S.buf("cai1")]; B_kd = S.buf("kd"); B_kp = S.buf("kp"); B_akn = S.buf("akn")
    S.dma("act", ddt, ddiag, writes=[B_dd, Bz], sem=B_dd)
    A("dve", lambda e: e.tensor_scalar(out=akimn, in0=akim, scalar1=-1.0, scalar2=None, op0=ALU.mult), [Bz], [B_akn, Bz])
    A("dve", lambda e: e.memset(bbr, 0.0), [], [Bz, B_bb])
    A("dve", lambda e: e.memset(bbi, 0.0), [], [Bz, B_bb])
    for q_ in range(2):
        A("dve", (lambda e, q_=q_: e.memset(car_l[q_], 0.0)), [], [Bz, B_car_l[q_]])
        A("dve", (lambda e, q_=q_: e.memset(cai_l[q_], 0.0)), [], [Bz, B_cai_l[q_]])
    A("dve", lambda e: e.memset(kp1, 0.0), [], [Bz, B_kp])
    A("dve", lambda e: e.memset(kt1, 0.0), [], [Bz, B_kd])
    wcst = wcb_v
    wkst = [wkb[i].ap() for i in range(2)]

    def bc4(col_ap, n=32):
        return bass.AP(tensor=col_ap.tensor, offset=col_ap.offset, ap=[list(col_ap.ap[0]), [1, 4], [0, n]])

    def v4(ap128):
        return ap128.rearrange("p (j c) -> p j c", j=4)

    def dg(t512):
        return bass.AP(tensor=t512.tensor, offset=t512.offset, ap=[list(t512.ap[0]), [160, 4], [1, 32]])

    for cb in range(8):
        i = cb % 2
        for q in range(4):
            S.dma("act", kin[i][q], ssmK[:, q, cb * 128:(cb + 1) * 128], writes=[B_kin[i], Bz], sem=B_kin[i])
        bpr, bpi, cpr, cpi = [v4(kin[i][q]) for q in range(4)]
        fr_b = bc4(fre[:, 4 * cb:4 * cb + 4]); fi_b = bc4(fim[:, 4 * cb:4 * cb + 4])
        cmul(dg(bbr), dg(bbi), bpr, bpi, fr_b, fi_b, v4(kt1), v4(kt2), [B_kin[i], Bz], [B_bb], [B_kd])
        for lag in range(9):
            ar_b = bc4(akre[:, lag, 4 * cb:4 * cb + 4]); ai_b = bc4(akimn[:, lag, 4 * cb:4 * cb + 4])
            car = car_l[lag % 2]; cai = cai_l[lag % 2]; B_car = B_car_l[lag % 2]; B_cai = B_cai_l[lag % 2]
            A("dve", (lambda e, cpr=cpr, ar_b=ar_b: e.tensor_tensor(out=v4(kt1), in0=cpr, in1=ar_b, op=ALU.mult)), [B_kin[i], Bz], [B_kd])
            A("dve", (lambda e, cpi=cpi, ai_b=ai_b: e.tensor_tensor(out=v4(kt2), in0=cpi, in1=ai_b, op=ALU.mult)), [B_kin[i], B_akn], [B_kd])
            A("dve", (lambda e, car=car: e.tensor_tensor(out=dg(car), in0=v4(kt1), in1=v4(kt2), op=ALU.add)), [B_kd], [B_car])
            A("dve", (lambda e, cpr=cpr, ai_b=ai_b: e.tensor_tensor(out=v4(kp1), in0=cpr, in1=ai_b, op=ALU.mult)), [B_kin[i], B_akn], [B_kp])
            A("dve", (lambda e, cpi=cpi, ar_b=ar_b: e.tensor_tensor(out=v4(kp2), in0=cpi, in1=ar_b, op=ALU.mult)), [B_kin[i], Bz], [B_kp])
            A("dve", (lambda e, cai=cai: e.tensor_tensor(out=dg(cai), in0=v4(kp1), in1=v4(kp2), op=ALU.subtract)), [B_kp], [B_cai])
            if lag >= 1:
                A("act", (lambda e, i=i, lag=lag, car=car: e.activation(out=wcst[i][:, :, lag - 1, 0, :], in_=dg(car), func=AF.Copy)), [B_car], [B_wobuf[i]])
                A("act", (lambda e, i=i, lag=lag, cai=cai: e.activation(out=wcst[i][:, :, lag - 1, 1, :], in_=dg(cai), func=AF.Copy)), [B_cai], [B_wobuf[i]])
            if lag <= 7:
                bk = gb()
                for j in range(4):
                    A("pe", (lambda e, bk=bk, j=j, car=car: e.matmul(bank(bk, 0, 128), lhsT=bbr[:, j * 128:(j + 1) * 128], rhs=car[:, j * 128:(j + 1) * 128], start=(j == 0), stop=False)),
                      [B_bb, B_car], [PB[bk]])
                for j in range(4):
                    A("pe", (lambda e, bk=bk, j=j, cai=cai: e.matmul(bank(bk, 0, 128), lhsT=bbi[:, j * 128:(j + 1) * 128], rhs=cai[:, j * 128:(j + 1) * 128], start=False, stop=(j == 3))),
                      [B_bb, B_cai], [PB[bk]])
                if lag == 0:
                    A("dve", (lambda e, bk=bk, i=i, cb=cb: e.tensor_tensor(out=wkst[i][:, 0, :], in0=bank(bk, 0, 128), in1=ddt[:, cb * 128:(cb + 1) * 128], op=ALU.add)),
                      [PB[bk], B_dd], [B_wkb[i]])
                else:
                    A("act", (lambda e, bk=bk, i=i, lag=lag: e.activation(out=wkst[i][:, lag, :], in_=bank(bk, 0, 128), func=AF.Copy)), [PB[bk]], [B_wkb[i]])
        S.dma("act", WC_scr[cb], wcst[i], reads=[B_wobuf[i]], writes=[], sem=B_wobuf[i])
        S.dma("act", WK_scr[cb], wkst[i], reads=[B_wkb[i]], writes=[], sem=B_wkb[i])
    A("dve", lambda e: e.memset(dmy_d.ap(), 0.0), [], [Bz, B_bb, B_kd, B_kp, B_akn] + B_car_l + B_cai_l + B_kin)

    B_Sl = S.buf("Sl", True)
    sl = [scr(2 * i, 1024) for i in range(5)]
    for i in range(5):
        S.dma("act", sl[i], ssmS[:, i, :], writes=[B_Sl, Bz], sem=B_Sl)
    tA = scr(10, 1024); tB = scr(12, 1024); tC = scr(20, 1024); tD = scr(22, 1024)
    tE = scr(24, 1024); tF = scr(26, 1024); tG = scr(28, 1024); tH = scr(30, 1024)
    wsst = wsb_v
    A("act", lambda e: e.activation(out=tA, in_=sl[2], func=AF.Exp), [B_Sl, Bz], [Bz])
    A("dve", lambda e: e.tensor_tensor(out=tB, in0=sl[0], in1=tA, op=ALU.mult), [B_Sl, Bz], [Bz])
    A("dve", lambda e: e.tensor_tensor(out=tC, in0=sl[1], in1=tA, op=ALU.mult), [B_Sl, Bz], [Bz])
    A("act", lambda e: e.activation(out=tA, in_=tB, func=AF.Exp), [Bz], [Bz])
    sincos_i(tC, tD, tE, [Bz])
    a1re = tB; a1im = tC
    A("dve", lambda e: e.tensor_tensor(out=a1re, in0=tA, in1=tE, op=ALU.mult), [Bz], [Bz])
    A("dve", lambda e: e.tensor_tensor(out=a1im, in0=tA, in1=tD, op=ALU.mult), [Bz], [Bz])
    nr = tA; st1 = tD; st2 = tE; rden = tF; cre = tG; cim = tH
    A("dve", lambda e: e.tensor_scalar(out=nr, in0=a1re, scalar1=-1.0, scalar2=None, op0=ALU.add), [Bz], [Bz])
    A("dve", lambda e: e.tensor_tensor(out=st1, in0=sl[0], in1=sl[0], op=ALU.mult), [B_Sl, Bz], [Bz])
    A("dve", lambda e: e.tensor_tensor(out=st2, in0=sl[1], in1=sl[1], op=ALU.mult), [B_Sl, Bz], [Bz])
    A("dve", lambda e: e.tensor_tensor(out=rden, in0=st1, in1=st2, op=ALU.add), [Bz], [Bz])
    A("dve", lambda e: e.reciprocal(out=rden, in_=rden), [Bz], [Bz])
    A("dve", lambda e: e.tensor_tensor(out=st1, in0=nr, in1=sl[0], op=ALU.mult), [B_Sl, Bz], [Bz])
    A("dve", lambda e: e.tensor_tensor(out=st2, in0=a1im, in1=sl[1], op=ALU.mult), [B_Sl, Bz], [Bz])
    A("dve", lambda e: e.tensor_tensor(out=st1, in0=st1, in1=st2, op=ALU.add), [Bz], [Bz])
    A("dve", lambda e: e.tensor_tensor(out=cre, in0=st1, in1=rden, op=ALU.mult), [Bz], [Bz])
    A("dve", lambda e: e.tensor_tensor(out=st1, in0=a1im, in1=sl[0], op=ALU.mult), [B_Sl, Bz], [Bz])
    A("dve", lambda e: e.tensor_tensor(out=st2, in0=nr, in1=sl[1], op=ALU.mult), [B_Sl, Bz], [Bz])
    A("dve", lambda e: e.tensor_tensor(out=st1, in0=st1, in1=st2, op=ALU.subtract), [Bz], [Bz])
    A("dve", lambda e: e.tensor_tensor(out=cim, in0=st1, in1=rden, op=ALU.mult), [Bz], [Bz])
    cur = (cre, cim)
    nxt = (tA, tF)
    WS_v = WS_scr.rearrange("cb p s r c -> p cb s r c")
    c8 = lambda ap: ap.rearrange("p (cb c) -> p cb c", cb=8)
    wsst8 = [kbuf_t.ap().rearrange("p a b -> p (a b)").rearrange("p (cb r c) -> p cb r c", cb=8, r=2),
             vbuf_t.ap().rearrange("p a b -> p (a b)").rearrange("p (cb r c) -> p cb r c", cb=8, r=2)]
    for k in range(8):
        sg = 7 - k
        i = k % 2
        o_re = wsst8[i][:, :, 0, :]
        o_im = wsst8[i][:, :, 1, :]
        cmul(o_re, o_im, c8(cur[0]), c8(cur[1]), c8(sl[3]), c8(sl[4]), c8(st1), c8(st2), [B_Sl, Bz], BL_wsb[i], [Bz])
        S.dma("act", WS_v[:, :, sg, :, :], wsst8[i], reads=BL_wsb[i], writes=[], sem=BL_wsb[i][0])
        if k < 7:
            cmul(nxt[0], nxt[1], cur[0], cur[1], a1re, a1im, st1, st2, [Bz], [Bz], [Bz])
            cur, nxt = nxt, cur

    S.barrier(bar_dummies)

    B_K = [[S.buf("K%d_%d" % (t, h)) for h in range(8)] for t in range(NT)]
    B_V = [[S.buf("V%d_%d" % (t, h)) for h in range(8)] for t in range(NT)]
    final_ops = []

    def wload(src):
        i = wbi[0]
        wbi[0] = (i + 1) % NWB
        S.dma("sp", wbuf[i].ap().rearrange("p c j -> p (c j)"), src, writes=[B_wbuf[i]], sem=B_wbuf[i])
        return wbuf[i].ap(), B_wbuf[i]

    def front_a(src_rows):
        for sub in range(4):
            S.dma("sp", xt_v[:, sub, :], src_rows[sub * 128:(sub + 1) * 128, :], writes=[B_R[2 * sub], B_R[2 * sub + 1]], sem=B_R[2 * sub])
        for sub in range(4):
            A("act", (lambda e, sub=sub: e.activation(out=sqjunk.ap(), in_=xt_v[:, sub, :], func=AF.Square, accum_out=ss.ap()[:, sub:sub + 1])),
              [B_R[2 * sub], B_R[2 * sub + 1]], [B_junk, B_ss])
        A("act", lambda e: e.activation(out=ss.ap()[:, 4:8], in_=ss.ap()[:, 0:4], func=AF.Ln, scale=1.0 / 1024.0, bias=EPS), [B_ss], [B_ss])
        A("act", lambda e: e.activation(out=ss.ap()[:, 4:8], in_=ss.ap()[:, 4:8], func=AF.Exp, scale=-0.5), [B_ss], [B_ss])
        for sub in range(4):
            A("dve", (lambda e, sub=sub: e.tensor_scalar(out=xt_v[:, sub, :], in0=xt_v[:, sub, :], scalar1=ss.ap()[:, 4 + sub:5 + sub], scalar2=None, op0=ALU.mult)),
              [B_ss], [B_R[2 * sub], B_R[2 * sub + 1]])

    def front_b(base):
        for c in range(8):
            bk = gb()
            for sub in range(4):
                A("pe", (lambda e, bk=bk, sub=sub, c=c: e.transpose(bank(bk, sub * 128, (sub + 1) * 128), xt_v[:, sub, c * 128:(c + 1) * 128], ident.ap())),
                  [B_R[2 * sub], B_R[2 * sub + 1], B_ident], [PB[bk]])
            A("dve", (lambda e, bk=bk, c=c: e.tensor_scalar(out=HTC[:, c, :], in0=bank(bk), scalar1=dc(D_A1 + c), scalar2=dc(D_SHIFT + c), op0=ALU.mult, op1=ALU.add)),
              [PB[bk], B_cols], [B_htc])
        tb = B_T8[0:4]
        A("dve", lambda e: e.tensor_scalar(out=T(0), in0=pos0.ap(), scalar1=float(base), scalar2=smc(C_INVF), op0=ALU.add, op1=ALU.mult), [B_pos0, B_sm], tb)
        sincos(T(0), ropeS.ap(), ropeC.ap(), T(1), T(2).bitcast(I32), T(3), [], [B_rope], tb)

    qkset = [0]
    QS = [[T(0), T(1), T(2)], [T(3), T(4), T(5)], [T(6), T(7), qx.ap()]]
    BQS = [[B_T8[0], B_T8[1], B_T8[2]], [B_T8[3], B_T8[4], B_T8[5]], [B_T8[6], B_T8[7], B_qx]]

    def qknorm_a(bk, gcol):
        z = qkset[0]; qkset[0] = (z + 1) % 3
        qrs, qkn = QS[z][0], QS[z][1]
        B_qrs, B_qkn = BQS[z][0], BQS[z][1]
        qsq_, qknb_ = qsq2[z].ap(), qknb2[z].ap()
        Bq_, Bkb_ = B_qsq2[z], B_qknb2[z]
        A("act", lambda e: e.activation(out=qsq_, in_=bank(bk), func=AF.Square), [PB[bk]], [Bq_])
        b2 = gb()
        A("pe", lambda e: e.matmul(bank(b2), lhsT=blk1_b.ap(), rhs=qsq_, start=True, stop=True), [B_blk1, Bq_], [PB[b2]])
        A("act", lambda e: e.activation(out=qrs, in_=bank(b2), func=AF.Ln, scale=1.0 / 64.0, bias=EPS), [PB[b2]], [B_qrs])
        A("act", lambda e: e.activation(out=qrs, in_=qrs, func=AF.Exp, scale=-0.5), [], [B_qrs])
        A("dve", lambda e: e.scalar_tensor_tensor(out=qkn, in0=bank(bk), scalar=gcol, in1=qrs, op0=ALU.mult, op1=ALU.mult), [PB[bk], B_qrs, B_cols, B_sm], [B_qkn])
        A("act", lambda e: e.activation(out=qknb_, in_=qkn, func=AF.Copy), [B_qkn], [Bkb_])
        return (z, 0)

    def qknorm_b(stt, C, Sn, Bcs, out_ap, Bout):
        z, _ = stt
        qkn, qt1, qt2 = QS[z][1], QS[z][2], QS[z][0]
        B_qkn, B_qt1, B_qt2 = BQS[z][1], BQS[z][2], BQS[z][0]
        qknb_ = qknb2[z].ap(); Bkb_ = B_qknb2[z]
        b3 = gb()
        A("pe", lambda e: e.matmul(bank(b3), lhsT=perm_b.ap(), rhs=qknb_, start=True, stop=True), [B_perm, Bkb_], [PB[b3]])
        A("dve", lambda e: e.tensor_tensor(out=qt1, in0=qkn, in1=C, op=ALU.mult), [B_qkn, Bcs], [B_qt1])
        A("dve", lambda e: e.tensor_tensor(out=qt2, in0=bank(b3), in1=Sn, op=ALU.mult), [PB[b3], Bcs], [B_qt2])
        A("dve", lambda e: e.tensor_tensor(out=out_ap, in0=qt1, in1=qt2, op=ALU.add), [B_qt1, B_qt2], [Bout])

    def qk_pipeline(blk0, hT, BhT, gcol, C, Sn, Bcs, sink, heads=range(8)):
        heads = list(heads)
        nb = len(heads)
        banks_ = {}
        sta = {}
        for t_ in range(nb + 2):
            if t_ < nb:
                banks_[t_] = proj_fm(blk0 + heads[t_], hT, BhT)
            if 1 <= t_ <= nb:
                sta[t_ - 1] = qknorm_a(banks_[t_ - 1], gcol)
            if 2 <= t_ <= nb + 1:
                out_ap, Bout, post = sink(heads[t_ - 2])
                qknorm_b(sta[t_ - 2], C, Sn, Bcs, out_ap, Bout)
                if post is not None:
                    post()

    def proj_fm(blk, hT, BhT):
        w, Bw = wload(w_in_b[blk])
        bk = gb()
        for c in range(8):
            A("pe", (lambda e, bk=bk, c=c, w=w: e.matmul(bank(bk), lhsT=w[:, c, :], rhs=hT[:, c, :], start=(c == 0), stop=(c == 7))), [Bw, BhT], [PB[bk]])
        return bk

    for s in range(NSTEP):
        for half in range(2):
            g = 2 * s + half
            if half == 0:
                front_a(xc[g * 512:(g + 1) * 512, :])
            front_b(512.0 * g)
            if half == 0:
                front_a(xc[(g + 1) * 512:(g + 2) * 512, :])
            hc = HTC.rearrange("p c t -> p (c t)"); ho = HTO.rearrange("p c t -> p (c t)")
            if half == 0:
                A("dve", lambda e: e.tensor_scalar(out=ho, in0=hc, scalar1=smc(C_OMR), scalar2=None, op0=ALU.mult), [B_htc, B_sm], [B_hto])
                A("dve", lambda e: e.tensor_scalar(out=ropeCo.ap(), in0=ropeC.ap(), scalar1=smc(C_OMR), scalar2=None, op0=ALU.mult), [B_rope, B_sm], [B_ropeo])
                A("dve", lambda e: e.tensor_scalar(out=ropeSo.ap(), in0=ropeS.ap(), scalar1=smc(C_OMR), scalar2=None, op0=ALU.mult), [B_rope, B_sm], [B_ropeo])
            else:
                A("dve", lambda e: e.scalar_tensor_tensor(out=ho, in0=hc, scalar=smc(C_R), in1=ho, op0=ALU.mult, op1=ALU.add), [B_htc, B_sm], [B_hto])
                A("dve", lambda e: e.scalar_tensor_tensor(out=ropeCo.ap(), in0=ropeC.ap(), scalar=smc(C_R), in1=ropeCo.ap(), op0=ALU.mult, op1=ALU.add), [B_rope, B_sm], [B_ropeo])
                A("dve", lambda e: e.scalar_tensor_tensor(out=ropeSo.ap(), in0=ropeS.ap(), scalar=smc(C_R), in1=ropeSo.ap(), op0=ALU.mult, op1=ALU.add), [B_rope, B_sm], [B_ropeo])
            def ksink(h, g=g):
                i = ksi[0]; ksi[0] = (i + 1) % NKS
                return kst[i].ap(), B_kst[i], (lambda i=i, h=h, g=g: S.dma("pool", K_scr[g, h], kst[i].ap(), reads=[B_kst[i]], writes=[B_K[g][h]], sem=B_kst[i]))
            qk_pipeline(8, HTC, B_htc, smc(C_GK), ropeC.ap(), ropeS.ap(), B_rope, ksink)
            for h in range(8):
                w, Bw = wload(w_in_b[16 + h])
                bk = gb()
                for sub in range(4):
                    for c in range(8):
                        A("pe", (lambda e, bk=bk, c=c, w=w, sub=sub: e.matmul(bank(bk, sub * 128, (sub + 1) * 128), lhsT=HTC[:, c, sub * 128:(sub + 1) * 128], rhs=w[:, c, :], start=(c == 0), stop=(c == 7))),
                          [Bw, B_htc], [PB[bk]])
                i = vsi[0]; vsi[0] = (i + 1) % NKS
                A("act", (lambda e, bk=bk, i=i: e.activation(out=vst[i].ap(), in_=bank(bk), func=AF.Copy)), [PB[bk]], [B_vst[i]])
                S.dma("pool", V_scr[g, h], vst[i].ap(), reads=[B_vst[i]], writes=[B_V[g][h]], sem=B_vst[i])
            for cb in range(8):
                bk = proj_fm(32 + cb, HTC, B_htc)
                A("act", (lambda e, bk=bk, cb=cb, half=half: e.activation(out=UTC[:, cb, half * 512:(half + 1) * 512], in_=bank(bk), func=AF.Copy)), [PB[bk]], [B_utc[half]])
        A("dve", lambda e: e.tensor_scalar(out=UTO, in0=UTC[:, :, 0:512], scalar1=smc(C_OMR), scalar2=None, op0=ALU.mult), [B_utcl, B_sm], [B_uto])
        A("dve", lambda e: e.scalar_tensor_tensor(out=UTO, in0=UTC[:, :, 512:1024], scalar=smc(C_R), in1=UTO, op0=ALU.mult, op1=ALU.add), [B_utch, B_sm], [B_uto])

        for hf in range(2):
            for cq in range(4):
                cb = 4 * hf + cq
                i = cb % 2
                S.dma("sp", wsb_v[i], WS_scr[cb], writes=BL_wsb[i], sem=BL_wsb[i][0])
                bks = [(4 * (cb % 2) + j) for j in range(4)]
                for ri in range(2):
                    for sg_ in range(8):
                        for j in range(4):
                            A("pe", (lambda e, i=i, ri=ri, sg_=sg_, j=j, cb=cb, bks=bks: e.matmul(
                                bank(bks[j], ri * 128, (ri + 1) * 128), lhsT=wsb_v[i][32 * j:32 * j + 32, sg_, ri, :],
                                rhs=UTC[32 * j:32 * j + 32, cb, sg_:1024:8], start=(ri == 0 and sg_ == 0), stop=(ri == 1 and sg_ == 7),
                                tile_position=(32 * j, 0), skip_group_check=True)),
                              BL_wsb[i] + B_utc, [PB[bks[j]]])
                b0 = bks[0]
                s_re = bass.AP(tensor=PS, offset=512 * b0, ap=[[4096, 128], [512, 4], [32, 4], [1, 32]])
                s_im = bass.AP(tensor=PS, offset=512 * b0 + 128, ap=[[4096, 128], [512, 4], [32, 4], [1, 32]])
                c_t = bass.AP(tensor=cosT, offset=32 * 4 * cb, ap=[[1024, 128], [32, 4], [0, 4], [1, 32]])
                s_t = bass.AP(tensor=sinT, offset=32 * 4 * cb, ap=[[1024, 128], [32, 4], [0, 4], [1, 32]])
                rt = [T(q).rearrange("p (j s m) -> p j s m", j=4, s=4) for q in range(4)]
                pbs = [PB[b] for b in bks]
                A("dve", (lambda e, s_re=s_re, c_t=c_t, rt=rt: e.tensor_tensor(out=rt[0], in0=s_re, in1=c_t, op=ALU.mult)), pbs + [B_tab], [B_T8[0]])
                A("dve", (lambda e, s_im=s_im, s_t=s_t, rt=rt: e.tensor_tensor(out=rt[1], in0=s_im, in1=s_t, op=ALU.mult)), pbs + [B_tab], [B_T8[1]])
                A("dve", (lambda e, s_im=s_im, c_t=c_t, rt=rt: e.tensor_tensor(out=rt[2], in0=s_im, in1=c_t, op=ALU.mult)), pbs + [B_tab], [B_T8[2]])
                A("dve", (lambda e, s_re=s_re, s_t=s_t, rt=rt: e.tensor_tensor(out=rt[3], in0=s_re, in1=s_t, op=ALU.mult)), pbs + [B_tab], [B_T8[3]])
                g_re = bass.AP(tensor=R16, offset=0 * 2048 + 32 * 4 * cq, ap=[[4096, 128], [32, 4], [512, 4], [1, 32]])
                g_im = bass.AP(tensor=R16, offset=1 * 2048 + 32 * 4 * cq, ap=[[4096, 128], [32, 4], [512, 4], [1, 32]])
                A("pool", (lambda e, g_re=g_re, rt=rt: e.tensor_tensor(out=g_re, in0=rt[0], in1=rt[1], op=ALU.add)), [B_T8[0], B_T8[1]], B_gin[0])
                A("pool", (lambda e, g_im=g_im, rt=rt: e.tensor_tensor(out=g_im, in0=rt[2], in1=rt[3], op=ALU.subtract)), [B_T8[2], B_T8[3]], B_gin[1])
            qk_pipeline(0, HTO, B_hto, dc(D_GQ8), ropeCo.ap(), ropeSo.ap(), B_ropeo, (lambda h: (QT[:, h, :], B_qt[h], None)), heads=range(4 * hf, 4 * hf + 4))
            p0 = 16 * hf
            rho_h = rhoT.ap()[:, p0:p0 + 16, :].rearrange("p a m -> p (a m)")
            cos31 = cosT.ap()[:, p0:p0 + 16, 31]; sin31 = sinT.ap()[:, p0:p0 + 16, 31]
            st = [stmp.ap()[:, q, :] for q in range(8)]
            for sg in range(4):
                gre = gin_v[:, 0, sg, :]; gim = gin_v[:, 1, sg, :]
                gre3 = gre.rearrange("p (a m) -> p a m", m=32); gim3 = gim.rearrange("p (a m) -> p a m", m=32)
                Bg = [B_gin[0][sg], B_gin[1][sg]]
                if sg == 0:
                    pr_re = hend.ap()[:, 0, p0:p0 + 16]; pr_im = hend.ap()[:, 1, p0:p0 + 16]
                    m_re = scon.ap()[:, 0, p0:p0 + 16]; m_im = scon.ap()[:, 1, p0:p0 + 16]
                    rb = [B_hend, B_scon]
                else:
                    pr_re = gin_v[:, 0, sg - 1, :].rearrange("p (a m) -> p a m", m=32)[:, :, 31]
                    pr_im = gin_v[:, 1, sg - 1, :].rearrange("p (a m) -> p a m", m=32)[:, :, 31]
                    m_re = scon.ap()[:, 2, p0:p0 + 16]; m_im = scon.ap()[:, 3, p0:p0 + 16]
                    rb = [B_gin[0][sg - 1], B_gin[1][sg - 1], B_scon]
                cmul(st[0], st[1], pr_re, pr_im, m_re, m_im, st[2], st[3], rb, [B_stmp], [B_stmp])
                A("dve", (lambda e, gre3=gre3, st=st: e.tensor_tensor(out=gre3[:, :, 0], in0=gre3[:, :, 0], in1=st[0], op=ALU.add)), [B_stmp], [Bg[0]])
                A("dve", (lambda e, gim3=gim3, st=st: e.tensor_tensor(out=gim3[:, :, 0], in0=gim3[:, :, 0], in1=st[1], op=ALU.add)), [B_stmp], [Bg[1]])
                A("dve", (lambda e, gre=gre, rho_h=rho_h: e.tensor_tensor_scan(out=gre, data0=rho_h, data1=gre, initial=0.0, op0=ALU.mult, op1=ALU.add)), [B_tab], [Bg[0]])
                A("dve", (lambda e, gim=gim, rho_h=rho_h: e.tensor_tensor_scan(out=gim, data0=rho_h, data1=gim, initial=0.0, op0=ALU.mult, op1=ALU.add)), [B_tab], [Bg[1]])
                w_, i_ = sg // 2, sg % 2
                for ri, gsrc in ((0, gre), (1, gim)):
                    dst = gsel_v[:, ri, i_, :]
                    Bd = B_pbuf[2 * ri + i_]
                    if w_ == 0:
                        A("dve", (lambda e, dst=dst, gsrc=gsrc: e.tensor_scalar(out=dst, in0=gsrc, scalar1=smc(C_OMR), scalar2=None, op0=ALU.mult)), [Bg[ri], B_sm], Bd)
                    else:
                        A("dve", (lambda e, dst=dst, gsrc=gsrc: e.scalar_tensor_tensor(out=dst, in0=gsrc, scalar=smc(C_R), in1=dst, op0=ALU.mult, op1=ALU.add)), [Bg[ri], B_sm], Bd)
                if sg == 1 or sg == 3:
                    g31r = gre3[:, :, 31]; g31i = gim3[:, :, 31]
                    tgt = hmid if sg == 1 else hend
                    Btgt = B_hmid if sg == 1 else B_hend
                    if sg == 3:
                        for ri in range(2):
                            A("dve", (lambda e, ri=ri, st=st: e.tensor_scalar(out=st[4 + ri], in0=hend.ap()[:, ri, p0:p0 + 16], scalar1=smc(C_OMR), scalar2=None, op0=ALU.mult)), [B_hend, B_sm], [B_stmp])
                            A("dve", (lambda e, ri=ri, st=st: e.scalar_tensor_tensor(out=hbuf.ap()[:, ri, p0:p0 + 16, 0], in0=hmid.ap()[:, ri, p0:p0 + 16], scalar=smc(C_R), in1=st[4 + ri], op0=ALU.mult, op1=ALU.add)),
                              [B_hmid, B_stmp, B_sm], [B_hbuf])
                    cmul(tgt.ap()[:, 0, p0:p0 + 16], tgt.ap()[:, 1, p0:p0 + 16], g31r, g31i, cos31, sin31, st[2], st[3], [Bg[0], Bg[1], B_tab], [Btgt], [B_stmp])
            gs_re = gsel_v[:, 0, :, :].rearrange("p i (a m) -> p i a m", m=32)
            gs_im = gsel_v[:, 1, :, :].rearrange("p i (a m) -> p i a m", m=32)
            c_t = bass.AP(tensor=cosT, offset=32 * p0, ap=[[1024, 128], [0, 2], [32, 16], [1, 32]])
            s_t = bass.AP(tensor=sinT, offset=32 * p0, ap=[[1024, 128], [0, 2], [32, 16], [1, 32]])
            tmp4 = TMP.ap().rearrange("p a b -> p (a b)")
            rt = [tmp4[:, 1024 * q:1024 * (q + 1)].rearrange("p (i a m) -> p i a m", i=2, a=16) for q in range(4)]
            Brt = [[B_T8[2 * q], B_T8[2 * q + 1]] for q in range(4)]
            Bgs_re = B_pbuf[0] + B_pbuf[1]; Bgs_im = B_pbuf[2] + B_pbuf[3]
            A("dve", (lambda e, rt=rt, gs_re=gs_re, c_t=c_t: e.tensor_tensor(out=rt[0], in0=gs_re, in1=c_t, op=ALU.mult)), Bgs_re + [B_tab], Brt[0])
            A("dve", (lambda e, rt=rt, gs_im=gs_im, s_t=s_t: e.tensor_tensor(out=rt[1], in0=gs_im, in1=s_t, op=ALU.mult)), Bgs_im + [B_tab], Brt[1])
            A("dve", (lambda e, rt=rt, gs_im=gs_im, c_t=c_t: e.tensor_tensor(out=rt[2], in0=gs_im, in1=c_t, op=ALU.mult)), Bgs_im + [B_tab], Brt[2])
            A("dve", (lambda e, rt=rt, gs_re=gs_re, s_t=s_t: e.tensor_tensor(out=rt[3], in0=gs_re, in1=s_t, op=ALU.mult)), Bgs_re + [B_tab], Brt[3])
            h_re = bass.AP(tensor=hbuf, offset=0 * 32 * 65 + p0 * 65 + 1, ap=[[2 * 32 * 65, 128], [32, 2], [65, 16], [1, 32]])
            h_im = bass.AP(tensor=hbuf, offset=1 * 32 * 65 + p0 * 65 + 1, ap=[[2 * 32 * 65, 128], [32, 2], [65, 16], [1, 32]])
            A("pool", (lambda e, rt=rt, h_re=h_re: e.tensor_tensor(out=h_re, in0=rt[0], in1=rt[1], op=ALU.subtract)), Brt[0] + Brt[1], [B_hbuf])
            A("pool", (lambda e, rt=rt, h_im=h_im: e.tensor_tensor(out=h_im, in0=rt[2], in1=rt[3], op=ALU.add)), Brt[2] + Brt[3], [B_hbuf])

        for h in range(8):
            bk = proj_fm(24 + h, HTO, B_hto)
            A("act", (lambda e, bk=bk, h=h: e.activation(out=GAT[:, h, :], in_=bank(bk), func=AF.Silu)), [PB[bk]], [B_gat[h]])
        for cb in range(8):
            bk = proj_fm(40 + cb, HTO, B_hto)
            A("act", (lambda e, bk=bk, cb=cb: e.activation(out=GST[:, cb, :], in_=bank(bk), func=AF.Silu)), [PB[bk]], [B_gst[cb]])

        for cb in range(8):
            i = cb % 2
            S.dma("sp", wkb[i].ap(), WK_scr[cb], writes=[B_wkb[i]], sem=B_wkb[i])
            S.dma("sp", wcb_v[i], WC_scr[cb], writes=[B_wobuf[i]], sem=B_wobuf[i])
            bk = gb()
            first = True
            for tau in range(8):
                for sg_ in range(tau + 1):
                    A("pe", (lambda e, bk=bk, i=i, tau=tau, sg_=sg_, cb=cb, f=first: e.matmul(
                        PS.ap()[:, 512 * bk + tau:512 * bk + 512:8], lhsT=wkb[i].ap()[:, tau - sg_, :], rhs=UTO[:, cb, sg_:512:8], start=f, stop=False, skip_group_check=True)),
                      [B_wkb[i], B_uto], [PB[bk]])
                    first = False
            for tau in range(8):
                for j in range(4):
                    for ri in range(2):
                        last = (tau == 7 and j == 3 and ri == 1)
                        A("pe", (lambda e, bk=bk, i=i, tau=tau, j=j, ri=ri, cb=cb, last=last: e.matmul(
                            PS.ap()[32 * j:32 * j + 32, 512 * bk + tau:512 * bk + 512:8], lhsT=wcb_v[i][:, j, tau, ri, :],
                            rhs=hbuf.ap()[:, ri, 4 * cb + j, 0:64], start=False, stop=last, tile_position=(0, 32 * j), skip_group_check=True)),
                          [B_wobuf[i], B_hbuf], [PB[bk]])
            A("act", (lambda e, bk=bk, cb=cb: e.activation(out=ZG[:, cb, :], in_=bank(bk), func=AF.Gelu_apprx_tanh)), [PB[bk], B_utch], [B_zg])
        sig, gtmp = T(5), T(6)
        B_sig, B_gtmp = B_T8[5], B_T8[6]
        for nb in range(8):
            w, Bw = wload(w_glu_b[nb])
            bk = gb()
            for c in range(8):
                A("pe", (lambda e, bk=bk, c=c, w=w: e.matmul(bank(bk), lhsT=w[:, c, :], rhs=ZG[:, c, :], start=(c == 0), stop=(c == 7))), [Bw, B_zg], [PB[bk]])
            A("act", (lambda e, bk=bk, nb=nb: e.activation(out=sig, in_=bank(bk), func=AF.Sigmoid, bias=smc(C_BGLU + nb))), [PB[bk], B_sm], [B_sig])
            A("dve", (lambda e, nb=nb: e.tensor_tensor(out=gtmp, in0=sig, in1=ZG[:, nb, :], op=ALU.mult)), [B_sig, B_zg], [B_gtmp])
            A("dve", (lambda e, nb=nb: e.tensor_tensor(out=ZT[:, nb, :], in0=gtmp, in1=GST[:, nb, :], op=ALU.mult)), [B_gtmp, B_gst[nb]], [B_zt])

        nkt = 2 * s + 2
        fr0, fr1, fo0, fo1, frs = T(0), T(1), T(2), T(3), T(4)
        B_fr0, B_fr1, B_fo0, B_fo1, B_frs = B_T8[0], B_T8[1], B_T8[2], B_T8[3], B_T8[4]
        pb8 = pbuf_t.ap().rearrange("p a (b c) -> p (a b) c", b=2)
        units = [(h, kt, kb) for h in range(8) for kt in range(nkt) for kb in range(4)]
        kvslot = {}

        def issue_qk(n):
            h, kt, kb = units[n]
            if kb == 0:
                i = kvi[0]; kvi[0] = (i + 1) % NKV
                kvslot[(h, kt)] = i
                S.dma("sp", kbuf_t.ap()[:, i, :], K_scr[kt, h], reads=[B_K[kt][h]], writes=[B_kbuf[i]], sem=B_kbuf[i])
                S.dma("sp", vbuf_t.ap()[:, i, :], V_scr[kt, h], reads=[B_V[kt][h]], writes=[B_vbuf[i]], sem=B_vbuf[i])
            i = kvslot[(h, kt)]
            kb_ap = kbuf_t.ap()[:, i, :]
            mvar = 0 if kt == 2 * s else (1 if kt == 2 * s + 1 else None)
            st_ = n % 2
            slots = []
            for m in range(2):
                bk = 2 * st_ + m
                A("pe", (lambda e, kb_ap=kb_ap, m=m, kb=kb, bk=bk, h=h, mv=mvar: e.matmul(
                    bank(bk), lhsT=kb_ap[64 * m:64 * m + 64, kb * 128:(kb + 1) * 128], rhs=QT[64 * m:64 * m + 64, h, :],
                    start=True, stop=(mv is None))), [B_kbuf[i], B_qt[h]], [PB[bk]])
                if mvar is not None:
                    A("pe", (lambda e, m=m, kb=kb, bk=bk, mv=mvar: e.matmul(
                        bank(bk), lhsT=a64_b.ap()[64 * m:64 * m + 64, kb * 128:(kb + 1) * 128], rhs=bm_b.ap()[64 * m:64 * m + 64, mv * 512:(mv + 1) * 512],
                        start=False, stop=True)), [B_a64, B_bm], [PB[bk]])
                pi_ = pbi[0]; pbi[0] = (pi_ + 1) % 8
                A("act", (lambda e, pi_=pi_, bk=bk: e.activation(out=pb8[:, pi_, :], in_=bank(bk), func=AF.Exp)), [PB[bk]], [B_pb8[pi_]])
                slots.append(pi_)
            return slots

        def issue_pv(n, slots):
            h, kt, kb = units[n]
            i = kvslot[(h, kt)]
            vb_ap = vbuf_t.ap()[:, i, :]
            st_ = (kt == 0 and kb == 0)
            sp_ = (kt == nkt - 1 and kb == 3)
            for m in range(2):
                pi_ = slots[m]
                A("pe", (lambda e, vb_ap=vb_ap, m=m, kb=kb, pi_=pi_, st_=st_, sp_=sp_: e.matmul(
                    bank(4 + m), lhsT=vb_ap[:, kb * 128:(kb + 1) * 128], rhs=pb8[:, pi_, :], start=st_, stop=sp_)),
                  [B_vbuf[i], B_pb8[pi_]], [PB[4 + m]])
                uidx = 4 * kt + kb
                if uidx % 4 == 1:
                    A("pe", (lambda e, m=m, pi_=pi_, f=(uidx == 1): e.matmul(bank(6 + m), lhsT=ones_b.ap(), rhs=pb8[:, pi_, :], start=f, stop=False)),
                      [B_ones, B_pb8[pi_]], [PB[6 + m]])
                else:
                    acc = sacc[h % 2][m].ap(); Bacc = B_sacc[h % 2][m]
                    if st_:
                        A("dve", (lambda e, acc=acc, pi_=pi_: e.tensor_copy(out=acc, in_=pb8[:, pi_, :])), [B_pb8[pi_]], [Bacc])
                    else:
                        A("dve", (lambda e, acc=acc, pi_=pi_: e.tensor_tensor(out=acc, in0=acc, in1=pb8[:, pi_, :], op=ALU.add)), [B_pb8[pi_]], [Bacc])
            if sp_:
                for m in range(2):
                    A("pe", (lambda e, m=m, hh=h % 2: e.matmul(bank(6 + m), lhsT=ones_f.ap(), rhs=sacc[hh][m].ap(), start=False, stop=True)), [B_onesf, B_sacc[h % 2][m]], [PB[6 + m]])
                A("dve", lambda e: e.reciprocal(out=fr0, in_=bank(6)), [PB[6]], [B_fr0])
                A("dve", lambda e: e.tensor_tensor(out=fo0, in0=bank(4), in1=fr0, op=ALU.mult), [PB[4], B_fr0], [B_fo0])
                A("dve", lambda e: e.reciprocal(out=fr1, in_=bank(7)), [PB[7]], [B_fr1])
                A("dve", lambda e: e.tensor_tensor(out=fo1, in0=bank(5), in1=fr1, op=ALU.mult), [PB[5], B_fr1], [B_fo1])
                A("dve", lambda e: e.scalar_tensor_tensor(out=fo0, in0=fo1, scalar=dc(D_NEGLAM), in1=fo0, op0=ALU.mult, op1=ALU.add), [B_fo1, B_cols], [B_fo0])
                A("act", lambda e: e.activation(out=qsq.ap(), in_=fo0, func=AF.Square), [B_fo0], [B_qsq])
                A("pe", lambda e: e.matmul(bank(6), lhsT=ones_b.ap(), rhs=qsq.ap(), start=True, stop=True), [B_ones, B_qsq], [PB[6]])
                A("act", lambda e: e.activation(out=frs, in_=bank(6), func=AF.Ln, scale=1.0 / 128.0, bias=EPS), [PB[6]], [B_frs])
                A("act", lambda e: e.activation(out=frs, in_=frs, func=AF.Exp, scale=-0.5), [], [B_frs])
                A("dve", lambda e: e.tensor_tensor(out=fo1, in0=fo0, in1=frs, op=ALU.mult), [B_fo0, B_frs], [B_fo1])
                A("dve", (lambda e, h=h: e.scalar_tensor_tensor(out=OT[:, h, :], in0=fo1, scalar=dc(D_GH8), in1=GAT[:, h, :], op0=ALU.mult, op1=ALU.mult)), [B_fo1, B_cols, B_gat[h]], [B_ot])

        pend = issue_qk(0)
        for n in range(len(units)):
            nxt = issue_qk(n + 1) if n + 1 < len(units) else None
            issue_pv(n, pend)
            pend = nxt

        for hf in range(2):
            S.dma("pool", xr_v, xo[s * 512:(s + 1) * 512, hf * 512:(hf + 1) * 512].rearrange("(sub p) c -> p sub c", p=128), writes=B_R[0:4], sem=B_R[0])
            obk = [gb() for _ in range(4)]
            for cq in range(4):
                blk = hf * 4 + cq
                i = blk % 2
                S.dma("sp", wobuf[i].ap().rearrange("p c j -> p (c j)"), w_out_b[blk], writes=[B_wobuf[i]], sem=B_wobuf[i])
                for sub in range(4):
                    for k in range(16):
                        src = OT[:, k, sub * 128:(sub + 1) * 128] if k < 8 else ZT[:, k - 8, sub * 128:(sub + 1) * 128]
                        Bsrc = B_ot if k < 8 else B_zt
                        A("pe", (lambda e, i=i, k=k, sub=sub, cq=cq, src=src, obk=obk: e.matmul(
                            bank(obk[sub], cq * 128, (cq + 1) * 128), lhsT=src, rhs=wobuf[i].ap()[:, k, :], start=(k == 0), stop=(k == 15))),
                          [B_wobuf[i], Bsrc], [PB[obk[sub]]])
            for sub in range(4):
                oi = oti[0]; oti[0] = (oi + 1) % 2
                A("dve", (lambda e, oi=oi, sub=sub, hf=hf, obk=obk: e.tensor_tensor(out=otile_v[oi], in0=bank(obk[sub]), in1=gate_row.ap()[:, hf * 512:(hf + 1) * 512], op=ALU.mult)),
                  [PB[obk[sub]], B_grow], [B_R[4 + oi]])
                A("dve", (lambda e, oi=oi, sub=sub: e.tensor_tensor(out=otile_v[oi], in0=otile_v[oi], in1=xr_v[:, sub, :], op=ALU.add)), B_R[0:4], [B_R[4 + oi]])
                o = S.dma("pool", out[s * 512 + sub * 128:s * 512 + (sub + 1) * 128, hf * 512:(hf + 1) * 512], otile_v[oi], reads=[B_R[4 + oi]], writes=[], sem=B_R[4 + oi])
                final_ops.append(o)

    if debug:
        for name, t, bufs in (("d_cols", cols, [B_cols]), ("d_hbuf", hbuf, [B_hbuf]),
                              ("d_arena", ARENA, [B_utcl, B_utch, B_htc, B_hto, B_uto] + B_qt + B_gat + B_gst),
                              ("d_cosT", cosT, [B_tab]), ("d_sinT", sinT, [B_tab]), ("d_rhoT", rhoT, [B_tab]), ("d_scon", scon, [B_scon]),
                              ("d_grow", gate_row, [B_grow])):
            bfl = S.buf(name, True)
            d = dbg_out(name, list(t.shape), t.dtype)
            final_ops.append(S.dma("sp", d, t.ap(), reads=bufs, writes=[], sem=bfl))

    with nc.Block() as block:
        S.emit(block, final_ops=final_ops)
    return nc, S


def _ssm_layouts(a_re, a_im, log_dt, b_re, b_im, c_re, c_im, d):
    f = np.float32
    AR_S = np.zeros((3, 128, 8, 2, 64), f)
    BT = np.zeros((2, 128, 8, 2, 64), f)
    for cb in range(8):
        for j in range(4):
            for g2 in range(2):
                g = 8 * cb + 2 * j + g2
                AR_S[0, 32 * j:32 * j + 32, cb, g2, :] = a_re[g][None, :]
                AR_S[1, 32 * j:32 * j + 32, cb, g2, :] = a_im[g][None, :]
                AR_S[2, 32 * j:32 * j + 32, cb, g2, :] = log_dt[g]
                BT[0, 32 * j + 16 * g2:32 * j + 16 * g2 + 16, cb, g2, :] = b_re[g].T
                BT[1, 32 * j + 16 * g2:32 * j + 16 * g2 + 16, cb, g2, :] = b_im[g].T
    ssmS = np.concatenate([AR_S.reshape(3, 128, 1024), BT.reshape(2, 128, 1024)], 0).transpose(1, 0, 2)
    K = np.zeros((4, 128, 32, 2, 16), f)
    T = np.zeros((4, 128, 32, 32), f)
    for pr in range(32):
        for g2 in range(2):
            g = 2 * pr + g2
            K[0, 64 * g2:64 * g2 + 64, pr, g2, :] = b_re[g]
            K[1, 64 * g2:64 * g2 + 64, pr, g2, :] = b_im[g]
            K[2, 64 * g2:64 * g2 + 64, pr, g2, :] = c_re[g].T
            K[3, 64 * g2:64 * g2 + 64, pr, g2, :] = c_im[g].T
            T[0, 64 * g2:64 * g2 + 64, pr, :] = a_re[g][:, None]
            T[1, 64 * g2:64 * g2 + 64, pr, :] = a_im[g][:, None]
            T[2, 64 * g2:64 * g2 + 64, pr, :] = log_dt[g]
    T[3] = np.arange(32, dtype=f)[None, None, :]
    ssmK = K.reshape(4, 128, 1024).transpose(1, 0, 2)
    ssmT = T.reshape(4, 128, 1024).transpose(1, 0, 2)
    dd = np.zeros((128, 8, 128), f)
    for cb in range(8):
        dd[np.arange(128), cb, np.arange(128)] = d[128 * cb:128 * (cb + 1)]
    return np.ascontiguousarray(ssmS), np.ascontiguousarray(ssmK), np.ascontiguousarray(ssmT), dd.reshape(128, 1024)


def _consts():
    f = np.float32
    ident = np.eye(128, dtype=f)
    perm = np.zeros((128, 128), f)
    for m in range(128):
        d = m % 64
        if d < 32:
            perm[m + 32, m] = -1.0
        else:
            perm[m - 32, m] = 1.0
    a64 = np.zeros((128, 512), f)
    for j in range(8):
        a64[j, 64 * j:64 * (j + 1)] = 1.0
        a64[64 + j, 64 * j:64 * (j + 1)] = 1.0
    diag = np.zeros((128, 512), f)
    none = np.zeros((128, 512), f)
    for j in range(8):
        row = np.where((np.arange(512) // 64) >= j, 0.0, NEG).astype(f)
        diag[j] = row
        diag[64 + j] = row
        none[j] = NEG
        none[64 + j] = NEG
    full = np.zeros((128, 512), f)
    pos0 = np.broadcast_to(np.arange(512, dtype=f)[None, :], (128, 512)).copy()
    invf = (1.0 / (np.float32(10000.0) ** (np.arange(0, 64, 2, dtype=f) / np.float32(64)))).astype(f)
    invf_col = np.tile(np.concatenate([invf, invf]), 2).astype(f)
    sel = np.zeros((128, 256), f)
    for m in range(2):
        sel[32 * m, 128 * m:128 * (m + 1)] = 1.0
        sel[64 + 32 * m, 128 * m:128 * (m + 1)] = 1.0
    return ident, perm, a64, (diag, none, full), pos0, invf_col, sel


def make_in_maps(inputs, NSTEP, nbatch):
    f = np.float32
    g = lambda k: np.asarray(inputs[k], dtype=f)
    x = g("x"); c = g("c")
    ident, perm, a64, (diag, none, full), pos0, invf_col, sel = _consts()
    ssmS, ssmK, ssmT, dd = _ssm_layouts(g("ssm_a_re")[0], g("ssm_a_im")[0], g("ssm_log_dt")[0], g("ssm_b_re")[0], g("ssm_b_im")[0],
                                        g("ssm_c_re")[0], g("ssm_c_im")[0], g("ssm_d")[0])
    lamv = np.broadcast_to(np.concatenate([g("lam_q1")[0], g("lam_k1")[0], g("lam_q2")[0], g("lam_k2")[0]])[None, :], (128, 256)).copy()
    b_ada = g("b_ada")[0]
    shared = {
        "w_ada": np.ascontiguousarray(g("w_ada")[0]), "w_in": np.ascontiguousarray(g("w_in")[0]),
        "w_glu": np.ascontiguousarray(g("w_glu")[0]), "w_out": np.ascontiguousarray(g("w_out")[0]),
        "lamv": lamv, "b_gate_row": np.broadcast_to(b_ada[None, 2048:3072], (128, 1024)).copy(),
        "ident": ident, "perm": perm, "sel": sel, "a64": a64, "pos0": pos0,
        "ssmS": ssmS, "ssmK": ssmK, "ssmT": ssmT, "ddiag": dd,
    }
    NT = 2 * NSTEP
    maps = []
    for core in range(2 * nbatch):
        b, r = core // 2, core % 2
        sm = np.zeros((128, 64), f)
        sm[:, 0:8] = c[b].reshape(8, 128).T
        sm[:, 8:32] = b_ada.reshape(24, 128).T
        sm[:, 32:40] = g("norm_g")[0].reshape(8, 128).T
        sm[:, 40] = np.tile(g("q_norm_g")[0], 2)
        sm[:, 41] = np.tile(g("k_norm_g")[0], 2)
        sm[:, 42] = g("head_norm_g")[0]
        sm[:, 43:51] = g("b_glu")[0].reshape(8, 128).T
        sm[:, 51] = invf_col
        sm[:, 52] = float(r)
        sm[:, 53] = float(1 - r)
        bo = np.zeros((128, 16), f)
        for s in range(NSTEP):
            bo[:, s] = 1024.0 * s + 512.0 * r
        bm = np.concatenate([diag, none] if r == 0 else [full, diag], axis=1)
        xb = np.ascontiguousarray(x[b][:NT * 512])
        m = dict(shared)
        m.update({"xc": xb, "xo": np.ascontiguousarray(xb.reshape(NT, 512, 1024)[r::2].reshape(-1, 1024)),
                  "smalls": sm, "base_own": bo, "bm": np.ascontiguousarray(bm)})
        maps.append(m)
    return maps


_NC_CACHE = {}


def run(inputs, NSTEP, nbatch, debug=False):
    key = (NSTEP, debug)
    if key not in _NC_CACHE:
        _NC_CACHE[key] = build_nc(NSTEP, debug)
    nc, S = _NC_CACHE[key]
    maps = make_in_maps(inputs, NSTEP, nbatch)
    res = run_bass_kernel_spmd(nc, maps, core_ids=list(range(2 * nbatch)))
    NT = 2 * NSTEP
    outp = np.zeros((nbatch, NT, 512, 1024), np.float32)
    for core in range(2 * nbatch):
        b, r = core // 2, core % 2
        outp[b, r::2] = np.asarray(res.results[core]["out"]).reshape(NSTEP, 512, 1024)
    return outp.reshape(nbatch, NT * 512, 1024), res


def kernel(**inputs):
    out, _ = run(inputs, 8, 4)
    return out
```
